# Optimizing a Trainium2 kernel written in Bass

```python
import math
import jax
import jax.numpy as jnp
from jax import lax
import numpy as np

D_MODEL = 1024
BATCH = 2
SEQ = 8192
DEPTH = 4
DEC_BATCH = 32
DEC_SEQ = 1
PAST_LEN = 8192
PAGE_SIZE = 128

N_MIXERS = 3
N_GMLP_LAYERS = (DEPTH + 2) // 3
N_NSA_LAYERS = (DEPTH + 1) // 3
N_SSM_LAYERS = DEPTH // 3

D_FF = ((8 * D_MODEL + 3 * 256 - 1) // (3 * 256)) * 256

GMLP_HALF = D_MODEL
GMLP_GROUPS = 8
GMLP_GROUP_WIDTH = GMLP_HALF // GMLP_GROUPS
GMLP_CHUNK = 128

NSA_HEAD_DIM = 64
NSA_HEADS = D_MODEL // NSA_HEAD_DIM
NSA_KV_HEADS = 4
NSA_REP = NSA_HEADS // NSA_KV_HEADS
NSA_BLOCK = 64
NSA_TOPK = 16
NSA_WINDOW = 512
NSA_CMP_HIDDEN = 2 * NSA_HEAD_DIM
NSA_Q_BLOCK = 64
NSA_W_BLOCK = 128
NSA_Q_COLS = NSA_HEADS * NSA_HEAD_DIM
NSA_KV_COLS = 2 * NSA_KV_HEADS * NSA_HEAD_DIM
NSA_IN_COLS = NSA_Q_COLS + 3 * NSA_KV_COLS + 3 * NSA_HEADS
SEL_FORCE = 1.0e4
SEL_MASKED = -1.0

SSM_GROUP_WIDTH = 16
SSM_GROUPS = D_MODEL // SSM_GROUP_WIDTH
SSM_STATE = 64
SSM_CHUNK = 256
DT_MIN = 1.0e-3
DT_MAX = 1.0e-1

RMS_EPS = 1.0e-6
LN_EPS = 1.0e-5

kernel_name = "hybrid_gmlp_nsa_s5_decoder_step"


def rmsnorm(x, g):
    xf = x.astype(jnp.float32)
    y = xf * lax.rsqrt(jnp.mean(xf * xf, axis=-1, keepdims=True) + RMS_EPS)
    return (y * g.astype(jnp.float32)).astype(x.dtype)


def adaln(c, w, b):
    m = (jax.nn.silu(c) @ w + b)[:, None, :]
    return jnp.split(m, 6, axis=-1)


def modulate(x, g, shift, scale):
    return rmsnorm(x, g) * (1 + scale) + shift


def swiglu(h, w_gate, w_up, w_down):
    return (jax.nn.silu(h @ w_gate) * (h @ w_up)) @ w_down


def masked_softmax(s, mask):
    s = jnp.where(mask, s, -1e30)
    m = jnp.max(s, axis=-1, keepdims=True)
    e = jnp.where(mask, jnp.exp(s - m), 0.0)
    return e / jnp.maximum(jnp.sum(e, axis=-1, keepdims=True), 1e-30)


def gmlp_mixer(h, w_in, b_in, ln_g, ln_b, w_s, b_s, w_out):
    B, T, _ = h.shape
    z = jax.nn.gelu(h @ w_in + b_in)
    u, v = jnp.split(z, 2, axis=-1)
    vf = v.astype(jnp.float32)
    mu = jnp.mean(vf, axis=-1, keepdims=True)
    var = jnp.mean(jnp.square(vf - mu), axis=-1, keepdims=True)
    v = ((vf - mu) * lax.rsqrt(var + LN_EPS) * ln_g.astype(jnp.float32) + ln_b.astype(jnp.float32)).astype(h.dtype)
    pad = (-T) % GMLP_CHUNK
    n_chunks = (T + pad) // GMLP_CHUNK
    vc = jnp.pad(v, ((0, 0), (0, pad), (0, 0))).reshape(B, n_chunks, GMLP_CHUNK, GMLP_GROUPS, GMLP_GROUP_WIDTH)
    causal = jnp.tril(jnp.ones((GMLP_CHUNK, GMLP_CHUNK), dtype=bool))
    w = jnp.where(causal, w_s, 0.0)
    mixed = jnp.einsum("hts,bcshe->bcthe", w, vc) + b_s.T[:, :, None]
    mixed = mixed.reshape(B, n_chunks * GMLP_CHUNK, GMLP_HALF)[:, :T]
    return (u * mixed) @ w_out, v


def nsa_project(h, w_in):
    B, T, _ = h.shape
    z = h @ w_in
    cuts = np.cumsum([NSA_Q_COLS, NSA_KV_COLS, NSA_KV_COLS, NSA_KV_COLS]).tolist()
    q, kv_c, kv_s, kv_w, g = jnp.split(z, cuts, axis=-1)
    kv_shape = (B, T, 2, NSA_KV_HEADS, NSA_HEAD_DIM)
    gates = jax.nn.sigmoid(g.astype(jnp.float32)).reshape(B, T, 3, NSA_KV_HEADS, NSA_REP)
    return (q.reshape(B, T, NSA_KV_HEADS, NSA_REP, NSA_HEAD_DIM), kv_c.reshape(kv_shape),
            kv_s.reshape(kv_shape), kv_w.reshape(kv_shape), gates)


def nsa_compress(kv, w1, w2, pe):
    B, L = kv.shape[:2]
    nb = L // NSA_BLOCK
    blk = kv[:, :nb * NSA_BLOCK].reshape(B, nb, NSA_BLOCK, 2, NSA_KV_HEADS, NSA_HEAD_DIM)
    pe_bias = jnp.einsum("zld,zlde->ze", pe, w1)
    hid = jax.nn.silu(jnp.einsum("bnlzgd,zlde->bnzge", blk, w1) + pe_bias[:, None, :])
    return jnp.einsum("bnzge,zed->bnzgd", hid, w2)


def nsa_cmp_slc(q, kv_c, q_pos0, nb_sel, gather_sel, w1, w2, pe):
    B, Tq = q.shape[:2]
    scale = NSA_HEAD_DIM ** -0.5
    qf = q.astype(jnp.float32)
    cmp = nsa_compress(kv_c, w1, w2, pe).astype(jnp.float32)
    kc, vc = cmp[:, :, 0], cmp[:, :, 1]
    nb_cmp = kc.shape[1]
    t = q_pos0 + jnp.arange(Tq)
    s = jnp.einsum("bqgrd,bngd->bqgrn", qf, kc) * scale
    mask_c = ((jnp.arange(nb_cmp) + 1) * NSA_BLOCK <= t[:, None] + 1)[None, :, None, None, :]
    p = masked_softmax(s, mask_c)
    o_cmp = jnp.einsum("bqgrn,bngd->bqgrd", p, vc)
    imp = jnp.pad(p.sum(axis=3), ((0, 0), (0, 0), (0, 0), (0, nb_sel - nb_cmp)))
    blk = jnp.arange(nb_sel)[None, :]
    jt = (t // NSA_BLOCK)[:, None]
    forced = (blk == 0) | (blk == jt) | (blk == jt - 1)
    score = jnp.where((blk <= jt)[:, None, :], jnp.where(forced[:, None, :], SEL_FORCE, imp), SEL_MASKED)
    k_sel = min(NSA_TOPK, nb_sel)
    top_s, top_i = lax.top_k(score, k_sel)
    valid = top_s > 0.5 * SEL_MASKED
    qb = NSA_Q_BLOCK if Tq % NSA_Q_BLOCK == 0 else Tq
    nq = Tq // qb

    def to_blocks(a):
        return jnp.moveaxis(a.reshape((B, nq, qb) + a.shape[2:]), 1, 0)

    def sel_block(args):
        q_b, i_b, v_b, t_b = args
        kv = gather_sel(i_b).astype(jnp.float32)
        kv = kv.reshape(B, qb, NSA_KV_HEADS, k_sel * NSA_BLOCK, 2, NSA_HEAD_DIM)
        kpos = (i_b[..., None] * NSA_BLOCK + jnp.arange(NSA_BLOCK)).reshape(B, qb, NSA_KV_HEADS, k_sel * NSA_BLOCK)
        mask = (jnp.repeat(v_b, NSA_BLOCK, axis=-1) & (kpos <= t_b[None, :, None, None]))[:, :, :, None, :]
        sb = jnp.einsum("bqgrd,bqgsd->bqgrs", q_b, kv[..., 0, :]) * scale
        return jnp.einsum("bqgrs,bqgsd->bqgrd", masked_softmax(sb, mask), kv[..., 1, :])

    o_slc = lax.map(sel_block, (to_blocks(qf), to_blocks(top_i), to_blocks(valid), t.reshape(nq, qb)))
    o_slc = jnp.moveaxis(o_slc, 0, 1).reshape(B, Tq, NSA_KV_HEADS, NSA_REP, NSA_HEAD_DIM)
    return o_cmp, o_slc


def nsa_window_block(q_b, kv_b, t_q, t_k):
    kvf = kv_b.astype(jnp.float32)
    mask = (t_k[None, :] <= t_q[:, None]) & (t_k[None, :] >= t_q[:, None] - NSA_WINDOW) & (t_k[None, :] >= 0)
    s = jnp.einsum("bqgrd,bkgd->bqgrk", q_b, kvf[:, :, 0]) * NSA_HEAD_DIM ** -0.5
    p = masked_softmax(s, mask[None, :, None, None, :])
    return jnp.einsum("bqgrk,bkgd->bqgrd", p, kvf[:, :, 1])


def nsa_window_prompt(q, kv_w):
    B, T = q.shape[:2]
    qw = NSA_W_BLOCK if T % NSA_W_BLOCK == 0 else T
    n = T // qw
    kpad = jnp.pad(kv_w, ((0, 0), (NSA_WINDOW, 0), (0, 0), (0, 0), (0, 0)))

    def block_fn(args):
        q_b, start = args
        kv_b = lax.dynamic_slice_in_dim(kpad, start, NSA_WINDOW + qw, axis=1)
        t_q = start + jnp.arange(qw)
        t_k = start - NSA_WINDOW + jnp.arange(NSA_WINDOW + qw)
        return nsa_window_block(q_b, kv_b, t_q, t_k)

    q_blocks = jnp.moveaxis(q.astype(jnp.float32).reshape(B, n, qw, NSA_KV_HEADS, NSA_REP, NSA_HEAD_DIM), 1, 0)
    o = lax.map(block_fn, (q_blocks, jnp.arange(n) * qw))
    return jnp.moveaxis(o, 0, 1).reshape(B, T, NSA_KV_HEADS, NSA_REP, NSA_HEAD_DIM)


def nsa_merge(o_cmp, o_slc, o_win, gates, w_out, dtype):
    B, T = o_cmp.shape[:2]
    o = (gates[:, :, 0][..., None] * o_cmp + gates[:, :, 1][..., None] * o_slc
         + gates[:, :, 2][..., None] * o_win)
    return o.reshape(B, T, NSA_Q_COLS).astype(dtype) @ w_out


def nsa_prompt(h, w_in, w1, w2, pe, w_out):
    B, T, _ = h.shape
    q, kv_c, kv_s, kv_w, gates = nsa_project(h, w_in)
    nb = -(-T // NSA_BLOCK)
    blocks = jnp.pad(kv_s, ((0, 0), (0, nb * NSA_BLOCK - T), (0, 0), (0, 0), (0, 0)))
    blocks = blocks.reshape(B, nb, NSA_BLOCK, 2, NSA_KV_HEADS, NSA_HEAD_DIM)
    bi = jnp.arange(B)[:, None, None, None]
    gi = jnp.arange(NSA_KV_HEADS)[None, None, :, None]

    def gather_sel(idx):
        return blocks[bi, idx, :, :, gi]

    o_cmp, o_slc = nsa_cmp_slc(q, kv_c, 0, nb, gather_sel, w1, w2, pe)
    o_win = nsa_window_prompt(q, kv_w)
    y = nsa_merge(o_cmp, o_slc, o_win, gates, w_out, h.dtype)
    keep = min(NSA_WINDOW, T)
    return y, kv_c, kv_s, kv_w[:, T - keep:]


def nsa_sample(h, cache_cmp, cache_slc, win_buf, page_table, layer, w_in, w1, w2, pe, w_out):
    B, T, _ = h.shape
    n_pool = cache_cmp.shape[1]
    n_pages = page_table.shape[1]
    past = n_pages * PAGE_SIZE
    q, kv_c, kv_s, kv_w, gates = nsa_project(h, w_in)
    pool_c = cache_cmp.reshape((-1,) + cache_cmp.shape[2:])
    past_c = pool_c[layer * n_pool + page_table].reshape(B, past, 2, NSA_KV_HEADS, NSA_HEAD_DIM)
    full_c = jnp.concatenate([past_c.astype(kv_c.dtype), kv_c], axis=1)
    nb = -(-(past + T) // NSA_BLOCK)
    nb_past = past // NSA_BLOCK
    nb_new = nb - nb_past
    per_page = PAGE_SIZE // NSA_BLOCK
    pool_s = cache_slc.reshape(-1, NSA_BLOCK, 2, NSA_KV_HEADS, NSA_HEAD_DIM)
    new_blocks = jnp.pad(kv_s, ((0, 0), (0, nb_new * NSA_BLOCK - T), (0, 0), (0, 0), (0, 0)))
    new_blocks = new_blocks.reshape(B, nb_new, NSA_BLOCK, 2, NSA_KV_HEADS, NSA_HEAD_DIM)
    bi = jnp.arange(B)[:, None, None, None]
    gi = jnp.arange(NSA_KV_HEADS)[None, None, :, None]

    def gather_sel(idx):
        is_new = idx >= nb_past
        lp = jnp.minimum(idx // per_page, n_pages - 1)
        phys = (layer * n_pool + page_table[bi, lp]) * per_page + idx % per_page
        from_pool = pool_s[phys, :, :, gi]
        from_new = new_blocks[bi, jnp.clip(idx - nb_past, 0, nb_new - 1), :, :, gi]
        return jnp.where(is_new[..., None, None, None], from_new, from_pool.astype(from_new.dtype))

    o_cmp, o_slc = nsa_cmp_slc(q, full_c, past, nb, gather_sel, w1, w2, pe)
    wb = win_buf.shape[1]
    win = jnp.concatenate([win_buf.astype(kv_w.dtype), kv_w], axis=1)
    t_k = past - wb + jnp.arange(wb + T)
    t_q = past + jnp.arange(T)
    o_win = nsa_window_block(q.astype(jnp.float32), win, t_q, t_k)
    y = nsa_merge(o_cmp, o_slc, o_win, gates, w_out, h.dtype)
    keep = min(NSA_WINDOW, past + T)
    return y, kv_c, kv_s, win[:, wb + T - keep:]


def _complex_affine_combine(e1, e2):
    a1r, a1i, b1r, b1i = e1
    a2r, a2i, b2r, b2i = e2
    return (a2r * a1r - a2i * a1i, a2r * a1i + a2i * a1r,
            a2r * b1r - a2i * b1i + b2r, a2r * b1i + a2i * b1r + b2i)


def ssm_mixer(h, h0, lam_re, lam_im, b_re, b_im, c_re, c_im, d, log_step, w1, b1, w2, b2):
    B, T, _ = h.shape
    f32 = jnp.float32
    u = h.astype(f32).reshape(B, T, SSM_GROUPS, SSM_GROUP_WIDTH)
    lr, li = lam_re.astype(f32), lam_im.astype(f32)
    dt = jnp.exp(log_step.astype(f32))[:, None]
    mag = jnp.exp(lr * dt)
    ab_re, ab_im = mag * jnp.cos(li * dt), mag * jnp.sin(li * dt)
    den = lr * lr + li * li
    f_re = ((ab_re - 1.0) * lr + ab_im * li) / den
    f_im = (ab_im * lr - (ab_re - 1.0) * li) / den
    br, bim = b_re.astype(f32), b_im.astype(f32)
    bb_re = f_re[..., None] * br - f_im[..., None] * bim
    bb_im = f_re[..., None] * bim + f_im[..., None] * br
    cr, ci = c_re.astype(f32), c_im.astype(f32)
    tc = SSM_CHUNK if T % SSM_CHUNK == 0 else T
    uc = jnp.moveaxis(u.reshape(B, T // tc, tc, SSM_GROUPS, SSM_GROUP_WIDTH), 1, 0)

    def step(carry, u_blk):
        hr, hi = carry
        bu_re = jnp.einsum("btgi,gpi->btgp", u_blk, bb_re)
        bu_im = jnp.einsum("btgi,gpi->btgp", u_blk, bb_im)
        a_re = jnp.broadcast_to(ab_re, bu_re.shape)
        a_im = jnp.broadcast_to(ab_im, bu_re.shape)
        pr, pi, sr, si = lax.associative_scan(_complex_affine_combine, (a_re, a_im, bu_re, bu_im), axis=1)
        s_re = pr * hr[:, None] - pi * hi[:, None] + sr
        s_im = pr * hi[:, None] + pi * hr[:, None] + si
        y = jnp.einsum("btgp,gip->btgi", s_re, cr) - jnp.einsum("btgp,gip->btgi", s_im, ci)
        return (s_re[:, -1], s_im[:, -1]), y

    h0f = h0.astype(f32)
    (hr, hi), ys = lax.scan(step, (h0f[..., 0], h0f[..., 1]), uc)
    y = jnp.moveaxis(ys, 0, 1).reshape(B, T, D_MODEL) + d.astype(f32) * u.reshape(B, T, D_MODEL)
    g = jax.nn.gelu(y)
    out = (g @ w1.astype(f32) + b1.astype(f32)) * jax.nn.sigmoid(g @ w2.astype(f32) + b2.astype(f32))
    return out.astype(h.dtype), jnp.stack([hr, hi], axis=-1)


def setup_inputs(seed: int = 0) -> dict:
    keys = iter(jax.random.split(jax.random.key(seed), 48))

    def nrm(shape, scale):
        return jax.random.normal(next(keys), shape, jnp.float32) * scale

    n_pages = PAST_LEN // PAGE_SIZE
    n_used = DEC_BATCH * n_pages
    n_pool = n_used + max(1, n_used // 4)
    win_buf = min(NSA_WINDOW, PAST_LEN)
    kv_row = (2, NSA_KV_HEADS, NSA_HEAD_DIM)
    page_table = jax.random.permutation(next(keys), n_pool)[:n_used].reshape(DEC_BATCH, n_pages).astype(jnp.int32)
    nA, nB, nC = N_GMLP_LAYERS, N_NSA_LAYERS, N_SSM_LAYERS
    lam_im0 = jnp.pi * jnp.arange(SSM_STATE, dtype=jnp.float32)
    return {
        "x_prompt": nrm((BATCH, SEQ, D_MODEL), 1.0),
        "x_sample": nrm((DEC_BATCH, DEC_SEQ, D_MODEL), 1.0),
        "cache_nsa_cmp": nrm((nB, n_pool, PAGE_SIZE) + kv_row, 1.0),
        "cache_nsa_slc": nrm((nB, n_pool, PAGE_SIZE) + kv_row, 1.0),
        "cache_nsa_win": nrm((nB, DEC_BATCH, win_buf) + kv_row, 1.0),
        "state_ssm": nrm((nC, DEC_BATCH, SSM_GROUPS, SSM_STATE, 2), 0.5),
        "page_table": page_table,
        "c_prompt": nrm((BATCH, D_MODEL), 1.0),
        "c_sample": nrm((DEC_BATCH, D_MODEL), 1.0),
        "w_mod": nrm((DEPTH, D_MODEL, 6 * D_MODEL), 0.5 * D_MODEL ** -0.5),
        "b_mod": nrm((DEPTH, 6 * D_MODEL), 0.02),
        "norm_g": 1.0 + nrm((DEPTH, 4, D_MODEL), 0.05),
        "ffn_w_gate": nrm((DEPTH, D_MODEL, D_FF), D_MODEL ** -0.5),
        "ffn_w_up": nrm((DEPTH, D_MODEL, D_FF), D_MODEL ** -0.5),
        "ffn_w_down": nrm((DEPTH, D_FF, D_MODEL), D_FF ** -0.5),
        "gmlp_w_in": nrm((nA, D_MODEL, 2 * GMLP_HALF), D_MODEL ** -0.5),
        "gmlp_b_in": nrm((nA, 2 * GMLP_HALF), 0.02),
        "gmlp_ln_g": 1.0 + nrm((nA, GMLP_HALF), 0.05),
        "gmlp_ln_b": nrm((nA, GMLP_HALF), 0.02),
        "gmlp_w_s": nrm((nA, GMLP_GROUPS, GMLP_CHUNK, GMLP_CHUNK), GMLP_CHUNK ** -0.5),
        "gmlp_b_s": 1.0 + nrm((nA, GMLP_GROUPS, GMLP_CHUNK), 0.1),
        "gmlp_w_out": nrm((nA, GMLP_HALF, D_MODEL), GMLP_HALF ** -0.5),
        "nsa_w_in": nrm((nB, D_MODEL, NSA_IN_COLS), D_MODEL ** -0.5),
        "nsa_w_cmp1": nrm((nB, 2, NSA_BLOCK, NSA_HEAD_DIM, NSA_CMP_HIDDEN), (NSA_BLOCK * NSA_HEAD_DIM) ** -0.5),
        "nsa_w_cmp2": nrm((nB, 2, NSA_CMP_HIDDEN, NSA_HEAD_DIM), NSA_CMP_HIDDEN ** -0.5),
        "nsa_pe_cmp": nrm((nB, 2, NSA_BLOCK, NSA_HEAD_DIM), 0.1),
        "nsa_w_out": nrm((nB, NSA_Q_COLS, D_MODEL), NSA_Q_COLS ** -0.5),
        "ssm_lambda_re": -0.5 + nrm((nC, SSM_GROUPS, SSM_STATE), 0.01),
        "ssm_lambda_im": lam_im0 + nrm((nC, SSM_GROUPS, SSM_STATE), 0.01),
        "ssm_b_re": nrm((nC, SSM_GROUPS, SSM_STATE, SSM_GROUP_WIDTH), (2 * SSM_GROUP_WIDTH) ** -0.5),
        "ssm_b_im": nrm((nC, SSM_GROUPS, SSM_STATE, SSM_GROUP_WIDTH), (2 * SSM_GROUP_WIDTH) ** -0.5),
        "ssm_c_re": nrm((nC, SSM_GROUPS, SSM_GROUP_WIDTH, SSM_STATE), (2 * SSM_STATE) ** -0.5),
        "ssm_c_im": nrm((nC, SSM_GROUPS, SSM_GROUP_WIDTH, SSM_STATE), (2 * SSM_STATE) ** -0.5),
        "ssm_d": nrm((nC, D_MODEL), 0.5),
        "ssm_log_step": jax.random.uniform(next(keys), (nC, SSM_GROUPS), jnp.float32,
                                           minval=math.log(DT_MIN), maxval=math.log(DT_MAX)),
        "ssm_w_glu1": nrm((nC, D_MODEL, D_MODEL), D_MODEL ** -0.5),
        "ssm_b_glu1": nrm((nC, D_MODEL), 0.02),
        "ssm_w_glu2": nrm((nC, D_MODEL, D_MODEL), D_MODEL ** -0.5),
        "ssm_b_glu2": nrm((nC, D_MODEL), 0.02),
    }


def reference(x_prompt, x_sample, cache_nsa_cmp, cache_nsa_slc, cache_nsa_win, state_ssm, page_table,
              c_prompt, c_sample, w_mod, b_mod, norm_g, ffn_w_gate, ffn_w_up, ffn_w_down,
              gmlp_w_in, gmlp_b_in, gmlp_ln_g, gmlp_ln_b, gmlp_w_s, gmlp_b_s, gmlp_w_out,
              nsa_w_in, nsa_w_cmp1, nsa_w_cmp2, nsa_pe_cmp, nsa_w_out,
              ssm_lambda_re, ssm_lambda_im, ssm_b_re, ssm_b_im, ssm_c_re, ssm_c_im, ssm_d, ssm_log_step,
              ssm_w_glu1, ssm_b_glu1, ssm_w_glu2, ssm_b_glu2):
    xp, xs = x_prompt, x_sample
    cmp_p, cmp_s, slc_p, slc_s, win_p, win_s = [], [], [], [], [], []
    ssm_p, ssm_s, gv_s = [], [], []
    for i in range(DEPTH):
        j = i // N_MIXERS
        mp = adaln(c_prompt, w_mod[i], b_mod[i])
        ms = adaln(c_sample, w_mod[i], b_mod[i])
        hp = modulate(xp, norm_g[i, 0], mp[0], mp[1])
        hs = modulate(xs, norm_g[i, 0], ms[0], ms[1])
        if i % N_MIXERS == 0:
            gw = (gmlp_w_in[j], gmlp_b_in[j], gmlp_ln_g[j], gmlp_ln_b[j], gmlp_w_s[j], gmlp_b_s[j], gmlp_w_out[j])
            yp, _ = gmlp_mixer(hp, *gw)
            ys, v_new = gmlp_mixer(hs, *gw)
            gv_s.append(v_new)
        elif i % N_MIXERS == 1:
            nw = (nsa_w_in[j], nsa_w_cmp1[j], nsa_w_cmp2[j], nsa_pe_cmp[j], nsa_w_out[j])
            yp, kc_p, ks_p, kw_p = nsa_prompt(hp, *nw)
            ys, kc_s, ks_s, kw_s = nsa_sample(hs, cache_nsa_cmp, cache_nsa_slc, cache_nsa_win[j], page_table, j, *nw)
            cmp_p.append(kc_p)
            cmp_s.append(kc_s)
            slc_p.append(ks_p)
            slc_s.append(ks_s)
            win_p.append(kw_p)
            win_s.append(kw_s)
        else:
            sw = (ssm_lambda_re[j], ssm_lambda_im[j], ssm_b_re[j], ssm_b_im[j], ssm_c_re[j], ssm_c_im[j],
                  ssm_d[j], ssm_log_step[j], ssm_w_glu1[j], ssm_b_glu1[j], ssm_w_glu2[j], ssm_b_glu2[j])
            h0 = jnp.zeros((hp.shape[0], SSM_GROUPS, SSM_STATE, 2), jnp.float32)
            yp, st_p = ssm_mixer(hp, h0, *sw)
            ys, st_s = ssm_mixer(hs, state_ssm[j], *sw)
            ssm_p.append(st_p)
            ssm_s.append(st_s)
        xp = xp + mp[2] * rmsnorm(yp, norm_g[i, 1])
        xs = xs + ms[2] * rmsnorm(ys, norm_g[i, 1])
        hp = modulate(xp, norm_g[i, 2], mp[3], mp[4])
        hs = modulate(xs, norm_g[i, 2], ms[3], ms[4])
        xp = xp + mp[5] * rmsnorm(swiglu(hp, ffn_w_gate[i], ffn_w_up[i], ffn_w_down[i]), norm_g[i, 3])
        xs = xs + ms[5] * rmsnorm(swiglu(hs, ffn_w_gate[i], ffn_w_up[i], ffn_w_down[i]), norm_g[i, 3])
    return (xp, xs, jnp.stack(cmp_p), jnp.stack(cmp_s), jnp.stack(slc_p), jnp.stack(slc_s),
            jnp.stack(win_p), jnp.stack(win_s), jnp.stack(ssm_p), jnp.stack(ssm_s), jnp.stack(gv_s))
```

```python
import os
import numpy as np
from contextlib import ExitStack
import concourse.bass as bass
import concourse.mybir as mybir
from concourse.bass_utils import run_bass_kernel_spmd

F32 = mybir.dt.float32
BF16 = mybir.dt.bfloat16
I32 = mybir.dt.int32
AF = mybir.ActivationFunctionType
ALU = mybir.AluOpType
P = 128
D = 1024
KD = 8
DFF = 2816
KF = 22
SLOTW = 4096
RMS_EPS = 1e-6
LN_EPS = 1e-5


class Res:
    __slots__ = ("name", "w", "r", "dsem", "dcnt", "outanchor")

    def __init__(self, name):
        self.name = name
        self.w = None
        self.r = {}
        self.dsem = None
        self.dcnt = 0
        self.outanchor = None


class Sched:
    def __init__(self, nc, es):
        self.nc = nc
        self.es = es
        self.eng = {"pe": nc.tensor, "act": nc.scalar, "dve": nc.vector, "pool": nc.gpsimd, "sp": nc.sync}
        self.sem = {k: es.enter_context(nc.semaphore("S_" + k)) for k in self.eng}
        self.cnt = {k: 0 for k in self.eng}
        self.known = {k: {} for k in self.eng}
        self.nsem = 0
        self.out_anchors = []

    def _wait(self, e, tok):
        sem, val, src = tok[0], tok[1], tok[2]
        if src == "dma":
            val = tok[3].dcnt
        key = id(sem)
        if self.known[e].get(key, 0) >= val:
            return
        self.eng[e].wait_ge(sem, val)
        self.known[e][key] = val

    def _deps(self, e, reads, writes):
        toks = []
        for r in reads:
            if r.w is not None:
                toks.append(r.w)
        for w in writes:
            if w.w is not None:
                toks.append(w.w)
            toks.extend(w.r.values())
        for t in toks:
            if t[2] == "pe" and e == "pe":
                continue
            self._wait(e, t)

    def op(self, e, fn, reads=(), writes=()):
        self._deps(e, reads, writes)
        ins = fn(self.eng[e])
        self.cnt[e] += 1
        ins.then_inc(self.sem[e], 1)
        tok = (self.sem[e], self.cnt[e], e)
        for r in reads:
            r.r[e] = tok
        for w in writes:
            w.w = tok
            w.r = {}
        return ins

    def dma(self, q, pairs, reads=(), writes=(), anchor=None, serialize=True, **kw):
        self._deps(q, reads, writes)
        if anchor is None:
            if writes:
                anchor = writes[0]
            else:
                if reads[0].outanchor is None:
                    reads[0].outanchor = Res("out_" + reads[0].name)
                    self.out_anchors.append(reads[0].outanchor)
                anchor = reads[0].outanchor
        if anchor.dsem is None:
            anchor.dsem = self.es.enter_context(self.nc.semaphore("D%d" % self.nsem))
            self.nsem += 1
        elif serialize and anchor.dcnt > 0:
            self._wait(q, (anchor.dsem, anchor.dcnt, "dma", anchor))
        for (o, i) in pairs:
            self.eng[q].dma_start(out=o, in_=i, **kw).then_inc(anchor.dsem, 16)
            anchor.dcnt += 16
        tok = (anchor.dsem, anchor.dcnt, "dma", anchor)
        for r in reads:
            r.r[("dma", id(anchor.dsem))] = tok
        for w in writes:
            w.w = tok
            w.r = {}
        return anchor

    def dma_gather(self, out_ap, in_ap, idx_ap, reads=(), writes=()):
        self._deps("pool", reads, writes)
        anchor = writes[0]
        if anchor.dsem is None:
            anchor.dsem = self.es.enter_context(self.nc.semaphore("D%d" % self.nsem))
            self.nsem += 1
        elif anchor.dcnt > 0:
            self._wait("pool", (anchor.dsem, anchor.dcnt, "dma", anchor))
        self.eng["pool"].indirect_dma_start(out=out_ap, out_offset=None, in_=in_ap,
                                            in_offset=bass.IndirectOffsetOnAxis(ap=idx_ap, axis=0)).then_inc(anchor.dsem, 16)
        anchor.dcnt += 16
        tok = (anchor.dsem, anchor.dcnt, "dma", anchor)
        for r in reads:
            r.r[("dma", id(anchor.dsem))] = tok
        for w in writes:
            w.w = tok
            w.r = {}

    def finish(self):
        for a in self.out_anchors:
            if a.dsem is not None:
                self._wait("sp", (a.dsem, a.dcnt, "dma", a))


class Buf:
    def __init__(self, S, name, shape, dt, psum=False):
        nc = S.nc
        if psum:
            self.t = S.es.enter_context(nc.psum_tensor(name, shape, dt))
        else:
            self.t = S.es.enter_context(nc.sbuf_tensor(name, shape, dt))
        self.res = Res(name)
        self.shape = shape

    def __getitem__(self, idx):
        return self.t[idx]


class Cfg:
    def __init__(self, seq=8192, nsamp=4, past=8192, layers=(0, 1, 2, 3), nslab_max=176):
        self.SEQ = seq
        self.TT = 512
        self.NT = seq // 512
        self.NSAMP = nsamp
        self.PAST = past
        self.layers = tuple(layers)
        self.NSLAB = nslab_max


class WStream:
    def __init__(self, S, wdram, nslot, nslab):
        self.S = S
        self.wdram = wdram
        self.nslot = nslot
        self.slots = [Buf(S, "wslot%d" % i, [P, SLOTW], BF16) for i in range(nslot)]
        self.recipes = []
        self.index = {}
        self.pos = 0
        self.nslab = nslab

    def next(self, key, ncols, fn, npart=P):
        if key not in self.index:
            self.index[key] = len(self.recipes)
            self.recipes.append((ncols, npart, fn))
            assert len(self.recipes) <= self.nslab, "too many slabs"
        j = self.index[key]
        slot = self.slots[self.pos % self.nslot]
        self.pos += 1
        half = ncols // 2
        if ncols >= 1024:
            pairs = [(slot.t[0:npart, 0:half], self.wdram[j, 0:npart, 0:half]),
                     (slot.t[0:npart, half:ncols], self.wdram[j, 0:npart, half:ncols])]
        else:
            pairs = [(slot.t[0:npart, 0:ncols], self.wdram[j, 0:npart, 0:ncols])]
        self.S.dma("pool", pairs, writes=[slot.res])
        return slot

    def host_array(self, weights):
        arr = np.zeros((self.nslab, P, SLOTW), np.float32)
        for j, (ncols, npart, fn) in enumerate(self.recipes):
            a = np.asarray(fn(weights), np.float32)
            assert a.shape == (npart, ncols), (a.shape, npart, ncols)
            arr[j, :npart, :ncols] = a
        return arr


class Prog:
    def __init__(self, cfg):
        self.cfg = cfg
        self.nc = bass.Bass("TRN2", target_bir_lowering=False)
        self.host = {}
        self.outs = {}
        self.es = ExitStack()

    def din(self, name, shape, fn, dt=F32):
        t = self.nc.dram_tensor(name, list(shape), dt, kind="ExternalInput").ap()
        self.host[name] = (tuple(shape), fn, np.int32 if dt == I32 else np.float32)
        return t

    def dout(self, name, shape, dt=F32):
        t = self.nc.dram_tensor(name, list(shape), dt, kind="ExternalOutput").ap()
        self.outs[name] = tuple(shape)
        return t

    def sb(self, name, shape, dt):
        return Buf(self.S, name, shape, dt)

    def const(self, name, shape, fn, dt=F32, sdt=None, q="sp"):
        d = self.din(name, shape, fn, dt)
        b = self.sb("c_" + name, list(shape), sdt or dt)
        idx = tuple(slice(None) for _ in shape)
        self.S.dma(q if (sdt is None or sdt == dt) else "pool", [(b.t[idx], d[idx])], writes=[b.res])
        return b

    def psum(self):
        b = self.pbanks[self.pidx % len(self.pbanks)]
        self.pidx += 1
        return b

    def sumsq_rstd(self, src_ap, src_res, N, eps, tag):
        S = self.S
        sqb = self.tmpb
        ps = self.psum()
        for k in range(KD):
            j = k % 2
            S.op("act", lambda e, k=k, j=j: e.activation(out=sqb[:, j, 0:N], in_=src_ap[:, k, :], func=AF.Square),
                 reads=[src_res], writes=[self.tmpbres[j]])
            S.op("pe", lambda e, k=k, j=j: e.matmul(ps[:, 0:N], lhsT=self.ones_bf[:, 0:P], rhs=sqb[:, j, 0:N],
                                                    start=(k == 0), stop=(k == KD - 1)),
                 reads=[self.tmpbres[j], self.cres], writes=[ps.res])
        rs = self.rstd
        S.op("act", lambda e: e.activation(out=rs[:, 0:N], in_=ps[:, 0:N], func=AF.Sqrt,
                                           bias=self.epsb[:, 0:1], scale=1.0 / D),
             reads=[ps.res, self.epsb.res], writes=[rs.res])
        S.op("dve", lambda e: e.reciprocal(out=rs[:, 0:N], in_=rs[:, 0:N]), reads=[rs.res], writes=[rs.res])
        return rs

    def prenorm(self, N, A, Bv, col=None):
        S = self.S
        xT, hy = self.xT, self.hy
        rs = self.sumsq_rstd(xT[:, :, 0:N], xT.res, N, RMS_EPS, "pre")
        hT = self.hT
        tmp = self.tmpf
        if col is not None:
            for k in range(KD):
                S.op("dve", lambda e, k=k: e.scalar_tensor_tensor(out=tmp[:, k % 2, 0:N], in0=xT[:, k, 0:N],
                                                                   scalar=A[:, k, col:col + 1], in1=rs[:, 0:N],
                                                                   op0=ALU.mult, op1=ALU.mult),
                     reads=[xT.res, rs.res, self.modres], writes=[self.tmpres[k % 2]])
                S.op("act", lambda e, k=k: e.activation(out=hT[:, k, 0:N], in_=tmp[:, k % 2, 0:N], func=AF.Identity,
                                                        bias=Bv[:, k, col:col + 1], scale=1.0),
                     reads=[self.tmpres[k % 2], self.modres], writes=[hy.res])
        else:
            for k in range(KD):
                S.op("dve", lambda e, k=k: e.tensor_tensor(out=tmp[:, 0, 0:N], in0=xT[:, k, 0:N], in1=rs[:, 0:N], op=ALU.mult),
                     reads=[xT.res, rs.res], writes=[self.tmpres[0]])
                S.op("dve", lambda e, k=k: e.tensor_tensor(out=tmp[:, 0, 0:N], in0=tmp[:, 0, 0:N], in1=A[:, k, :], op=ALU.mult),
                     reads=[self.tmpres[0], self.modres], writes=[self.tmpres[0]])
                S.op("dve", lambda e, k=k: e.tensor_tensor(out=hT[:, k, 0:N], in0=tmp[:, 0, 0:N], in1=Bv[:, k, :], op=ALU.add),
                     reads=[self.tmpres[0], self.modres], writes=[hy.res])

    def postnorm_residual(self, N, G, col=None):
        S = self.S
        xT, yT, hy = self.xT, self.yT, self.hy
        rs = self.sumsq_rstd(yT[:, :, 0:N], hy.res, N, RMS_EPS, "post")
        tmp = self.tmpf
        for k in range(KD):
            S.op("dve", lambda e, k=k: e.tensor_tensor(out=tmp[:, k % 2, 0:N], in0=yT[:, k, 0:N], in1=rs[:, 0:N], op=ALU.mult),
                 reads=[hy.res, rs.res], writes=[self.tmpres[k % 2]])
            if col is not None:
                S.op("dve", lambda e, k=k: e.scalar_tensor_tensor(out=xT[:, k, 0:N], in0=tmp[:, k % 2, 0:N],
                                                                   scalar=G[:, k, col:col + 1], in1=xT[:, k, 0:N],
                                                                   op0=ALU.mult, op1=ALU.add),
                     reads=[self.tmpres[k % 2], xT.res, self.modres], writes=[xT.res])
            else:
                S.op("dve", lambda e, k=k: e.tensor_tensor(out=tmp[:, k % 2, 0:N], in0=tmp[:, k % 2, 0:N], in1=G[:, k, :], op=ALU.mult),
                     reads=[self.tmpres[k % 2], self.modres], writes=[self.tmpres[k % 2]])
                S.op("dve", lambda e, k=k: e.tensor_tensor(out=xT[:, k, 0:N], in0=tmp[:, k % 2, 0:N], in1=xT[:, k, 0:N], op=ALU.add),
                     reads=[self.tmpres[k % 2], xT.res], writes=[xT.res])

    def mm_fm(self, key, wfn, nk, nm, rhs_fn, rhs_res, N, evac):
        S = self.S
        for mg in range(0, nm, 4):
            mcnt = min(4, nm - mg)
            kgroups = [(k0, min(8, nk - k0)) for k0 in range(0, nk, 8)]
            if len(kgroups) == 1:
                k0, kc = kgroups[0]
                slab = self.ws.next((key, mg, 0), kc * mcnt * P,
                                    lambda w, mg=mg, mcnt=mcnt, kc=kc: slab_fm(wfn(w), 0, kc, mg, mcnt))
                for m in range(mcnt):
                    ps = self.psum()
                    for k in range(kc):
                        S.op("pe", lambda e, k=k, m=m, ps=ps, slab=slab: e.matmul(
                            ps[:, 0:N], lhsT=slab[:, (k * mcnt + m) * P:(k * mcnt + m + 1) * P], rhs=rhs_fn(k),
                            start=(k == 0), stop=(k == kc - 1)),
                            reads=[slab.res, rhs_res], writes=[ps.res])
                    evac(mg + m, ps)
            else:
                pss = [self.psum() for _ in range(mcnt)]
                for (k0, kc) in kgroups:
                    slab = self.ws.next((key, mg, k0), kc * mcnt * P,
                                        lambda w, mg=mg, mcnt=mcnt, kc=kc, k0=k0: slab_fm(wfn(w), k0, kc, mg, mcnt))
                    for m in range(mcnt):
                        for k in range(kc):
                            kg = k0 + k
                            S.op("pe", lambda e, k=k, m=m, kg=kg, slab=slab: e.matmul(
                                pss[m][:, 0:N], lhsT=slab[:, (k * mcnt + m) * P:(k * mcnt + m + 1) * P], rhs=rhs_fn(kg),
                                start=(kg == 0), stop=(kg == nk - 1)),
                                reads=[slab.res, rhs_res], writes=[pss[m].res])
                for m in range(mcnt):
                    evac(mg + m, pss[m])

    def evac_y(self, N):
        yT, hy = self.yT, self.hy

        def f(m, ps):
            self.S.op("act", lambda e: e.activation(out=yT[:, m, 0:N], in_=ps[:, 0:N], func=AF.Copy),
                      reads=[ps.res], writes=[hy.res])
        return f

    def ffn(self, l, N):
        S = self.S
        hT, hy, act = self.hT, self.hy, self.act
        gtmp = self.tmpb
        for mg in range(0, KF, 4):
            mcnt = min(4, KF - mg)
            sg = self.ws.next(("ffg", l, mg), 8 * mcnt * P, lambda w, mg=mg, mcnt=mcnt: slab_fm(w["ffn_w_gate"][l], 0, 8, mg, mcnt))
            su = self.ws.next(("ffu", l, mg), 8 * mcnt * P, lambda w, mg=mg, mcnt=mcnt: slab_fm(w["ffn_w_up"][l], 0, 8, mg, mcnt))
            for m in range(mcnt):
                pg = self.psum()
                pu = self.psum()
                for (slab, ps) in ((sg, pg), (su, pu)):
                    for k in range(KD):
                        S.op("pe", lambda e, k=k, slab=slab, ps=ps: e.matmul(
                            ps[:, 0:N], lhsT=slab[:, (k * mcnt + m) * P:(k * mcnt + m + 1) * P], rhs=hT[:, k, 0:N],
                            start=(k == 0), stop=(k == KD - 1)), reads=[slab.res, hy.res], writes=[ps.res])
                j = (mg + m) % 2
                S.op("act", lambda e, j=j, pg=pg: e.activation(out=gtmp[:, j, 0:N], in_=pg[:, 0:N], func=AF.Silu),
                     reads=[pg.res], writes=[self.tmpbres[j]])
                S.op("dve", lambda e, j=j, pu=pu, c=mg + m: e.tensor_tensor(out=act[:, c, 0:N], in0=gtmp[:, j, 0:N], in1=pu[:, 0:N], op=ALU.mult),
                     reads=[self.tmpbres[j], pu.res], writes=[self.actres])
        self.mm_fm(("ffd", l), lambda w: w["ffn_w_down"][l], KF, KD, lambda k: act[:, k, 0:N], self.actres, N, self.evac_y(N))

    def gmlp(self, j, N, sample, vout=None):
        S = self.S
        hT, hy, act = self.hT, self.hy, self.act
        uT = act
        bu = self.gm_bu[j]

        def evac_u(m, ps):
            S.op("act", lambda e: e.activation(out=uT[:, m, 0:N], in_=ps[:, 0:N], func=AF.Gelu, bias=bu[:, m:m + 1], scale=1.0),
                 reads=[ps.res, self.cres], writes=[self.actres])
        self.mm_fm(("gmu", j), lambda w: w["gmlp_w_in"][j][:, 0:D], KD, KD, lambda k: hT[:, k, 0:N], hy.res, N, evac_u)
        sv = [self.ws.next(("gmv", j, hf), 8 * 512, lambda w, hf=hf: slab_tm(w["gmlp_w_in"][j][:, D + hf * 512:D + (hf + 1) * 512]))
              for hf in range(2)]
        nsub = 1 if sample else N // P
        TS = N if sample else P
        vtm, vln = self.vtm, self.vln
        g_bv = g_lng = g_lnb = self.gm_row

        def load_row(i3):
            S.dma("sp", [(self.gm_row[:, :], self.gm_rows_d[j][i3][:, :])], writes=[self.gm_row.res])
        for s in range(nsub):
            load_row(0)
            for hf in range(2):
                ps = self.psum()
                for k in range(KD):
                    S.op("pe", lambda e, k=k, ps=ps, hf=hf: e.matmul(ps[0:TS, 0:512], lhsT=hT[:, k, s * P:s * P + TS],
                                                                     rhs=sv[hf][:, k * 512:(k + 1) * 512],
                                                                     start=(k == 0), stop=(k == KD - 1)),
                         reads=[sv[hf].res, hy.res], writes=[ps.res])
                S.op("dve", lambda e, ps=ps, hf=hf: e.tensor_tensor(out=vtm[0:TS, hf * 512:(hf + 1) * 512], in0=ps[0:TS, 0:512],
                                                                    in1=g_bv[0:TS, hf * 512:(hf + 1) * 512], op=ALU.add),
                     reads=[ps.res, g_bv.res], writes=[self.vtmres])
            S.op("act", lambda e: e.activation(out=vtm[0:TS, :], in_=vtm[0:TS, :], func=AF.Gelu), reads=[self.vtmres], writes=[self.vtmres])
            st = self.stat
            for hf in range(2):
                S.op("dve", lambda e, hf=hf: e.bn_stats(out=st[0:TS, hf * 6:(hf + 1) * 6], in_=vtm[0:TS, hf * 512:(hf + 1) * 512]),
                     reads=[self.vtmres], writes=[self.statres])
            S.op("dve", lambda e: e.bn_aggr(out=st[0:TS, 12:14], in_=st[0:TS, 0:12]), reads=[self.statres], writes=[self.statres])
            S.op("act", lambda e: e.activation(out=st[0:TS, 14:15], in_=st[0:TS, 13:14], func=AF.Sqrt, bias=self.lnepsb[0:TS, 0:1], scale=1.0),
                 reads=[self.statres, self.cres], writes=[self.statres])
            S.op("dve", lambda e: e.reciprocal(out=st[0:TS, 15:16], in_=st[0:TS, 14:15]), reads=[self.statres], writes=[self.statres])
            S.op("dve", lambda e: e.tensor_scalar(out=vtm[0:TS, :], in0=vtm[0:TS, :], scalar1=st[0:TS, 12:13], scalar2=st[0:TS, 15:16],
                                                  op0=ALU.subtract, op1=ALU.mult),
                 reads=[self.vtmres, self.statres], writes=[self.vtmres])
            load_row(1)
            S.op("dve", lambda e: e.tensor_tensor(out=vtm[0:TS, :], in0=vtm[0:TS, :], in1=g_lng[0:TS, :], op=ALU.mult),
                 reads=[self.vtmres, g_lng.res], writes=[self.vtmres])
            load_row(2)
            S.op("dve", lambda e: e.tensor_tensor(out=vtm[0:TS, :], in0=vtm[0:TS, :], in1=g_lnb[0:TS, :], op=ALU.add),
                 reads=[self.vtmres, g_lnb.res], writes=[self.vtmres])
            S.op("act", lambda e: e.activation(out=vln[0:TS, :], in_=vtm[0:TS, :], func=AF.Copy), reads=[self.vtmres], writes=[self.vlnres])
            if vout is not None:
                S.dma("sp", [(vout, vtm[0:TS, :])], reads=[self.vtmres])
            for g in range(8):
                ps = self.psum()
                if sample:
                    rhs_w = self.gm_wsS[j][0:TS, g, 0:TS]
                    rhs_b = self.gm_bsS[j][0:1, g, 0:TS]
                else:
                    rhs_w = self.gm_ws[j][:, g, :]
                    rhs_b = self.gm_bs[j][0:1, g, :]
                S.op("pe", lambda e, g=g, ps=ps, rhs_w=rhs_w: e.matmul(ps[:, 0:TS], lhsT=vln[0:TS, g * P:(g + 1) * P], rhs=rhs_w, start=True, stop=False),
                     reads=[self.vlnres, self.cres], writes=[ps.res])
                S.op("pe", lambda e, g=g, ps=ps, rhs_b=rhs_b: e.matmul(ps[:, 0:TS], lhsT=self.ones_bf[0:1, 0:P], rhs=rhs_b, start=False, stop=True),
                     reads=[self.cres, self.ones_bf.res], writes=[ps.res])
                S.op("dve", lambda e, g=g, ps=ps: e.tensor_tensor(out=uT[:, g, s * P:s * P + TS], in0=uT[:, g, s * P:s * P + TS], in1=ps[:, 0:TS], op=ALU.mult),
                     reads=[ps.res, self.actres], writes=[self.actres])
        self.mm_fm(("gmo", j), lambda w: w["gmlp_w_out"][j], KD, KD, lambda k: uT[:, k, 0:N], self.actres, N, self.evac_y(N))

    def build(self):
        cfg = self.cfg
        nc = self.nc
        es = self.es
        S = self.S = Sched(nc, es)
        TT, NT, NS = cfg.TT, cfg.NT, cfg.NSAMP
        NC = 1 + NS
        self.NC = NC
        wdram = self.din("wts", [cfg.NSLAB, P, SLOTW], None)
        xp = self.din("xp", [cfg.SEQ, D], lambda I, c: I["x_prompt"][c % 2])
        xs = self.din("xs", [NS, D], lambda I, c: I["x_sample"][c * NS:(c + 1) * NS, 0])
        yp = self.dout("yp", [cfg.SEQ, D])
        ys = self.dout("ys", [NS, D])
        gv = self.dout("gv", [2, NS, D])
        self.ws = WStream(S, wdram, 4, cfg.NSLAB)
        self.xT = self.sb("xT", [P, KD, TT], F32)
        self.hy = self.sb("hy", [P, KD, TT], F32)
        self.yT = self.hy.t
        self.hT = self.hy.t[:].rearrange("p k t -> p (k t)")[:, 0:KD * TT // 2].bitcast(BF16).rearrange("p (k t) -> p k t", k=KD)
        self.act = self.sb("act", [P, KF, TT], BF16)
        self.actres = self.act.res
        self.rstd = self.sb("rstd", [P, TT], F32)
        self.tmpf = self.sb("tmpf", [P, 2, TT], F32)
        self.tmpres = [Res("tmpf0"), Res("tmpf1")]
        self.tmpb = self.sb("tmpb", [P, 2, TT], BF16)
        self.tmpbres = [Res("tmpb0"), Res("tmpb1")]
        self.vtm = self.act.t[:, 8:12, :].rearrange("p k t -> p (k t)").bitcast(F32)
        self.vtmres = self.actres
        self.vln = self.act.t[:, 12:14, :].rearrange("p k t -> p (k t)")
        self.vlnres = self.actres
        self.stat = self.sb("stat", [P, 16], F32)
        self.statres = self.stat.res
        self.xin = Buf.__new__(Buf)
        self.xin.t = self.hy.t[:].rearrange("p k t -> p (k t)").rearrange("p (s d) -> p s d", s=4)
        self.xin.res = self.hy.res
        self.pbanks = [Buf(S, "ps%d" % i, [P, 512], F32, psum=True) for i in range(6)]
        self.pidx = 0
        self.pacc = [Buf(S, "pacc%d" % i, [P, 512], F32, psum=True) for i in range(2)]
        self.paidx = 0
        self.vout_anchor = Res("vout")
        S.out_anchors.append(self.vout_anchor)
        self.cres = Res("consts")
        self.cbar = self.sb("cbar", [P, 2], F32)
        self.modres = Res("mod")

        self.cres_hw = Res("consts_hw")
        self.cres_sw = Res("consts_sw")
        cload = self._cload
        self.ident = cload("ident", [P, P], lambda I, c: np.eye(P, dtype=np.float32))
        self.ones_bf = cload("ones", [P, P], lambda I, c: np.ones((P, P), np.float32), BF16)
        self.epsb = cload("epsb", [P, 1], lambda I, c: np.full((P, 1), RMS_EPS, np.float32))
        self.lnepsb = cload("lnepsb", [P, 1], lambda I, c: np.full((P, 1), LN_EPS, np.float32))
        nG = 2
        self.gm_bu = [cload("gm_bu%d" % j, [P, KD], lambda I, c, j=j: I["gmlp_b_in"][j][0:D].reshape(KD, P).T) for j in range(nG)]
        self.gm_rows_d = [[self.din("gm_bv%d" % j, [P, D], lambda I, c, j=j: np.broadcast_to(I["gmlp_b_in"][j][D:2 * D], (P, D))),
                           self.din("gm_lng%d" % j, [P, D], lambda I, c, j=j: np.broadcast_to(I["gmlp_ln_g"][j], (P, D))),
                           self.din("gm_lnb%d" % j, [P, D], lambda I, c, j=j: np.broadcast_to(I["gmlp_ln_b"][j], (P, D)))] for j in range(nG)]
        self.gm_row = self.sb("gm_row", [P, D], F32)
        self.gm_ws = [cload("gm_ws%d" % j, [P, 8, P], lambda I, c, j=j: np.where(np.tril(np.ones((P, P), bool))[None], I["gmlp_w_s"][j], 0.0).transpose(2, 0, 1), BF16) for j in range(nG)]
        self.gm_bs = [cload("gm_bs%d" % j, [1, 8, P], lambda I, c, j=j: I["gmlp_b_s"][j][None], BF16) for j in range(nG)]

        def wsS(I, c, j):
            a = np.zeros((NS, 8, NS), np.float32)
            for g in range(8):
                for q in range(NS):
                    a[q, g, q] = I["gmlp_w_s"][j][g, 0, 0]
            return a
        self.gm_wsS = [cload("gm_wsS%d" % j, [NS, 8, NS], lambda I, c, j=j: wsS(I, c, j), BF16) for j in range(nG)]
        self.gm_bsS = [cload("gm_bsS%d" % j, [1, 8, NS], lambda I, c, j=j: np.broadcast_to(I["gmlp_b_s"][j][:, 0][None, :, None], (1, 8, NS)), BF16) for j in range(nG)]
        scT = cload("scT", [P, KD, NC], lambda I, c: np.concatenate([I["c_prompt"][c % 2][None], I["c_sample"][c * NS:(c + 1) * NS]], 0).T.reshape(KD, P, NC).transpose(1, 0, 2))
        bmod = cload("bmod", [P, 4, 6, KD], lambda I, c: I["b_mod"].reshape(4, 6, KD, P).transpose(3, 0, 1, 2))
        ng = cload("ng", [P, 4, 4, KD], lambda I, c: I["norm_g"].reshape(4, 4, KD, P).transpose(3, 0, 1, 2))
        if 2 in cfg.layers:
            self.s5d = cload("s5d", [P, KD], lambda I, c: I["ssm_d"][0].reshape(KD, P).T)
            self.s5b1 = cload("s5b1", [P, KD], lambda I, c: I["ssm_b_glu1"][0].reshape(KD, P).T)
            self.s5b2 = cload("s5b2", [P, KD], lambda I, c: I["ssm_b_glu2"][0].reshape(KD, P).T)
        if 1 in cfg.layers:
            self.nsa_consts()
        self.consts_barrier()
        if 2 in cfg.layers:
            self.s5_setup()
        if 1 in cfg.layers:
            self.nsa_setup()
        scb = self.sb("scb", [P, KD, NC], BF16)
        S.op("act", lambda e: e.activation(out=scb[:, :, :], in_=scT[:, :, :], func=AF.Silu), reads=[self.cres], writes=[scb.res])
        self.mod = self.sb("mod", [P, 4, 6, KD, NC], F32)
        self.mod.res = self.modres
        for l in cfg.layers:
            for jj in range(6):
                def ev(m, ps, l=l, jj=jj):
                    S.op("act", lambda e: e.activation(out=self.mod[:, l, jj, m, :], in_=ps[:, 0:NC], func=AF.Identity,
                                                       bias=bmod[:, l, jj, m:m + 1], scale=1.0),
                         reads=[ps.res, self.cres], writes=[self.modres])
                self.mm_fm(("mod", l, jj), lambda w, l=l, jj=jj: w["w_mod"][l][:, jj * D:(jj + 1) * D], KD, KD,
                           lambda k: scb[:, k, :], scb.res, NC, ev)
            for (sc_i, g_i) in ((1, 0), (4, 2)):
                for col in range(NC):
                    S.op("dve", lambda e, col=col, sc_i=sc_i, g_i=g_i: e.scalar_tensor_tensor(
                        out=self.mod[:, l, sc_i, :, col], in0=self.mod[:, l, sc_i, :, col], scalar=1.0, in1=ng[:, l, g_i, :],
                        op0=ALU.add, op1=ALU.mult), reads=[self.modres, self.cres], writes=[self.modres])
            for (ga_i, g_i) in ((2, 1), (5, 3)):
                for col in range(NC):
                    S.op("dve", lambda e, col=col, ga_i=ga_i, g_i=g_i: e.tensor_tensor(
                        out=self.mod[:, l, ga_i, :, col], in0=self.mod[:, l, ga_i, :, col], in1=ng[:, l, g_i, :], op=ALU.mult),
                        reads=[self.modres, self.cres], writes=[self.modres])
        yout_anchor = Res("yout")
        S.out_anchors.append(yout_anchor)
        for it in range(NT + 1):
            sample = (it == NT)
            N = NS if sample else TT
            nsub = 1 if sample else 4
            TS = NS if sample else P
            xin = self.xin
            if sample:
                S.dma("sp", [(xin[0:NS, 0, :], xs[:, :])], writes=[xin.res])
            else:
                S.dma("sp", [(xin[:, s, :], xp[it * TT + s * P:it * TT + (s + 1) * P, :]) for s in range(4)], writes=[xin.res])
            for k in range(KD):
                ps = self.psum()
                for s in range(nsub):
                    S.op("pe", lambda e, k=k, s=s, ps=ps: e.transpose(ps[:, s * P:s * P + TS], xin[0:TS, s, k * P:(k + 1) * P], self.ident[0:TS, 0:TS]),
                         reads=[xin.res, self.cres], writes=[ps.res])
                S.op("act", lambda e, k=k, ps=ps: e.activation(out=self.xT[:, k, 0:N], in_=ps[:, 0:N], func=AF.Copy),
                     reads=[ps.res], writes=[self.xT.res])
            for l in cfg.layers:
                mod = self.mod
                if sample:
                    A1, B1, G1 = mod[:, l, 1, :, 1:NC], mod[:, l, 0, :, 1:NC], mod[:, l, 2, :, 1:NC]
                    A2, B2, G2 = mod[:, l, 4, :, 1:NC], mod[:, l, 3, :, 1:NC], mod[:, l, 5, :, 1:NC]
                    col = None
                else:
                    A1, B1, G1 = mod[:, l, 1], mod[:, l, 0], mod[:, l, 2]
                    A2, B2, G2 = mod[:, l, 4], mod[:, l, 3], mod[:, l, 5]
                    col = 0
                self.prenorm(N, A1, B1, col)
                kind = l % 3
                if kind == 0:
                    self.gmlp(l // 3, N, sample, vout=(gv[l // 3, :, :] if sample else None))
                elif kind == 1:
                    self.nsa(it, N, sample)
                else:
                    self.s5(it, N, sample)
                self.postnorm_residual(N, G1, col)
                self.prenorm(N, A2, B2, col)
                self.ffn(l, N)
                self.postnorm_residual(N, G2, col)
            xo = self.xin
            for s in range(nsub):
                for kq in range(2):
                    ps = self.psum()
                    for kk in range(4):
                        k = kq * 4 + kk
                        S.op("pe", lambda e, k=k, kk=kk, s=s, ps=ps: e.transpose(ps[0:TS, kk * P:(kk + 1) * P], self.xT[:, k, s * P:s * P + TS], self.ident[:, :]),
                             reads=[self.xT.res, self.cres], writes=[ps.res])
                    S.op("act", lambda e, s=s, kq=kq, ps=ps: e.activation(out=xo[0:TS, s, kq * 512:(kq + 1) * 512], in_=ps[0:TS, 0:512], func=AF.Copy),
                         reads=[ps.res], writes=[xo.res])
            if sample:
                S.dma("sp", [(ys[:, :], xo[0:NS, 0, :])], reads=[xo.res])
            else:
                S.dma("sp", [(yp[it * TT + s * P:it * TT + (s + 1) * P, :], xo[:, s, :]) for s in range(4)], reads=[xo.res])
        S.finish()
        return nc

    def nsa_setup(self):
        S = self.S
        cfg = self.cfg
        NT, NS, SEQ = cfg.NT, cfg.NSAMP, cfg.SEQ
        self.HE = [0, 1, 2, 3, 8, 9, 10, 11]
        self.HO = [4, 5, 6, 7, 12, 13, 14, 15]
        self.KTs = self.sb("KTs", [P, 2, SEQ], BF16)
        NKT = SEQ // P
        self.Vs = self.sb("Vg", [P, NKT, 66], BF16)
        self.Vst = self.sb("Vst", [P, 4, 66], BF16)
        self.vscr = self.nc.dram_tensor("vscr", [4, P, NKT, 66], BF16, kind="Internal").ap()
        self.vscr_res = Res("vscr")
        self.KTw = self.sb("KTw", [P, 2, 2, 512], BF16)
        self.Vw = self.sb("Vw", [P, 2, 4, 4, 66], BF16)
        self.KcT = self.sb("KcT", [P, 2, P], BF16)
        self.Vc = self.sb("Vc", [P, 4, 66], BF16)
        S.op("pool", lambda e: e.memset(self.Vst[:, :, :], 1.0), writes=[self.Vst.res])
        S.op("pool", lambda e: e.memset(self.Vw[:, :, :, :, :], 1.0), writes=[self.Vw.res])
        S.op("pool", lambda e: e.memset(self.KcT[:, :, :], 0.0), writes=[self.KcT.res])
        S.op("pool", lambda e: e.memset(self.Vc[:, :, 0:64], 0.0), writes=[self.Vc.res])
        self.Vc32 = self.sb("Vc32", [P, 256], F32)
        S.op("pool", lambda e: e.memset(self.Vc32[:, :], 0.0), writes=[self.Vc32.res])
        S.op("pool", lambda e: e.memset(self.Vc[:, :, 64:66], 1.0), writes=[self.Vc.res])
        self.XT = Buf.__new__(Buf)
        self.XT.t = self.act.t[:, 16:20, :]
        self.XT.res = Res("XT")
        self.gates = self.sb("gates", [P, 4, 48], F32)
        self.negmT = self.sb("negmT", [P, 2, 5, 512], BF16)
        S.op("pool", lambda e: e.memset(self.negmT[:, :, :, :], 0.0), writes=[self.negmT.res])
        self.tk1 = Buf.__new__(Buf)
        self.tk1.t = self.tmpf.t[:, 0, :].rearrange("p (a b) -> p a b", a=4)
        self.tk1.res = self.tmpres[0]
        self.tk2 = Buf.__new__(Buf)
        self.tk2.t = self.tmpf.t[:, 1, 0:384].rearrange("p (a b) -> p a b", a=3)
        self.tk2.res = self.tmpres[1]
        self.tks = self.sb("tks", [P, 32], F32)
        self.negm = self.sb("negm", [P, P], BF16)
        self.hidK = self.sb("hidK", [P, 2, 2, 16], BF16)
        self.hidV = self.sb("hidV", [P, 4, P], BF16)
        self.pebias = self.sb("pebias", [P, 2], F32)
        self.mk = self.sb("mk", [P, 3, 4, P], BF16)
        self.f4 = self.sb("f4", [P, 8], F32)


        def masks(I, c):
            a = np.zeros((NT, P, 3, 4, P), np.float32)
            n = np.arange(P)[None, :]
            for it in range(NT):
                for s_ in range(4):
                    t = (it * 512 + s_ * 128 + np.arange(P))[:, None]
                    jt = t // 64
                    a[it, :, 0, s_, :] = np.where((n + 1) * 64 <= t + 1, 0.0, -30000.0)
                    vd = n <= jt
                    f = vd & ((n == 0) | (n == jt) | (n == jt - 1))
                    a[it, :, 1, s_, :] = (vd & ~f).astype(np.float32)
                    a[it, :, 2, s_, :] = np.where(f, 1.0e4, np.where(vd, 0.0, -1.0))
            return a
        self.mk_d = self.din("nsa_masks", [NT, P, 3, 4, P], masks)

        def cmpbt(I, c):
            a = np.zeros((NT, P, 512), np.float32)
            n = np.arange(P)[:, None]
            for it in range(NT):
                t = (it * 512 + np.arange(512))[None, :]
                a[it] = np.where((n + 1) * 64 <= t + 1, 0.0, -30000.0)
            return a
        self.cmpbt_d = self.din("nsa_cmpbt", [NT, P, 512], cmpbt)
        self.cmp_p = self.dout("cmp_p", [SEQ, 512])
        self.slc_p = self.dout("slc_p", [SEQ, 512])
        self.win_p = self.dout("win_p", [512, 512])
        self.kv_anchor = Res("kvout")
        S.out_anchors.append(self.kv_anchor)
        NPG = cfg.PAST // P
        self.NPG = NPG
        self.cmp_s = self.dout("cmp_s", [NS, 512])
        self.slc_s = self.dout("slc_s", [NS, 512])
        self.win_s = self.dout("win_s", [NS, 512, 512])
        self.cache_c = self.din("cache_c", [2560 * P, 512], lambda I, c: I["cache_nsa_cmp"][0].reshape(-1, 512))
        self.cache_s = self.din("cache_s", [2560 * P, 512], lambda I, c: I["cache_nsa_slc"][0].reshape(-1, 512))
        self.cache_w = self.din("cache_w", [NS, 512, 512], lambda I, c: I["cache_nsa_win"][0][c * NS:(c + 1) * NS].reshape(NS, 512, 512))
        pt_d = self.din("ptab", [1, NS * NPG], lambda I, c: I["page_table"][c * NS:(c + 1) * NS].reshape(1, NS * NPG), I32)
        hyflat = self.hy.t[:].rearrange("p k t -> p (k t)")
        self.PGn = Buf.__new__(Buf)
        self.PGn.t = hyflat[:, 2048:3072].rearrange("p (a b) -> p a b", a=2)
        self.pgn_res = Res("PGn")
        self.PGn.res = self.pgn_res
        self.XTs = Buf.__new__(Buf)
        self.XTs.t = hyflat[:, 0:2048].bitcast(BF16).rearrange("p (a b) -> p a b", a=4)
        self.XTs.res = self.hy.res
        self.KTp = [self.sb("KTp%d" % i, [P, 2, P], BF16) for i in range(2)]
        self.Vp = [self.sb("Vp%d" % i, [P, 4, 2, P], BF16) for i in range(1)]
        for i in range(1):
            S.op("pool", lambda e, i=i: e.memset(self.Vp[i][:, :, :, :], 0.0), writes=[self.Vp[i].res])
        self.pTs = [self.sb("pTs%d" % i, [P, 16], BF16) for i in range(2)]
        self.KcS = self.sb("KcS", [P, 2, P], BF16)
        self.hidVs = self.sb("hidVs", [P, 4, P], BF16)
        self.impT = self.sb("impT", [P, 4 * NS], F32)
        self.sw = self.sb("sampw", [P, 64], F32)
        self.sc16 = self.sb("sc16", [4 * NS, 2, P], F32)
        self.sk16 = self.sb("sk16", [4 * NS, 32], F32)
        self.ng16 = self.sb("ng16", [4 * NS, P], BF16)
        self.negS = self.sb("negS", [P, 2, 4 * NS], BF16)
        S.op("pool", lambda e: e.memset(self.negS[:, :, :], 0.0), writes=[self.negS.res])
        self.res_o = self.sb("res_o", [P, 3, NS, 8], F32)
        self.res_s = self.sb("res_s", [P, 3, NS, 8], F32)
        self.idxf = self.sb("idxf", [P, NS * NPG], F32)
        self.idxi = self.sb("idxi", [P, NS * NPG], I32)
        pti = self.sb("pti", [P, NS * NPG], I32)
        S.dma("sp", [(pti[:, :], pt_d[0:1, :].to_broadcast([P, NS * NPG]))], writes=[pti.res])
        S.op("dve", lambda e: e.tensor_copy(out=self.idxf[:, :], in_=pti[:, :]), reads=[pti.res], writes=[self.idxf.res])
        S.op("dve", lambda e: e.tensor_scalar(out=self.idxf[:, :], in0=self.idxf[:, :], scalar1=float(P), scalar2=self.iota_p[:, 0:1], op0=ALU.mult, op1=ALU.add),
             reads=[self.idxf.res, self.cres], writes=[self.idxf.res])
        S.op("dve", lambda e: e.tensor_copy(out=self.idxi[:, :], in_=self.idxf[:, :]), reads=[self.idxf.res], writes=[self.idxi.res])
        self.kti = 0
        self.wcopy_res = Res("wcopy")
        pb = self.psum()
        for z in range(2):
            for lh in range(2):
                slab = self.w1slab(z, lh, 0)
                for l in range(32):
                    la = lh * 32 + l
                    S.op("pe", lambda e, z=z, l=l, la=la, slab=slab: e.matmul(pb[:, z:z + 1], lhsT=slab[:, l * P:(l + 1) * P], rhs=self.peT[:, z, la:la + 1],
                                                                              start=(la == 0), stop=(la == 63)),
                         reads=[slab.res, self.cres], writes=[pb.res])
        S.op("act", lambda e: e.activation(out=self.pebias[:, :], in_=pb[:, 0:2], func=AF.Copy), reads=[pb.res], writes=[self.pebias.res])

    def nsa_consts(self):
        self.ident_bf = self._cload("ident_b", [P, P], lambda I, c: np.eye(P, dtype=np.float32), BF16)

        def w2k(I, c):
            a = np.zeros((P, 2, P), np.float32)
            a[:, 0, 0:64] = I["nsa_w_cmp2"][0][0]
            a[:, 1, 64:128] = I["nsa_w_cmp2"][0][0]
            return a
        self.W2K = self._cload("w2k", [P, 2, P], w2k, BF16)
        self.W2V = self._cload("w2v", [P, 64], lambda I, c: I["nsa_w_cmp2"][0][1], BF16)
        self.peT = self._cload("peT", [P, 2, 64], lambda I, c: np.concatenate([I["nsa_pe_cmp"][0].transpose(2, 0, 1), np.zeros((64, 2, 64), np.float32)], 0), BF16)

        NS = self.cfg.NSAMP
        self.iota_p = self._cload("iota_p", [P, 1], lambda I, c: np.arange(P, dtype=np.float32)[:, None])

        def sel(I, c):
            a = np.zeros((NS, NS, 2, P), np.float32)
            for b in range(NS):
                a[b, b, 0, 0:64] = 1.0
                a[b, b, 1, 64:128] = 1.0
            return a
        self.Sel = self._cload("sel", [NS, NS, 2, P], sel)

        def newmask(I, c):
            a = np.full((P, NS, 4), -30000.0, np.float32)
            for b in range(NS):
                a[b, b, :] = 0.0
            return a
        self.newmask = self._cload("newmask", [P, NS, 4], newmask, BF16)

        def onesv(I, c):
            a = np.zeros((P, 2, P), np.float32)
            a[:, 0, 0:64] = 1.0
            a[:, 1, 64:128] = 1.0
            return a
        self.OnesV = self._cload("onesv", [P, 2, P], onesv, BF16)

    def _cload(self, name, shape, fn, sdt=F32):
        d = self.din(name, shape, fn)
        b = Buf(self.S, "c_" + name, list(shape), sdt)
        b.res = self.cres
        idx = tuple(slice(None) for _ in shape)
        if sdt != F32:
            self.S.dma("pool", [(b.t[idx], d[idx])], writes=[self.cres_sw], anchor=self.cres_sw, serialize=False)
        else:
            self.S.dma("sp", [(b.t[idx], d[idx])], writes=[self.cres_hw], anchor=self.cres_hw, serialize=False)
        return b

    def consts_barrier(self):
        self.S.op("dve", lambda e: e.memset(self.cbar[:, 0:1], 0.0), reads=[self.cres_hw, self.cres_sw], writes=[self.cres, self.cbar.res])

    def w1slab(self, z, lh, g2):
        def fn(w, z=z, lh=lh, g2=g2):
            w1 = np.asarray(w["nsa_w_cmp1"][0][z])[lh * 32:(lh + 1) * 32]
            a = np.zeros((P, 32 * P), np.float32)
            a[g2 * 64:(g2 + 1) * 64] = w1.transpose(1, 0, 2).reshape(64, 32 * P)
            return a
        return self.ws.next(("w1", z, lh, g2), 32 * P, fn)

    def head_loc(self, h):
        g, r = h // 4, h % 4
        return 4 * (g // 2) + r, g % 2

    def nsa_project(self, N, sample, t0, slot):
        S = self.S
        cfg = self.cfg
        hT, hy, act = self.hT, self.hy, self.act
        qperm = np.concatenate([np.concatenate([np.arange(self.HE[j] * 64, self.HE[j] * 64 + 64), np.arange(self.HO[j] * 64, self.HO[j] * 64 + 64)]) for j in range(8)])

        S.op("pool", lambda e: e.memset(act[64:128, 0:8, 0:N], 0.0), writes=[self.actres])
        S.op("pool", lambda e: e.memset(act[0:64, 8:16, 0:N], 0.0), writes=[self.actres])

        def evq(m, ps):
            S.op("act", lambda e: e.activation(out=act[0:64, m, 0:N], in_=ps[0:64, 0:N], func=AF.Copy, scale=0.125), reads=[ps.res], writes=[self.actres])
            S.op("act", lambda e: e.activation(out=act[64:128, 8 + m, 0:N], in_=ps[64:128, 0:N], func=AF.Copy, scale=0.125), reads=[ps.res], writes=[self.actres])
        self.mm_fm(("nq",), lambda w: w["nsa_w_in"][0][:, qperm], KD, KD, lambda k: hT[:, k, 0:N], hy.res, N, evq)
        sub = int(os.environ.get("NSA_SUB", "9"))
        if sub < 2:
            return
        kcols = np.concatenate([np.arange(1024, 1280), np.arange(1280, 1536), np.arange(1536, 1792), np.arange(2048, 2304)])

        def evk(m, ps):
            if m < 4:
                S.op("act", lambda e: e.activation(out=self.XT[:, m, 0:N], in_=ps[:, 0:N], func=AF.Copy), reads=[ps.res], writes=[self.XT.res])
            elif m < 6:
                S.op("act", lambda e: e.activation(out=self.KTs[:, m - 4, t0:t0 + N], in_=ps[:, 0:N], func=AF.Copy), reads=[ps.res], writes=[self.KTs.res])
            else:
                S.op("act", lambda e: e.activation(out=self.KTw[:, slot, m - 6, 0:N], in_=ps[:, 0:N], func=AF.Copy), reads=[ps.res], writes=[self.KTw.res])
        if not sample:
            self.mm_fm(("nk",), lambda w: w["nsa_w_in"][0][:, kcols], KD, KD, lambda k: hT[:, k, 0:N], hy.res, N, evk)
        if sub < 3:
            return
        skv = [self.ws.next(("nkv", b), 8 * 512, lambda w, b=b: slab_tm(w["nsa_w_in"][0][:, 1024 + 512 * b:1536 + 512 * b])) for b in range(3)]
        sg = self.ws.next(("ngate",), 8 * 48, lambda w: np.ascontiguousarray(w["nsa_w_in"][0][:, 2560:2608].reshape(8, P, 48).transpose(1, 0, 2)).reshape(P, 8 * 48))
        nsub = 1 if sample else 4
        TS = N if sample else P
        tf = self.tmpf
        cnt = 0
        for s_ in range(nsub):
            for b in range(3):
                ps = self.psum()
                for k in range(KD):
                    S.op("pe", lambda e, k=k, ps=ps, b=b: e.matmul(ps[0:TS, 0:512], lhsT=hT[:, k, s_ * P:s_ * P + TS], rhs=skv[b][:, k * 512:(k + 1) * 512],
                                                                   start=(k == 0), stop=(k == KD - 1)), reads=[skv[b].res, hy.res], writes=[ps.res])
                j = cnt % 2
                cnt += 1
                S.op("act", lambda e, ps=ps, j=j: e.activation(out=tf[0:TS, j, :], in_=ps[0:TS, 0:512], func=AF.Copy), reads=[ps.res], writes=[self.tmpres[j]])
                if sample:
                    outs = [self.cmp_s[:, :], self.slc_s[:, :], self.win_s[:, 511, :]][b]
                    S.dma("sp", [(outs, tf[0:TS, j, :])], reads=[self.tmpres[j]])
                    if b > 0:
                        S.op("act", lambda e, ps=ps, b=b: e.activation(out=self.PGn[0:TS, b - 1, :], in_=ps[0:TS, 0:512], func=AF.Copy), reads=[ps.res], writes=[self.PGn.res])
                else:
                    r0 = t0 + s_ * P
                    if b == 0:
                        if sub >= 4:
                            S.dma("sp", [(self.cmp_p[r0:r0 + P, :], tf[:, j, :])], reads=[self.tmpres[j]])
                    elif b == 1:
                        if sub >= 4:
                            S.dma("sp", [(self.slc_p[r0:r0 + P, :], tf[:, j, :])], reads=[self.tmpres[j]])
                        kt = r0 // P
                        if sub >= 5:
                            S.op("act", lambda e, ps=ps, kt=kt: e.activation(out=self.Vst[:, :, 0:64], in_=ps[:, 256:512].rearrange("p (g d) -> p g d", g=4), func=AF.Copy),
                                 reads=[ps.res], writes=[self.Vst.res])
                            S.dma("sp", [(self.vscr[g_, :, kt, :], self.Vst[:, g_, :]) for g_ in range(4)], reads=[self.Vst.res], writes=[self.vscr_res])
                    else:
                        if t0 == cfg.SEQ - 512 and sub >= 4:
                            S.dma("sp", [(self.win_p[s_ * P:(s_ + 1) * P, :], tf[:, j, :])], reads=[self.tmpres[j]])
                        if sub >= 5:
                            S.op("act", lambda e, ps=ps: e.activation(out=self.Vw[:, slot, s_, :, 0:64], in_=ps[:, 256:512].rearrange("p (g d) -> p g d", g=4), func=AF.Copy),
                                 reads=[ps.res], writes=[self.Vw.res])
            if sub < 6:
                continue
            ps = self.psum()
            for k in range(KD):
                S.op("pe", lambda e, k=k, ps=ps: e.matmul(ps[0:TS, 0:48], lhsT=hT[:, k, s_ * P:s_ * P + TS], rhs=sg[:, k * 48:(k + 1) * 48],
                                                          start=(k == 0), stop=(k == KD - 1)), reads=[sg.res, hy.res], writes=[ps.res])
            S.op("act", lambda e, ps=ps: e.activation(out=self.gates[0:TS, s_, :], in_=ps[0:TS, 0:48], func=AF.Sigmoid), reads=[ps.res], writes=[self.gates.res])

    def nsa_compress(self, nblk, xt_fn, xt_res, k_out, v_out_hid, hid_res=None):
        S = self.S
        pc = self.psum()
        nb2 = 2 * nblk
        for z in range(2):
            for lh in range(2):
                for g2 in range(2):
                    slab = self.w1slab(z, lh, g2)
                    for l in range(32):
                        la = lh * 32 + l
                        S.op("pe", lambda e, z=z, l=l, la=la, g2=g2, slab=slab: e.matmul(
                            pc[:, (z * 2 + g2) * nb2:(z * 2 + g2 + 1) * nb2], lhsT=slab[:, l * P:(l + 1) * P], rhs=xt_fn(g2, z, la),
                            start=(la == 0 and z == 0 and g2 == 0), stop=(la == 63), skip_group_check=True), reads=[slab.res, xt_res], writes=[pc.res])
        for g2 in range(2):
            S.op("act", lambda e, g2=g2: e.activation(out=self.hidK[:, g2, :, 0:nblk], in_=pc[:, g2 * nb2:(g2 + 1) * nb2].rearrange("p (a b) -> p a b", a=2),
                                                      func=AF.Silu, bias=self.pebias[:, 0:1], scale=1.0), reads=[pc.res, self.pebias.res], writes=[self.hidK.res])
            S.op("act", lambda e, g2=g2: e.activation(out=v_out_hid(g2), in_=pc[:, (2 + g2) * nb2:(3 + g2) * nb2].rearrange("p (a b) -> p a b", a=2),
                                                      func=AF.Silu, bias=self.pebias[:, 1:2], scale=1.0), reads=[pc.res, self.pebias.res], writes=[hid_res or self.hidV.res])
        for gp in range(2):
            ps = self.psum()
            for g2 in range(2):
                S.op("pe", lambda e, g2=g2, gp=gp, ps=ps: e.matmul(ps[:, 0:nblk], lhsT=self.W2K[:, g2, :], rhs=self.hidK[:, g2, gp, 0:nblk], start=(g2 == 0), stop=(g2 == 1)),
                     reads=[self.cres, self.hidK.res], writes=[ps.res])
            k_out(gp, ps)

    def nsa_attend(self, N, groups):
        raise NotImplementedError

    def nsa(self, it, N, sample):
        if sample:
            return self.nsa_sample(N)
        S = self.S
        cfg = self.cfg
        t0 = it * 512
        slot = it % 2
        hT, hy, act = self.hT, self.hy, self.act
        qT = act
        S.dma("pool", [(self.mk[:, :, :, :], self.mk_d[it])], writes=[self.mk.res])
        S.dma("pool", [(self.negmT[:, 0, 4, :], self.cmpbt_d[it])], writes=[self.negmT.res])
        stage = int(os.environ.get("NSA_STAGE", "9"))
        if stage < 1:
            return self.nsa_sample(N)
        self.nsa_project(N, False, t0, slot)
        if stage < 2:
            return self.nsa_sample(N)
        S.op("pool", lambda e: e.memset(self.hidV[:, :, :], 0.0), writes=[self.hidV.res])
        XT = self.XT

        def xt_fn(half, z, l):
            return XT[:, z * 2:z * 2 + 2, l:512:64]

        def k_out(gp, ps):
            S.op("act", lambda e: e.activation(out=self.KcT[:, gp, it * 8:it * 8 + 8], in_=ps[:, 0:8], func=AF.Copy), reads=[ps.res], writes=[self.KcT.res])
        self.nsa_compress(8, xt_fn, XT.res, k_out, lambda g2: self.hidV[:, g2:4:2, it * 8:it * 8 + 8])
        ps = self.psum()
        for g in range(4):
            S.op("pe", lambda e, g=g: e.matmul(ps[:, g * 64:(g + 1) * 64], lhsT=self.hidV[:, g, :], rhs=self.W2V[:, :], start=True, stop=True),
                 reads=[self.hidV.res, self.cres], writes=[ps.res])
        S.op("dve", lambda e: e.tensor_tensor(out=self.Vc32[:, :], in0=self.Vc32[:, :], in1=ps[:, 0:256], op=ALU.add),
             reads=[ps.res, self.Vc32.res], writes=[self.Vc32.res])
        S.op("act", lambda e: e.activation(out=self.Vc[:, :, 0:64], in_=self.Vc32[:, :].rearrange("p (g d) -> p g d", g=4), func=AF.Copy),
             reads=[self.Vc32.res], writes=[self.Vc.res])
        if stage < 3:
            return self.nsa_sample(N)
        tk1, tk2, tks = self.tk1, self.tk2, self.tks
        mk = self.mk
        for s_ in range(4):
            for g in range(4):
                half, gp = g % 2, g // 2
                ps = self.psum()
                for r in range(4):
                    ch = 4 * gp + r
                    S.op("pe", lambda e, r=r, ch=ch, ps=ps: e.matmul(ps[:, r * P:(r + 1) * P], lhsT=qT[:, 8 * half + ch, s_ * P:(s_ + 1) * P],
                                                                     rhs=self.KcT[:, gp, :], start=True, stop=True),
                         reads=[self.actres, self.KcT.res], writes=[ps.res])
                S.op("dve", lambda e, ps=ps: e.tensor_tensor(out=tk1[:, :, :], in0=ps[:, 0:512].rearrange("p (r n) -> p r n", r=4),
                                                             in1=mk[:, 0, s_, :].unsqueeze(1).to_broadcast([P, 4, P]), op=ALU.add),
                     reads=[ps.res, mk.res], writes=[tk1.res])
                for r in range(4):
                    S.op("act", lambda e, r=r: e.activation(out=tk1[:, r, :], in_=tk1[:, r, :], func=AF.Exp, accum_out=tks[:, r:r + 1]),
                         reads=[tk1.res], writes=[tk1.res, tks.res])
                S.op("dve", lambda e: e.tensor_scalar(out=tks[:, 4:8], in0=tks[:, 0:4], scalar1=1e-30, scalar2=None, op0=ALU.max), reads=[tks.res], writes=[tks.res])
                S.op("dve", lambda e: e.reciprocal(out=tks[:, 4:8], in_=tks[:, 4:8]), reads=[tks.res], writes=[tks.res])
                S.op("dve", lambda e: e.tensor_scalar(out=tk2[:, 0, :], in0=tk1[:, 0, :], scalar1=tks[:, 4:5], scalar2=None, op0=ALU.mult),
                     reads=[tk1.res, tks.res], writes=[tk2.res])
                for r in range(1, 4):
                    S.op("dve", lambda e, r=r: e.scalar_tensor_tensor(out=tk2[:, 0, :], in0=tk1[:, r, :], scalar=tks[:, 4 + r:5 + r], in1=tk2[:, 0, :], op0=ALU.mult, op1=ALU.add),
                         reads=[tk1.res, tks.res, tk2.res], writes=[tk2.res])
                S.op("dve", lambda e: e.tensor_tensor(out=tk2[:, 0, :], in0=tk2[:, 0, :], in1=mk[:, 1, s_, :], op=ALU.mult), reads=[tk2.res, mk.res], writes=[tk2.res])
                S.op("dve", lambda e: e.tensor_tensor(out=tk2[:, 0, :], in0=tk2[:, 0, :], in1=mk[:, 2, s_, :], op=ALU.add), reads=[tk2.res, mk.res], writes=[tk2.res])
                S.op("dve", lambda e: e.max(out=tks[:, 8:16], in_=tk2[:, 0, :]), reads=[tk2.res], writes=[tks.res])
                S.op("dve", lambda e: e.match_replace(out=tk2[:, 1, :], in_to_replace=tks[:, 8:16], in_values=tk2[:, 0, :], imm_value=-2.0),
                     reads=[tk2.res, tks.res], writes=[tk2.res])
                S.op("dve", lambda e: e.max(out=tks[:, 16:24], in_=tk2[:, 1, :]), reads=[tk2.res], writes=[tks.res])
                S.op("dve", lambda e: e.tensor_scalar(out=tks[:, 24:25], in0=tks[:, 23:24], scalar1=-0.5, scalar2=None, op0=ALU.max), reads=[tks.res], writes=[tks.res])
                S.op("dve", lambda e: e.tensor_scalar(out=self.negm[:, :], in0=tk2[:, 0, :], scalar1=tks[:, 24:25], scalar2=1.0, op0=ALU.is_ge, op1=ALU.subtract),
                     reads=[tk2.res, tks.res], writes=[self.negm.res])
                pst = self.psum()
                pv = pst.t[:].bitcast(BF16)
                S.op("pe", lambda e, pv=pv, pst=pst: e.transpose(pv[:, 0:P], self.negm[:, :], self.ident_bf[:, :]), reads=[self.negm.res, self.cres], writes=[pst.res])
                S.op("act", lambda e, pv=pv, pst=pst, g=g: e.activation(out=self.negmT[0:64, 0, g, s_ * P:(s_ + 1) * P], in_=pv[0:64, 0:P], func=AF.Copy),
                     reads=[pst.res], writes=[self.negmT.res])
                S.op("act", lambda e, pv=pv, pst=pst, g=g: e.activation(out=self.negmT[64:128, 1, g, s_ * P:(s_ + 1) * P], in_=pv[64:128, 0:P], func=AF.Copy),
                     reads=[pst.res], writes=[self.negmT.res])
        if stage < 4:
            return self.nsa_sample(N)
        sE = self.ws.next(("E64",), 4096, lambda w: e64_host())
        sC = self.ws.next(("caus",), 4096, lambda w: caus_host())
        o_tm = self.xin
        pT = self.tmpb
        npt = 0
        for h in range(16):
            g = h // 4
            ch, half = self.head_loc(h)
            gp = g // 2
            hs = slice(0, P)
            qh = qT[:, 8 * half + ch, 0:512]
            for br in range(3):
                tiles = []
                if br == 0:
                    tiles.append((self.KcT[hs, gp, :], self.KcT.res, self.Vc[:, g, 0:65], self.Vc.res,
                                  [(self.ident_bf[:, :], self.negmT[:, 0, 4, :])]))
                elif br == 1:
                    if h % 4 == 0:
                        nkt = 4 * it + 4
                        S.dma("sp", [(self.Vs[:, 0:nkt, :], self.vscr[g, :, 0:nkt, :])], reads=[self.vscr_res], writes=[self.Vs.res])
                    for kt in range(4 * it + 4):
                        b64, v = kt // 32, kt % 32
                        ms = [(sE[:, v * P:(v + 1) * P], self.negmT[:, b64, g, :])]
                        if kt >= 4 * it:
                            j = kt - 4 * it
                            ms.append((self.ident_bf[:, :], sC[:, j * 512:(j + 1) * 512]))
                        tiles.append((self.KTs[hs, gp, kt * P:(kt + 1) * P], self.KTs.res, self.Vs[:, kt, 0:65], self.Vs.res, ms))
                else:
                    if it > 0:
                        for j in range(4):
                            tiles.append((self.KTw[hs, 1 - slot, gp, j * P:(j + 1) * P], self.KTw.res, self.Vw[:, 1 - slot, j, g, 0:65], self.Vw.res,
                                          [(self.ident_bf[:, :], sC[:, (4 + j) * 512:(5 + j) * 512])]))
                    for j in range(4):
                        tiles.append((self.KTw[hs, slot, gp, j * P:(j + 1) * P], self.KTw.res, self.Vw[:, slot, j, g, 0:65], self.Vw.res,
                                      [(self.ident_bf[:, :], sC[:, j * 512:(j + 1) * 512])]))
                po = self.pacc[self.paidx % 2]
                self.paidx += 1
                nt_ = len(tiles)
                for ti, (kt_ap, kres, v_ap, vres, ms) in enumerate(tiles):
                    ps = self.psum()
                    S.op("pe", lambda e, ps=ps, kt_ap=kt_ap: e.matmul(ps[:, 0:512], lhsT=kt_ap, rhs=qh, start=True, stop=False),
                         reads=[kres, self.actres], writes=[ps.res])
                    for mi, (ml, mr) in enumerate(ms):
                        S.op("pe", lambda e, ps=ps, ml=ml, mr=mr, mi=mi, nm=len(ms): e.matmul(ps[:, 0:512], lhsT=ml, rhs=mr, start=False, stop=(mi == nm - 1)),
                             reads=[self.cres, self.negmT.res, sE.res, sC.res], writes=[ps.res])
                    pj = npt % 2
                    npt += 1
                    S.op("act", lambda e, ps=ps, pj=pj: e.activation(out=pT[:, pj, :], in_=ps[:, 0:512], func=AF.Exp), reads=[ps.res], writes=[self.tmpbres[pj]])
                    for qs in range(4):
                        S.op("pe", lambda e, qs=qs, pj=pj, v_ap=v_ap, ti=ti: e.matmul(po[:, qs * 65:(qs + 1) * 65], lhsT=pT[:, pj, qs * P:(qs + 1) * P], rhs=v_ap,
                                                                                     start=(ti == 0 and qs == 0), stop=(ti == nt_ - 1), skip_group_check=True),
                             reads=[self.tmpbres[pj], vres], writes=[po.res])
                f4 = self.f4
                pov = po[:, 0:260].rearrange("p (q c) -> p q c", q=4)
                S.op("dve", lambda e, pov=pov: e.tensor_scalar(out=f4[:, 0:4], in0=pov[:, :, 64], scalar1=1e-30, scalar2=None, op0=ALU.max), reads=[po.res], writes=[f4.res])
                S.op("dve", lambda e: e.reciprocal(out=f4[:, 0:4], in_=f4[:, 0:4]), reads=[f4.res], writes=[f4.res])
                S.op("dve", lambda e, br=br, h=h: e.tensor_tensor(out=f4[:, 4:8], in0=f4[:, 0:4], in1=self.gates[:, :, br * 16 + h], op=ALU.mult),
                     reads=[f4.res, self.gates.res], writes=[f4.res])
                for qs in range(4):
                    if br == 0:
                        S.op("dve", lambda e, qs=qs, pov=pov, h=h: e.tensor_scalar(out=o_tm[:, qs, h * 64:(h + 1) * 64], in0=pov[:, qs, 0:64], scalar1=f4[:, 4 + qs:5 + qs],
                                                                                 scalar2=None, op0=ALU.mult), reads=[po.res, f4.res], writes=[hy.res])
                    else:
                        S.op("dve", lambda e, qs=qs, pov=pov, h=h: e.scalar_tensor_tensor(out=o_tm[:, qs, h * 64:(h + 1) * 64], in0=pov[:, qs, 0:64], scalar=f4[:, 4 + qs:5 + qs],
                                                                                        in1=o_tm[:, qs, h * 64:(h + 1) * 64], op0=ALU.mult, op1=ALU.add),
                             reads=[po.res, f4.res, hy.res], writes=[hy.res])
        for k in range(KD):
            ps = self.psum()
            for qs in range(4):
                S.op("pe", lambda e, k=k, qs=qs, ps=ps: e.transpose(ps[:, qs * P:(qs + 1) * P], o_tm[:, qs, k * P:(k + 1) * P], self.ident[:, :]),
                     reads=[hy.res, self.cres], writes=[ps.res])
            S.op("act", lambda e, k=k, ps=ps: e.activation(out=act[:, k, 0:512], in_=ps[:, 0:512], func=AF.Copy), reads=[ps.res], writes=[self.actres])
        self.mm_fm(("nwo",), lambda w: w["nsa_w_out"][0], KD, KD, lambda k: act[:, k, 0:N], self.actres, N, self.evac_y(N))

    def samp_page(self, b, v_src, v_res, kt_buf, masks, first, last, po, psm, pg_ap=None, pg_res=None):
        S = self.S
        act = self.act
        if kt_buf is None:
            kt_buf = self.KTp[self.kti % 2]
            for gp in range(2):
                ps = self.psum()
                S.op("pe", lambda e, gp=gp, ps=ps: e.transpose(ps[:, 0:P], pg_ap[:, gp * P:(gp + 1) * P], self.ident[:, :]), reads=[pg_res, self.cres], writes=[ps.res])
                S.op("act", lambda e, gp=gp, ps=ps: e.activation(out=kt_buf[:, gp, :], in_=ps[:, 0:P], func=AF.Copy), reads=[ps.res], writes=[kt_buf.res])
        Vp = self.Vp[0]
        pT = self.pTs[self.kti % 2]
        self.kti += 1
        vv = v_src.rearrange("p (g d) -> p g d", g=4)
        S.op("act", lambda e: e.activation(out=Vp[:, :, 0, 0:64], in_=vv, func=AF.Copy), reads=[v_res], writes=[Vp.res])
        S.op("act", lambda e: e.activation(out=Vp[:, :, 1, 64:128], in_=vv, func=AF.Copy), reads=[v_res], writes=[Vp.res])
        ps = self.psum()
        for g in range(4):
            half, gp = g % 2, g // 2
            q_ap = act[:, 8 * half + 4 * gp:8 * half + 4 * gp + 4, b]
            S.op("pe", lambda e, g=g, gp=gp, q_ap=q_ap: e.matmul(ps[:, g * 4:(g + 1) * 4], lhsT=kt_buf[:, gp, :], rhs=q_ap, start=True, stop=(len(masks) == 0)),
                 reads=[kt_buf.res, self.actres], writes=[ps.res])
            for mi, (ml, mr, mres) in enumerate(masks):
                S.op("pe", lambda e, g=g, ml=ml, mr=mr, mi=mi: e.matmul(ps[:, g * 4:(g + 1) * 4], lhsT=ml, rhs=mr(g), start=False, stop=(mi == len(masks) - 1)),
                     reads=[self.cres] + mres, writes=[ps.res])
        S.op("act", lambda e: e.activation(out=pT[:, :], in_=ps[:, 0:16], func=AF.Exp), reads=[ps.res], writes=[pT.res])
        for g in range(4):
            for var in range(2):
                rhs = pT[:, g * 4 + var:g * 4 + 4:2]
                S.op("pe", lambda e, g=g, var=var, rhs=rhs: e.matmul(po[:, 2 * g:2 * g + 2], lhsT=Vp[:, g, var, :], rhs=rhs,
                                                                     start=(first and g == 0 and var == 0), stop=last, skip_group_check=True),
                     reads=[Vp.res, pT.res], writes=[po.res])
                S.op("pe", lambda e, g=g, var=var, rhs=rhs: e.matmul(psm[:, 2 * g:2 * g + 2], lhsT=self.OnesV[:, var, :], rhs=rhs,
                                                                     start=(first and g == 0 and var == 0), stop=last, skip_group_check=True),
                     reads=[self.cres, pT.res], writes=[psm.res])
        return pT

    def nsa_sample(self, N):
        S = self.S
        cfg = self.cfg
        NS, NPG = cfg.NSAMP, self.NPG
        hT, hy, act = self.hT, self.hy, self.act
        stage = int(os.environ.get("NSA_SSTAGE", "9"))
        if stage < 1 or 1 not in cfg.layers or not hasattr(self, "cmp_s"):
            for k in range(KD):
                S.op("act", lambda e, k=k: e.activation(out=self.yT[:, k, 0:N], in_=self.xT[:, k, 0:N], func=AF.Copy),
                     reads=[self.xT.res], writes=[self.hy.res])
            return
        S.op("pool", lambda e: e.memset(self.PGn[:, :, :], 0.0), reads=[self.hy.res], writes=[self.pgn_res])
        self.nsa_project(N, True, 0, 0)
        tf = self.tmpf
        pgi = 0
        res_o, res_s = self.res_o, self.res_s

        def fin(b, br, po, psm):
            S.op("act", lambda e: e.activation(out=res_o[:, br, b, :], in_=po[:, 0:8], func=AF.Copy), reads=[po.res], writes=[res_o.res])
            S.op("act", lambda e: e.activation(out=res_s[:, br, b, :], in_=psm[:, 0:8], func=AF.Copy), reads=[psm.res], writes=[res_s.res])
        newm = lambda b: [(self.ident_bf[:, :], (lambda g, b=b: self.newmask[:, b, :]), [])]
        XTs = self.XTs
        for b in range(NS):
            for grp in range(NPG // 8):
                for pi in range(8):
                    page = grp * 8 + pi
                    j = pgi % 2
                    pgi += 1
                    col = b * NPG + page
                    S.dma_gather(tf[:, j, :], self.cache_c[:, :], self.idxi[:, col:col + 1], reads=[self.idxi.res], writes=[self.tmpres[j]])
                    for c in range(4):
                        ps = self.psum()
                        S.op("pe", lambda e, c=c, ps=ps, j=j: e.transpose(ps[:, 0:P], tf[:, j, c * P:(c + 1) * P], self.ident[:, :]),
                             reads=[self.tmpres[j], self.cres], writes=[ps.res])
                        S.op("act", lambda e, c=c, ps=ps, pi=pi: e.activation(out=XTs[:, c, pi * P:(pi + 1) * P], in_=ps[:, 0:P], func=AF.Copy),
                             reads=[ps.res], writes=[XTs.res])

                def k_out(gp, ps, grp=grp):
                    S.op("act", lambda e: e.activation(out=self.KcS[:, gp, grp * 16:(grp + 1) * 16], in_=ps[:, 0:16], func=AF.Copy), reads=[ps.res], writes=[self.KcS.res])
                self.nsa_compress(16, lambda half, z, l: XTs[:, z * 2:z * 2 + 2, l:1024:64], XTs.res, k_out,
                                  lambda g2, grp=grp: self.hidVs[:, g2:4:2, grp * 16:(grp + 1) * 16], hid_res=self.hidVs.res)
            psv = self.psum()
            for g in range(4):
                S.op("pe", lambda e, g=g: e.matmul(psv[:, g * 64:(g + 1) * 64], lhsT=self.hidVs[:, g, :], rhs=self.W2V[:, :], start=True, stop=True),
                     reads=[self.hidVs.res, self.cres], writes=[psv.res])
            po, psm = self.pacc[0], self.pacc[1]
            pT = self.samp_page(b, psv[:, 0:256], psv.res, self.KcS, [], True, True, po, psm)
            fin(b, 0, po, psm)
            psr = self.psum()
            S.op("pe", lambda e: e.matmul(psr[:, 0:16], lhsT=self.ones_bf[:, 0:P], rhs=pT[:, :], start=True, stop=True), reads=[self.cres, pT.res], writes=[psr.res])
            sw = self.sw
            S.op("dve", lambda e: e.tensor_scalar(out=sw[:, 0:16], in0=psr[:, 0:16], scalar1=1e-30, scalar2=None, op0=ALU.max), reads=[psr.res], writes=[sw.res])
            S.op("dve", lambda e: e.reciprocal(out=sw[:, 0:16], in_=sw[:, 0:16]), reads=[sw.res], writes=[sw.res])
            S.op("dve", lambda e: e.tensor_tensor(out=sw[:, 16:32], in0=sw[:, 0:16], in1=pT[:, :], op=ALU.mult), reads=[sw.res, pT.res], writes=[sw.res])
            S.op("dve", lambda e, b=b: e.tensor_reduce(out=self.impT[:, b * 4:(b + 1) * 4], in_=sw[:, 16:32].rearrange("p (g r) -> p g r", g=4),
                                                       axis=mybir.AxisListType.X, op=ALU.add), reads=[sw.res], writes=[self.impT.res])
        R16 = 4 * NS
        pst = self.psum()
        S.op("pe", lambda e: e.transpose(pst[0:R16, 0:P], self.impT[:, :], self.ident[:, :]), reads=[self.impT.res, self.cres], writes=[pst.res])
        sc, sk = self.sc16, self.sk16
        S.op("act", lambda e: e.activation(out=sc[:, 0, :], in_=pst[0:R16, 0:P], func=AF.Copy), reads=[pst.res], writes=[sc.res])
        S.op("dve", lambda e: e.memset(sc[:, 0, 0:1], 1.0e4), reads=[sc.res], writes=[sc.res])
        S.op("dve", lambda e: e.memset(sc[:, 0, P - 1:P], 1.0e4), reads=[sc.res], writes=[sc.res])
        S.op("dve", lambda e: e.max(out=sk[:, 0:8], in_=sc[:, 0, :]), reads=[sc.res], writes=[sk.res])
        S.op("dve", lambda e: e.match_replace(out=sc[:, 1, :], in_to_replace=sk[:, 0:8], in_values=sc[:, 0, :], imm_value=-2.0), reads=[sc.res, sk.res], writes=[sc.res])
        S.op("dve", lambda e: e.max(out=sk[:, 8:16], in_=sc[:, 1, :]), reads=[sc.res], writes=[sk.res])
        S.op("dve", lambda e: e.tensor_scalar(out=self.ng16[:, :], in0=sc[:, 0, :], scalar1=sk[:, 14:15], scalar2=1.0, op0=ALU.is_ge, op1=ALU.subtract),
             reads=[sc.res, sk.res], writes=[self.ng16.res])
        pst2 = self.psum()
        pv = pst2.t[:].bitcast(BF16)
        S.op("pe", lambda e: e.transpose(pv[:, 0:R16], self.ng16[:, :], self.ident_bf[0:R16, 0:R16]), reads=[self.ng16.res, self.cres], writes=[pst2.res])
        S.op("act", lambda e: e.activation(out=self.negS[0:64, 0, :], in_=pv[0:64, 0:R16], func=AF.Copy), reads=[pst2.res], writes=[self.negS.res])
        S.op("act", lambda e: e.activation(out=self.negS[64:128, 1, :], in_=pv[64:128, 0:R16], func=AF.Copy), reads=[pst2.res], writes=[self.negS.res])
        sE = self.ws.next(("E64",), 4096, lambda w: e64_host())
        for b in range(NS):
            po, psm = self.pacc[0], self.pacc[1]
            for page in range(NPG):
                j = pgi % 2
                pgi += 1
                col = b * NPG + page
                S.dma_gather(tf[:, j, :], self.cache_s[:, :], self.idxi[:, col:col + 1], reads=[self.idxi.res], writes=[self.tmpres[j]])
                b64, v = page // 32, page % 32
                ms = [(sE[:, v * P:(v + 1) * P], (lambda g, b=b, b64=b64: self.negS[:, b64, b * 4 + g:b * 4 + g + 1].to_broadcast([P, 4])), [sE.res, self.negS.res])]
                self.samp_page(b, tf[:, j, 256:512], self.tmpres[j], None, ms, page == 0, False, po, psm, pg_ap=tf[:, j, :], pg_res=self.tmpres[j])
            self.samp_page(b, self.PGn[:, 0, 256:512], self.PGn.res, None, newm(b), False, True, po, psm, pg_ap=self.PGn[:, 0, :], pg_res=self.PGn.res)
            fin(b, 1, po, psm)
        for b in range(NS):
            po, psm = self.pacc[0], self.pacc[1]
            for page in range(4):
                j = pgi % 2
                pgi += 1
                S.dma("pool", [(tf[:, j, :], self.cache_w[b, page * P:(page + 1) * P, :])], writes=[self.tmpres[j]])
                self.samp_page(b, tf[:, j, 256:512], self.tmpres[j], None, [], page == 0, False, po, psm, pg_ap=tf[:, j, :], pg_res=self.tmpres[j])
            self.samp_page(b, self.PGn[:, 1, 256:512], self.PGn.res, None, newm(b), False, True, po, psm, pg_ap=self.PGn[:, 1, :], pg_res=self.PGn.res)
            fin(b, 2, po, psm)
            S.dma("sp", [(self.win_s[b, 0:511, :], self.cache_w[b, 1:512, :])], reads=[self.wcopy_res])
        S.op("dve", lambda e: e.tensor_scalar(out=res_s[:, :, :, :], in0=res_s[:, :, :, :], scalar1=1e-30, scalar2=None, op0=ALU.max), reads=[res_s.res], writes=[res_s.res])
        S.op("dve", lambda e: e.reciprocal(out=res_s[:, :, :, :], in_=res_s[:, :, :, :]), reads=[res_s.res], writes=[res_s.res])
        S.op("dve", lambda e: e.tensor_tensor(out=res_o[:, :, :, :], in0=res_o[:, :, :, :], in1=res_s[:, :, :, :], op=ALU.mult), reads=[res_s.res, res_o.res], writes=[res_o.res])
        gv = self.gates[0:NS, 0, :].rearrange("p (a c h) -> p a c h", a=3, c=8)
        sw = self.sw
        for b in range(NS):
            pg_ = self.psum()
            for hf in range(2):
                S.op("pe", lambda e, b=b, hf=hf: e.matmul(pg_[:, 0:24], lhsT=self.Sel[0:NS, b, hf, :], rhs=gv[:, :, :, hf], start=(hf == 0), stop=(hf == 1)),
                     reads=[self.cres, self.gates.res], writes=[pg_.res])
            S.op("dve", lambda e, b=b: e.tensor_tensor(out=sw[:, 32:56].rearrange("p (a c) -> p a c", a=3), in0=res_o[:, :, b, :],
                                                       in1=pg_[:, 0:24].rearrange("p (a c) -> p a c", a=3), op=ALU.mult),
                 reads=[res_o.res, pg_.res], writes=[sw.res])
            S.op("dve", lambda e: e.tensor_tensor(out=sw[:, 32:40], in0=sw[:, 32:40], in1=sw[:, 40:48], op=ALU.add), reads=[sw.res], writes=[sw.res])
            S.op("dve", lambda e: e.tensor_tensor(out=sw[:, 32:40], in0=sw[:, 32:40], in1=sw[:, 48:56], op=ALU.add), reads=[sw.res], writes=[sw.res])
            S.op("act", lambda e, b=b: e.activation(out=act[:, 0:8, b], in_=sw[:, 32:40], func=AF.Copy), reads=[sw.res], writes=[self.actres])
        self.mm_fm(("nwo",), lambda w: w["nsa_w_out"][0], KD, KD, lambda k: act[:, k, 0:N], self.actres, N, self.evac_y(N))

    def s5_setup(self):
        S = self.S
        cfg = self.cfg
        NS = cfg.NSAMP
        TC = 16
        self.TC = TC

        def st_layout(a):
            return np.asarray(a).reshape(32, 2, 64).transpose(1, 2, 0).reshape(P, 32)
        d_lr = self.din("s5_lr", [P, 32], lambda I, c: st_layout(I["ssm_lambda_re"][0]))
        d_li = self.din("s5_li", [P, 32], lambda I, c: st_layout(I["ssm_lambda_im"][0]))
        d_ls = self.din("s5_ls", [P, 32], lambda I, c: st_layout(np.broadcast_to(I["ssm_log_step"][0][:, None], (64, 64))))
        W = self.sb("s5w", [P, 24, 32], F32)
        self.s5res = Res("s5const")
        W.res = self.s5res
        S.dma("sp", [(W[:, 0, :], d_lr[:, :]), (W[:, 1, :], d_li[:, :]), (W[:, 2, :], d_ls[:, :])], writes=[W.res])
        LR, LI, DT, A_, TH, X, X2, SN, CS, T1, T2, MAG, ABR, ABI, DEN, FR, FI = range(17)
        LR, LI, LS = 0, 1, 2
        DT, A_, TH, X, X2, SN, CS, T1, T2, MAG, ABR, ABI, DEN, FR, FI = range(3, 18)

        def dve(fn):
            S.op("dve", fn, reads=[W.res], writes=[W.res])

        def tt(o, a, b, op):
            dve(lambda e: e.tensor_tensor(out=W[:, o, :], in0=W[:, a, :], in1=W[:, b, :], op=op))

        def ts(o, a, s1, op0, s2=None, op1=None):
            if op1 is None:
                dve(lambda e: e.tensor_scalar(out=W[:, o, :], in0=W[:, a, :], scalar1=s1, scalar2=None, op0=op0))
            else:
                dve(lambda e: e.tensor_scalar(out=W[:, o, :], in0=W[:, a, :], scalar1=s1, scalar2=s2, op0=op0, op1=op1))
        S.op("act", lambda e: e.activation(out=W[:, DT, :], in_=W[:, LS, :], func=AF.Exp), reads=[W.res], writes=[W.res])
        tt(A_, LR, DT, ALU.mult)
        tt(TH, LI, DT, ALU.mult)
        ts(MAG, A_, 1.0 / 5, ALU.mult, 1.0, ALU.add)
        for dd in (4.0, 3.0, 2.0, 1.0):
            tt(MAG, MAG, A_, ALU.mult)
            ts(MAG, MAG, 1.0 / dd, ALU.mult, 1.0, ALU.add)
        ts(X, TH, 1.0 / 16, ALU.mult)
        tt(X2, X, X, ALU.mult)
        ts(SN, X2, -1.0 / 156, ALU.mult, 1.0, ALU.add)
        for dd in (110.0, 72.0, 42.0, 20.0, 6.0):
            tt(SN, SN, X2, ALU.mult)
            ts(SN, SN, -1.0 / dd, ALU.mult, 1.0, ALU.add)
        tt(SN, SN, X, ALU.mult)
        ts(CS, X2, -1.0 / 182, ALU.mult, 1.0, ALU.add)
        for dd in (132.0, 90.0, 56.0, 30.0, 12.0, 2.0):
            tt(CS, CS, X2, ALU.mult)
            ts(CS, CS, -1.0 / dd, ALU.mult, 1.0, ALU.add)

        def double(c, s_):
            tt(T1, s_, c, ALU.mult)
            tt(T2, s_, s_, ALU.mult)
            ts(s_, T1, 2.0, ALU.mult)
            ts(c, T2, -2.0, ALU.mult, 1.0, ALU.add)
        for _ in range(4):
            double(CS, SN)
        tt(ABR, MAG, CS, ALU.mult)
        tt(ABI, MAG, SN, ALU.mult)
        tt(T1, LR, LR, ALU.mult)
        tt(T2, LI, LI, ALU.mult)
        tt(DEN, T1, T2, ALU.add)
        dve(lambda e: e.reciprocal(out=W[:, DEN, :], in_=W[:, DEN, :]))
        ts(T1, ABR, -1.0, ALU.add)
        tt(FR, T1, LR, ALU.mult)
        tt(T2, ABI, LI, ALU.mult)
        tt(FR, FR, T2, ALU.add)
        tt(FR, FR, DEN, ALU.mult)
        tt(FI, ABI, LR, ALU.mult)
        tt(T2, T1, LI, ALU.mult)
        tt(FI, FI, T2, ALU.subtract)
        tt(FI, FI, DEN, ALU.mult)
        self.s5W = W
        self.s5i = dict(MAG=MAG, CS=CS, SN=SN, ABR=ABR, ABI=ABI, FR=FR, FI=FI)
        self.cosT = self.sb("cosT", [P, 32, TC], F32)
        self.sinT = self.sb("sinT", [P, 32, TC], F32)
        self.rhoT = self.sb("rhoT", [P, 32, TC], F32)
        for bb in (self.cosT, self.sinT, self.rhoT):
            bb.res = self.s5res
        cT, sT, rT = self.cosT, self.sinT, self.rhoT
        dve(lambda e: e.memset(cT[:, :, 0:1], 1.0))
        dve(lambda e: e.memset(sT[:, :, 0:1], 0.0))
        dve(lambda e: e.tensor_copy(out=cT[:, :, 1], in_=W[:, CS, :]))
        dve(lambda e: e.tensor_copy(out=sT[:, :, 1], in_=W[:, SN, :]))
        for t in range(TC):
            dve(lambda e, t=t: e.tensor_copy(out=rT[:, :, t], in_=W[:, MAG, :]))
        WC, WS_, T3, T4 = 18, 19, 20, 21
        dve(lambda e: e.tensor_copy(out=W[:, WC, :], in_=W[:, CS, :]))
        dve(lambda e: e.tensor_copy(out=W[:, WS_, :], in_=W[:, SN, :]))
        tmpc = Buf.__new__(Buf)
        tmpc.t = self.act.t[:, 8:10, :].rearrange("p k t -> p (k t)").bitcast(F32).rearrange("p (c t) -> p c t", t=TC)
        tmpc.res = self.s5res
        n = 2
        while n < TC:
            double(WC, WS_)
            wc = W[:, WC, :].unsqueeze(2).to_broadcast([P, 32, n])
            wsn = W[:, WS_, :].unsqueeze(2).to_broadcast([P, 32, n])
            dve(lambda e, n=n, wc=wc: e.tensor_tensor(out=cT[:, :, n:2 * n], in0=cT[:, :, 0:n], in1=wc, op=ALU.mult))
            dve(lambda e, n=n, wsn=wsn: e.tensor_tensor(out=tmpc[:, :, 0:n], in0=sT[:, :, 0:n], in1=wsn, op=ALU.mult))
            dve(lambda e, n=n, wsn=wsn: e.tensor_tensor(out=sT[:, :, n:2 * n], in0=cT[:, :, 0:n], in1=wsn, op=ALU.mult))
            dve(lambda e, n=n: e.tensor_tensor(out=cT[:, :, n:2 * n], in0=cT[:, :, n:2 * n], in1=tmpc[:, :, 0:n], op=ALU.subtract))
            dve(lambda e, n=n, wc=wc: e.tensor_tensor(out=tmpc[:, :, 0:n], in0=sT[:, :, 0:n], in1=wc, op=ALU.mult))
            dve(lambda e, n=n: e.tensor_tensor(out=sT[:, :, n:2 * n], in0=sT[:, :, n:2 * n], in1=tmpc[:, :, 0:n], op=ALU.add))
            n *= 2
        self.s5tmp = tmpc
        self.s5x = self.sb("s5x", [P, 32, 2], F32)
        S.op("dve", lambda e: e.memset(self.s5x[:, :, :], 0.0), writes=[self.s5x.res])
        self.s5xs = self.sb("s5xs", [P, 32, NS, 2], F32)
        d_st = self.din("s5_st", [P, 32, NS, 2], lambda I, c: I["state_ssm"][0][c * NS:(c + 1) * NS].reshape(NS, 32, 2, 64, 2).transpose(2, 3, 1, 0, 4).reshape(P, 32, NS, 2))
        S.dma("sp", [(self.s5xs[:, :, :, :], d_st[:, :, :, :])], writes=[self.s5xs.res])
        self.s5_dv = None
        self.ssm_p = self.dout("ssm_p", [P, 32, 2])
        self.ssm_s = self.dout("ssm_s", [P, 32, NS, 2])
        self.ssm_anchor = Res("ssm_out")
        S.out_anchors.append(self.ssm_anchor)
        self.s5work = Buf.__new__(Buf)
        self.s5work.t = self.act.t[:, 8:20, :].rearrange("p k t -> p (k t)").bitcast(F32).rearrange("p (r c t) -> p r c t", r=6, c=32)
        self.s5work.res = self.actres
        self.s5xb = Buf.__new__(Buf)
        self.s5xb.t = self.act.t[:, 20:22, :].rearrange("p k t -> p (k t)").rearrange("p (z c t) -> p z c t", z=2, c=32)
        self.s5xb.res = self.actres
        self.s5zi = self.sb("s5zi", [P, 32, 2], F32)

    def s5(self, it, N, sample):
        S = self.S
        cfg = self.cfg
        NS = cfg.NSAMP
        TC = self.TC
        hT, hy, act = self.hT, self.hy, self.act
        W = self.s5W
        ix = self.s5i
        wk = self.s5work

        def padB(w, key):
            b = np.asarray(w[key][0])
            out = np.zeros((P, 32, P), np.float32)
            for c in range(32):
                for g2 in range(2):
                    g = 2 * c + g2
                    r0 = (c % 4) * 32 + g2 * 16
                    out[r0:r0 + 16, c, g2 * 64:(g2 + 1) * 64] = b[g].T
            return out.reshape(P, 32 * P)

        def padC(w, key):
            cm = np.asarray(w[key][0])
            out = np.zeros((P, 32, P), np.float32)
            for c in range(32):
                for g2 in range(2):
                    g = 2 * c + g2
                    f0 = (c % 4) * 32 + g2 * 16
                    out[g2 * 64:(g2 + 1) * 64, c, f0:f0 + 16] = cm[g].T
            return out.reshape(P, 32 * P)
        sBr = self.ws.next(("s5", "br"), 4096, lambda w: padB(w, "ssm_b_re"))
        sBi = self.ws.next(("s5", "bi"), 4096, lambda w: padB(w, "ssm_b_im"))
        sCr = self.ws.next(("s5", "cr"), 4096, lambda w: padC(w, "ssm_c_re"))
        sCi = self.ws.next(("s5", "ci"), 4096, lambda w: padC(w, "ssm_c_im"))
        nch = 1 if sample else N // TC
        T = N if sample else TC
        gT = act
        fr = W[:, ix["FR"], :].unsqueeze(2).to_broadcast([P, 32, T])
        fi = W[:, ix["FI"], :].unsqueeze(2).to_broadcast([P, 32, T])
        for ch in range(nch):
            t0 = ch * TC
            pb = [[self.psum(), self.psum()], [self.psum(), self.psum()]]
            for z, slab in ((0, sBr), (1, sBi)):
                for c in range(32):
                    ps = pb[z][c // 16]
                    S.op("pe", lambda e, c=c, ps=ps, slab=slab: e.matmul(ps[:, (c % 16) * T:(c % 16 + 1) * T], lhsT=slab[:, c * P:(c + 1) * P],
                                                                         rhs=hT[:, c // 4, t0:t0 + T], start=True, stop=True),
                         reads=[slab.res, hy.res], writes=[ps.res])
            def dv(fn):
                S.op("dve", fn, reads=[wk.res, self.s5res], writes=[wk.res])

            def mul(o, a, b):
                dv(lambda e: e.tensor_tensor(out=o, in0=a, in1=b, op=ALU.mult))
            R = lambda j: wk[:, j, :, 0:T]
            for hb in range(2):
                Rh = lambda j, hb=hb: wk[:, j, hb * 16:(hb + 1) * 16, 0:T]
                PSv = lambda z, hb=hb: pb[z][hb][:, 0:16 * T].rearrange("p (c t) -> p c t", t=T)
                frh = W[:, ix["FR"], hb * 16:(hb + 1) * 16].unsqueeze(2).to_broadcast([P, 16, T])
                fih = W[:, ix["FI"], hb * 16:(hb + 1) * 16].unsqueeze(2).to_broadcast([P, 16, T])

                def dp(fn, hb=hb):
                    S.op("dve", fn, reads=[wk.res, self.s5res, pb[0][hb].res, pb[1][hb].res], writes=[wk.res])
                dp(lambda e, Rh=Rh, PSv=PSv, frh=frh: e.tensor_tensor(out=Rh(4), in0=PSv(0), in1=frh, op=ALU.mult))
                dp(lambda e, Rh=Rh, PSv=PSv, fih=fih: e.tensor_tensor(out=Rh(5), in0=PSv(1), in1=fih, op=ALU.mult))
                dp(lambda e, Rh=Rh: e.tensor_tensor(out=Rh(0), in0=Rh(4), in1=Rh(5), op=ALU.subtract))
                dp(lambda e, Rh=Rh, PSv=PSv, frh=frh: e.tensor_tensor(out=Rh(4), in0=PSv(1), in1=frh, op=ALU.mult))
                dp(lambda e, Rh=Rh, PSv=PSv, fih=fih: e.tensor_tensor(out=Rh(5), in0=PSv(0), in1=fih, op=ALU.mult))
                dp(lambda e, Rh=Rh: e.tensor_tensor(out=Rh(1), in0=Rh(4), in1=Rh(5), op=ALU.add))
            if sample:
                xs_ = self.s5xs
                abr = W[:, ix["ABR"], :].unsqueeze(2).to_broadcast([P, 32, NS])
                abi = W[:, ix["ABI"], :].unsqueeze(2).to_broadcast([P, 32, NS])

                def dx(fn):
                    S.op("dve", fn, reads=[wk.res, self.s5res, xs_.res], writes=[wk.res])
                dx(lambda e: e.tensor_tensor(out=R(4), in0=xs_[:, :, :, 0], in1=abr, op=ALU.mult))
                dx(lambda e: e.tensor_tensor(out=R(5), in0=xs_[:, :, :, 1], in1=abi, op=ALU.mult))
                dv(lambda e: e.tensor_tensor(out=R(4), in0=R(4), in1=R(5), op=ALU.subtract))
                dv(lambda e: e.tensor_tensor(out=R(2), in0=R(4), in1=R(0), op=ALU.add))
                dx(lambda e: e.tensor_tensor(out=R(4), in0=xs_[:, :, :, 1], in1=abr, op=ALU.mult))
                dx(lambda e: e.tensor_tensor(out=R(5), in0=xs_[:, :, :, 0], in1=abi, op=ALU.mult))
                dv(lambda e: e.tensor_tensor(out=R(4), in0=R(4), in1=R(5), op=ALU.add))
                dv(lambda e: e.tensor_tensor(out=R(3), in0=R(4), in1=R(1), op=ALU.add))
                xo = self.s5xs
                S.op("dve", lambda e: e.tensor_copy(out=xo[:, :, :, 0], in_=R(2)), reads=[wk.res], writes=[xo.res])
                S.op("dve", lambda e: e.tensor_copy(out=xo[:, :, :, 1], in_=R(3)), reads=[wk.res], writes=[xo.res])
                S.dma("sp", [(self.ssm_s[:, :, :, :], xo[:, :, :, :])], reads=[xo.res])
                xb = self.s5xb
                S.op("dve", lambda e: e.tensor_copy(out=xb[:, 0, :, 0:T], in_=R(2)), reads=[wk.res], writes=[xb.res])
                S.op("dve", lambda e: e.tensor_scalar(out=xb[:, 1, :, 0:T], in0=R(3), scalar1=-1.0, scalar2=None, op0=ALU.mult), reads=[wk.res], writes=[xb.res])
            else:
                cT, sT, rT = self.cosT, self.sinT, self.rhoT
                mul(R(4), R(0), cT[:, :, :])
                mul(R(5), R(1), sT[:, :, :])
                dv(lambda e: e.tensor_tensor(out=R(2), in0=R(4), in1=R(5), op=ALU.add))
                mul(R(4), R(1), cT[:, :, :])
                mul(R(5), R(0), sT[:, :, :])
                dv(lambda e: e.tensor_tensor(out=R(3), in0=R(4), in1=R(5), op=ALU.subtract))
                xp_, zi_ = self.s5x, self.s5zi
                c1, s1 = W[:, ix["CS"], :], W[:, ix["SN"], :]

                def dz(fn):
                    S.op("dve", fn, reads=[xp_.res, self.s5res, zi_.res, wk.res], writes=[zi_.res, wk.res])
                dz(lambda e: e.tensor_tensor(out=wk[:, 4, :, 0], in0=xp_[:, :, 0], in1=c1, op=ALU.mult))
                dz(lambda e: e.tensor_tensor(out=wk[:, 4, :, 1], in0=xp_[:, :, 1], in1=s1, op=ALU.mult))
                dz(lambda e: e.tensor_tensor(out=zi_[:, :, 0], in0=wk[:, 4, :, 0], in1=wk[:, 4, :, 1], op=ALU.subtract))
                dz(lambda e: e.tensor_tensor(out=wk[:, 4, :, 2], in0=xp_[:, :, 1], in1=c1, op=ALU.mult))
                dz(lambda e: e.tensor_tensor(out=wk[:, 4, :, 3], in0=xp_[:, :, 0], in1=s1, op=ALU.mult))
                dz(lambda e: e.tensor_tensor(out=zi_[:, :, 1], in0=wk[:, 4, :, 2], in1=wk[:, 4, :, 3], op=ALU.add))
                zres = [Res("zr"), Res("zi")]
                for z in range(2):
                    for c in range(32):
                        S.op("dve", lambda e, z=z, c=c: e.tensor_tensor_scan(out=wk[:, 0 + z, c, :], data0=rT[:, c, :], data1=wk[:, 2 + z, c, :],
                                                                             initial=zi_[:, c, z:z + 1], op0=ALU.mult, op1=ALU.add),
                             reads=[wk.res, self.s5res, zi_.res], writes=[zres[z]])
                S.op("dve", lambda e: e.tensor_copy(out=wk[:, 5, 0, 0:1], in_=wk[:, 5, 0, 0:1]), reads=[zres[0], zres[1]], writes=[wk.res])
                xb = self.s5xb
                mul(R(4), R(0), cT[:, :, :])
                mul(R(5), R(1), sT[:, :, :])
                dv(lambda e: e.tensor_tensor(out=R(2), in0=R(4), in1=R(5), op=ALU.subtract))
                mul(R(4), R(0), sT[:, :, :])
                mul(R(5), R(1), cT[:, :, :])
                dv(lambda e: e.tensor_tensor(out=R(3), in0=R(4), in1=R(5), op=ALU.add))
                S.op("dve", lambda e: e.tensor_copy(out=xp_[:, :, 0], in_=wk[:, 2, :, T - 1]), reads=[wk.res], writes=[xp_.res])
                S.op("dve", lambda e: e.tensor_copy(out=xp_[:, :, 1], in_=wk[:, 3, :, T - 1]), reads=[wk.res], writes=[xp_.res])
                S.op("act", lambda e: e.activation(out=xb[:, 0, :, 0:T], in_=R(2), func=AF.Copy), reads=[wk.res], writes=[xb.res])
                S.op("act", lambda e: e.activation(out=xb[:, 1, :, 0:T], in_=R(3), func=AF.Copy, scale=-1.0), reads=[wk.res], writes=[xb.res])
            xb = self.s5xb
            py = self.psum()
            for k in range(KD):
                n_ = 0
                for j in range(4):
                    c = 4 * k + j
                    for z, slab in ((0, sCr), (1, sCi)):
                        S.op("pe", lambda e, k=k, c=c, z=z, slab=slab, n_=n_: e.matmul(py[:, k * T:(k + 1) * T], lhsT=slab[:, c * P:(c + 1) * P],
                                                                                       rhs=xb[:, z, c, 0:T], start=(n_ == 0), stop=(n_ == 7)),
                             reads=[slab.res, xb.res], writes=[py.res])
                        n_ += 1
            tf = self.tmpf
            S.op("dve", lambda e: e.tensor_tensor(out=tf[:, 0, 0:KD * T].rearrange("p (k t) -> p k t", t=T), in0=hT[:, :, t0:t0 + T],
                                                  in1=self.s5d[:, :].unsqueeze(2).to_broadcast([P, KD, T]), op=ALU.mult),
                 reads=[hy.res, self.cres], writes=[self.tmpres[0]])
            S.op("dve", lambda e: e.tensor_tensor(out=tf[:, 0, 0:KD * T], in0=tf[:, 0, 0:KD * T], in1=py[:, 0:KD * T], op=ALU.add),
                 reads=[self.tmpres[0], py.res], writes=[self.tmpres[0]])
            S.op("act", lambda e: e.activation(out=gT[:, 0:KD, t0:t0 + T], in_=tf[:, 0, 0:KD * T].rearrange("p (k t) -> p k t", t=T), func=AF.Gelu),
                 reads=[self.tmpres[0]], writes=[self.actres])
        if (not sample) and it == cfg.NT - 1:
            S.dma("sp", [(self.ssm_p[:, :, :], self.s5x[:, :, :])], reads=[self.s5x.res])
        yT = self.yT
        sg = self.tmpf
        for mg in range(0, KD, 4):
            s1_ = self.ws.next(("glu1", mg), 8 * 4 * P, lambda w, mg=mg: slab_fm(w["ssm_w_glu1"][0], 0, 8, mg, 4))
            s2_ = self.ws.next(("glu2", mg), 8 * 4 * P, lambda w, mg=mg: slab_fm(w["ssm_w_glu2"][0], 0, 8, mg, 4))
            for m in range(4):
                p1, p2 = self.psum(), self.psum()
                for (slab, ps) in ((s1_, p1), (s2_, p2)):
                    for k in range(KD):
                        S.op("pe", lambda e, k=k, slab=slab, ps=ps, m=m: e.matmul(ps[:, 0:N], lhsT=slab[:, (k * 4 + m) * P:(k * 4 + m + 1) * P], rhs=gT[:, k, 0:N],
                                                                                 start=(k == 0), stop=(k == KD - 1)), reads=[slab.res, self.actres], writes=[ps.res])
                mm = mg + m
                j = mm % 2
                S.op("act", lambda e, p2=p2, mm=mm, j=j: e.activation(out=sg[:, j, 0:N], in_=p2[:, 0:N], func=AF.Sigmoid, bias=self.s5b2[:, mm:mm + 1], scale=1.0),
                     reads=[p2.res, self.cres], writes=[self.tmpres[j]])
                S.op("dve", lambda e, p1=p1, mm=mm, j=j: e.scalar_tensor_tensor(out=yT[:, mm, 0:N], in0=p1[:, 0:N], scalar=self.s5b1[:, mm:mm + 1], in1=sg[:, j, 0:N],
                                                                               op0=ALU.add, op1=ALU.mult),
                     reads=[p1.res, self.tmpres[j], self.cres], writes=[hy.res])


def e64_host():
    a = np.zeros((P, 32, P), np.float32)
    for m in range(P):
        for v in range(32):
            for j in range(2):
                if (m % 64) == 2 * v + j:
                    a[m, v, j * 64:(j + 1) * 64] = 30000.0
    return a.reshape(P, 32 * P)


def caus_host():
    a = np.zeros((P, 8, 512), np.float32)
    key = np.arange(P)[:, None]
    q = np.arange(512)[None, :]
    for j in range(4):
        a[:, j, :] = np.where(j * P + key <= q, 0.0, -30000.0)
        a[:, 4 + j, :] = np.where(j * P + key >= q, 0.0, -30000.0)
    return a.reshape(P, 8 * 512)


def slab_fm(W, k0, kc, mg, mcnt):
    W = np.asarray(W)
    sub = W[k0 * P:(k0 + kc) * P, mg * P:(mg + mcnt) * P].reshape(kc, P, mcnt * P)
    return np.ascontiguousarray(sub.transpose(1, 0, 2)).reshape(P, kc * mcnt * P)


def slab_tm(W):
    W = np.asarray(W)
    return np.ascontiguousarray(W.reshape(8, P, 512).transpose(1, 0, 2)).reshape(P, 8 * 512)


_CACHE = {}


def get_prog(cfg_key=(8192, 4, 8192, (0, 1, 2, 3))):
    if cfg_key not in _CACHE:
        cfg = Cfg(seq=cfg_key[0], nsamp=cfg_key[1], past=cfg_key[2], layers=cfg_key[3])
        pr = Prog(cfg)
        pr.build()
        n = len(pr.ws.recipes)
        cfg2 = Cfg(seq=cfg_key[0], nsamp=cfg_key[1], past=cfg_key[2], layers=cfg_key[3], nslab_max=n)
        pr = Prog(cfg2)
        pr.build()
        _CACHE[cfg_key] = pr
    return _CACHE[cfg_key]


def run_prog(pr, inputs, ncores):
    I = {k: np.asarray(v) for k, v in inputs.items()}
    wts = pr.ws.host_array(I)
    in_maps = []
    for c in range(ncores):
        m = {}
        for name, (shape, fn, npdt) in pr.host.items():
            if name == "wts":
                m[name] = wts
            else:
                key = (name, c)
                a = np.ascontiguousarray(np.asarray(fn(I, c), dtype=npdt)).reshape(shape)
                m[name] = a
        in_maps.append(m)
    res = run_bass_kernel_spmd(pr.nc, in_maps, core_ids=list(range(ncores)))
    return res.results


def kernel(**inputs):
    pr = get_prog()
    NS = 4
    res = run_prog(pr, inputs, 8)
    f32 = np.float32
    y_prompt = np.stack([res[0]["yp"], res[1]["yp"]]).astype(f32)
    y_sample = np.concatenate([res[c]["ys"] for c in range(8)], 0)[:, None, :].astype(f32)
    kv = (2, 4, 64)
    new_cmp_p = np.stack([res[0]["cmp_p"], res[1]["cmp_p"]]).reshape((1, 2, 8192) + kv).astype(f32)
    new_slc_p = np.stack([res[0]["slc_p"], res[1]["slc_p"]]).reshape((1, 2, 8192) + kv).astype(f32)
    new_win_p = np.stack([res[0]["win_p"], res[1]["win_p"]]).reshape((1, 2, 512) + kv).astype(f32)
    new_cmp_s = np.concatenate([res[c]["cmp_s"] for c in range(8)], 0).reshape((1, 32, 1) + kv).astype(f32)
    new_slc_s = np.concatenate([res[c]["slc_s"] for c in range(8)], 0).reshape((1, 32, 1) + kv).astype(f32)
    new_win_s = np.concatenate([res[c]["win_s"] for c in range(8)], 0).reshape((1, 32, 512) + kv).astype(f32)

    def st_p(a):
        return a.reshape(2, 64, 32, 2).transpose(2, 0, 1, 3).reshape(64, 64, 2)

    def st_s(a):
        return a.reshape(2, 64, 32, NS, 2).transpose(3, 2, 0, 1, 4).reshape(NS, 64, 64, 2)
    new_ssm_p = np.stack([st_p(res[0]["ssm_p"]), st_p(res[1]["ssm_p"])])[None].astype(f32)
    new_ssm_s = np.concatenate([st_s(res[c]["ssm_s"]) for c in range(8)], 0)[None].astype(f32)
    gvs = np.concatenate([res[c]["gv"] for c in range(8)], 1)[:, :, None, :].astype(f32)
    return (y_prompt, y_sample, new_cmp_p, new_cmp_s, new_slc_p, new_slc_s, new_win_p, new_win_s, new_ssm_p, new_ssm_s, gvs)
```

```python
import os
import numpy as np
from contextlib import ExitStack
import concourse.bass as bass
import concourse.mybir as mybir
from concourse.bass_utils import run_bass_kernel_spmd

F32 = mybir.dt.float32
BF16 = mybir.dt.bfloat16
I32 = mybir.dt.int32
AF = mybir.ActivationFunctionType
ALU = mybir.AluOpType
P = 128
D = 1024
KD = 8
DFF = 2816
KF = 22
SLOTW = 4096
RMS_EPS = 1e-6
LN_EPS = 1e-5


class Res:
    __slots__ = ("name", "w", "r", "dsem", "dcnt", "outanchor")

    def __init__(self, name):
        self.name = name
        self.w = None
        self.r = {}
        self.dsem = None
        self.dcnt = 0
        self.outanchor = None


class Sched:
    def __init__(self, nc, es):
        self.nc = nc
        self.es = es
        self.eng = {"pe": nc.tensor, "act": nc.scalar, "dve": nc.vector, "pool": nc.gpsimd, "sp": nc.sync}
        self.sem = {k: es.enter_context(nc.semaphore("S_" + k)) for k in self.eng}
        self.cnt = {k: 0 for k in self.eng}
        self.known = {k: {} for k in self.eng}
        self.nsem = 0
        self.out_anchors = []

    def _wait(self, e, tok):
        sem, val, src = tok[0], tok[1], tok[2]
        if src == "dma":
            val = tok[3].dcnt
        key = id(sem)
        if self.known[e].get(key, 0) >= val:
            return
        self.eng[e].wait_ge(sem, val)
        self.known[e][key] = val

    def _deps(self, e, reads, writes):
        toks = []
        for r in reads:
            if r.w is not None:
                toks.append(r.w)
        for w in writes:
            if w.w is not None:
                toks.append(w.w)
            toks.extend(w.r.values())
        for t in toks:
            if t[2] == "pe" and e == "pe":
                continue
            self._wait(e, t)

    def op(self, e, fn, reads=(), writes=()):
        self._deps(e, reads, writes)
        ins = fn(self.eng[e])
        self.cnt[e] += 1
        ins.then_inc(self.sem[e], 1)
        tok = (self.sem[e], self.cnt[e], e)
        for r in reads:
            r.r[e] = tok
        for w in writes:
            w.w = tok
            w.r = {}
        return ins

    def dma(self, q, pairs, reads=(), writes=(), anchor=None, serialize=True, **kw):
        self._deps(q, reads, writes)
        if anchor is None:
            if writes:
                anchor = writes[0]
            else:
                if reads[0].outanchor is None:
                    reads[0].outanchor = Res("out_" + reads[0].name)
                    self.out_anchors.append(reads[0].outanchor)
                anchor = reads[0].outanchor
        if anchor.dsem is None:
            anchor.dsem = self.es.enter_context(self.nc.semaphore("D%d" % self.nsem))
            self.nsem += 1
        elif serialize and anchor.dcnt > 0:
            self._wait(q, (anchor.dsem, anchor.dcnt, "dma", anchor))
        for (o, i) in pairs:
            self.eng[q].dma_start(out=o, in_=i, **kw).then_inc(anchor.dsem, 16)
            anchor.dcnt += 16
        tok = (anchor.dsem, anchor.dcnt, "dma", anchor)
        for r in reads:
            r.r[("dma", id(anchor.dsem))] = tok
        for w in writes:
            w.w = tok
            w.r = {}
        return anchor

    def dma_gather(self, out_ap, in_ap, idx_ap, reads=(), writes=()):
        self._deps("pool", reads, writes)
        anchor = writes[0]
        if anchor.dsem is None:
            anchor.dsem = self.es.enter_context(self.nc.semaphore("D%d" % self.nsem))
            self.nsem += 1
        elif anchor.dcnt > 0:
            self._wait("pool", (anchor.dsem, anchor.dcnt, "dma", anchor))
        self.eng["pool"].indirect_dma_start(out=out_ap, out_offset=None, in_=in_ap,
                                            in_offset=bass.IndirectOffsetOnAxis(ap=idx_ap, axis=0)).then_inc(anchor.dsem, 16)
        anchor.dcnt += 16
        tok = (anchor.dsem, anchor.dcnt, "dma", anchor)
        for r in reads:
            r.r[("dma", id(anchor.dsem))] = tok
        for w in writes:
            w.w = tok
            w.r = {}

    def finish(self):
        for a in self.out_anchors:
            if a.dsem is not None:
                self._wait("sp", (a.dsem, a.dcnt, "dma", a))


class Buf:
    def __init__(self, S, name, shape, dt, psum=False):
        nc = S.nc
        if psum:
            self.t = S.es.enter_context(nc.psum_tensor(name, shape, dt))
        else:
            self.t = S.es.enter_context(nc.sbuf_tensor(name, shape, dt))
        self.res = Res(name)
        self.shape = shape

    def __getitem__(self, idx):
        return self.t[idx]


class Cfg:
    def __init__(self, seq=8192, nsamp=4, past=8192, layers=(0, 1, 2, 3), nslab_max=176):
        self.SEQ = seq
        self.TT = 512
        self.NT = seq // 512
        self.NSAMP = nsamp
        self.PAST = past
        self.layers = tuple(layers)
        self.NSLAB = nslab_max


class WStream:
    def __init__(self, S, wdram, nslot, nslab):
        self.S = S
        self.wdram = wdram
        self.nslot = nslot
        self.slots = [Buf(S, "wslot%d" % i, [P, SLOTW], BF16) for i in range(nslot)]
        self.recipes = []
        self.index = {}
        self.pos = 0
        self.nslab = nslab

    def next(self, key, ncols, fn, npart=P):
        if key not in self.index:
            self.index[key] = len(self.recipes)
            self.recipes.append((ncols, npart, fn))
            assert len(self.recipes) <= self.nslab, "too many slabs"
        j = self.index[key]
        slot = self.slots[self.pos % self.nslot]
        self.pos += 1
        half = ncols // 2
        if ncols >= 1024:
            pairs = [(slot.t[0:npart, 0:half], self.wdram[j, 0:npart, 0:half]),
                     (slot.t[0:npart, half:ncols], self.wdram[j, 0:npart, half:ncols])]
        else:
            pairs = [(slot.t[0:npart, 0:ncols], self.wdram[j, 0:npart, 0:ncols])]
        if os.environ.get("NO_WDMA") == "1" and self.pos > self.nslot:
            return slot
        self.S.dma("pool", pairs, writes=[slot.res])
        return slot

    def host_array(self, weights):
        arr = np.zeros((self.nslab, P, SLOTW), np.float32)
        for j, (ncols, npart, fn) in enumerate(self.recipes):
            a = np.asarray(fn(weights), np.float32)
            assert a.shape == (npart, ncols), (a.shape, npart, ncols)
            arr[j, :npart, :ncols] = a
        return arr


class Prog:
    def __init__(self, cfg):
        self.cfg = cfg
        self.nc = bass.Bass("TRN2", target_bir_lowering=False)
        self.host = {}
        self.outs = {}
        self.es = ExitStack()

    def din(self, name, shape, fn, dt=F32):
        t = self.nc.dram_tensor(name, list(shape), dt, kind="ExternalInput").ap()
        self.host[name] = (tuple(shape), fn, np.int32 if dt == I32 else np.float32)
        return t

    def dout(self, name, shape, dt=F32):
        t = self.nc.dram_tensor(name, list(shape), dt, kind="ExternalOutput").ap()
        self.outs[name] = tuple(shape)
        return t

    def sb(self, name, shape, dt):
        return Buf(self.S, name, shape, dt)

    def const(self, name, shape, fn, dt=F32, sdt=None, q="sp"):
        d = self.din(name, shape, fn, dt)
        b = self.sb("c_" + name, list(shape), sdt or dt)
        idx = tuple(slice(None) for _ in shape)
        self.S.dma(q if (sdt is None or sdt == dt) else "pool", [(b.t[idx], d[idx])], writes=[b.res])
        return b

    def psum(self):
        b = self.pbanks[self.pidx % len(self.pbanks)]
        self.pidx += 1
        return b

    def sumsq_rstd(self, src_ap, src_res, N, eps, tag):
        S = self.S
        sqb = self.tmpb
        ps = self.psum()
        for k in range(KD):
            j = k % 2
            S.op("act", lambda e, k=k, j=j: e.activation(out=sqb[:, j, 0:N], in_=src_ap[:, k, :], func=AF.Square),
                 reads=[src_res], writes=[self.tmpbres[j]])
            S.op("pe", lambda e, k=k, j=j: e.matmul(ps[:, 0:N], lhsT=self.ones_bf[:, 0:P], rhs=sqb[:, j, 0:N],
                                                    start=(k == 0), stop=(k == KD - 1)),
                 reads=[self.tmpbres[j], self.cres], writes=[ps.res])
        rs = self.rstd
        S.op("act", lambda e: e.activation(out=rs[:, 0:N], in_=ps[:, 0:N], func=AF.Sqrt,
                                           bias=self.epsb[:, 0:1], scale=1.0 / D),
             reads=[ps.res, self.epsb.res], writes=[rs.res])
        S.op("dve", lambda e: e.reciprocal(out=rs[:, 0:N], in_=rs[:, 0:N]), reads=[rs.res], writes=[rs.res])
        return rs

    def prenorm(self, N, A, Bv, col=None):
        S = self.S
        xT, hy = self.xT, self.hy
        rs = self.sumsq_rstd(xT[:, :, 0:N], xT.res, N, RMS_EPS, "pre")
        hT = self.hT
        tmp = self.tmpf
        if col is not None:
            for k in range(KD):
                S.op("dve", lambda e, k=k: e.scalar_tensor_tensor(out=tmp[:, k % 2, 0:N], in0=xT[:, k, 0:N],
                                                                   scalar=A[:, k, col:col + 1], in1=rs[:, 0:N],
                                                                   op0=ALU.mult, op1=ALU.mult),
                     reads=[xT.res, rs.res, self.modres], writes=[self.tmpres[k % 2]])
                S.op("act", lambda e, k=k: e.activation(out=hT[:, k, 0:N], in_=tmp[:, k % 2, 0:N], func=AF.Identity,
                                                        bias=Bv[:, k, col:col + 1], scale=1.0),
                     reads=[self.tmpres[k % 2], self.modres], writes=[hy.res])
        else:
            for k in range(KD):
                S.op("dve", lambda e, k=k: e.tensor_tensor(out=tmp[:, 0, 0:N], in0=xT[:, k, 0:N], in1=rs[:, 0:N], op=ALU.mult),
                     reads=[xT.res, rs.res], writes=[self.tmpres[0]])
                S.op("dve", lambda e, k=k: e.tensor_tensor(out=tmp[:, 0, 0:N], in0=tmp[:, 0, 0:N], in1=A[:, k, :], op=ALU.mult),
                     reads=[self.tmpres[0], self.modres], writes=[self.tmpres[0]])
                S.op("dve", lambda e, k=k: e.tensor_tensor(out=hT[:, k, 0:N], in0=tmp[:, 0, 0:N], in1=Bv[:, k, :], op=ALU.add),
                     reads=[self.tmpres[0], self.modres], writes=[hy.res])

    def postnorm_residual(self, N, G, col=None):
        S = self.S
        xT, yT, hy = self.xT, self.yT, self.hy
        rs = self.sumsq_rstd(yT[:, :, 0:N], hy.res, N, RMS_EPS, "post")
        tmp = self.tmpf
        for k in range(KD):
            S.op("dve", lambda e, k=k: e.tensor_tensor(out=tmp[:, k % 2, 0:N], in0=yT[:, k, 0:N], in1=rs[:, 0:N], op=ALU.mult),
                 reads=[hy.res, rs.res], writes=[self.tmpres[k % 2]])
            if col is not None:
                S.op("dve", lambda e, k=k: e.scalar_tensor_tensor(out=xT[:, k, 0:N], in0=tmp[:, k % 2, 0:N],
                                                                   scalar=G[:, k, col:col + 1], in1=xT[:, k, 0:N],
                                                                   op0=ALU.mult, op1=ALU.add),
                     reads=[self.tmpres[k % 2], xT.res, self.modres], writes=[xT.res])
            else:
                S.op("dve", lambda e, k=k: e.tensor_tensor(out=tmp[:, k % 2, 0:N], in0=tmp[:, k % 2, 0:N], in1=G[:, k, :], op=ALU.mult),
                     reads=[self.tmpres[k % 2], self.modres], writes=[self.tmpres[k % 2]])
                S.op("dve", lambda e, k=k: e.tensor_tensor(out=xT[:, k, 0:N], in0=tmp[:, k % 2, 0:N], in1=xT[:, k, 0:N], op=ALU.add),
                     reads=[self.tmpres[k % 2], xT.res], writes=[xT.res])

    def mm_fm(self, key, wfn, nk, nm, rhs_fn, rhs_res, N, evac):
        S = self.S
        for mg in range(0, nm, 4):
            mcnt = min(4, nm - mg)
            kgroups = [(k0, min(8, nk - k0)) for k0 in range(0, nk, 8)]
            if len(kgroups) == 1:
                k0, kc = kgroups[0]
                slab = self.ws.next((key, mg, 0), kc * mcnt * P,
                                    lambda w, mg=mg, mcnt=mcnt, kc=kc: slab_fm(wfn(w), 0, kc, mg, mcnt))
                for m in range(mcnt):
                    ps = self.psum()
                    for k in range(kc):
                        S.op("pe", lambda e, k=k, m=m, ps=ps, slab=slab: e.matmul(
                            ps[:, 0:N], lhsT=slab[:, (k * mcnt + m) * P:(k * mcnt + m + 1) * P], rhs=rhs_fn(k),
                            start=(k == 0), stop=(k == kc - 1)),
                            reads=[slab.res, rhs_res], writes=[ps.res])
                    evac(mg + m, ps)
            else:
                pss = [self.psum() for _ in range(mcnt)]
                for (k0, kc) in kgroups:
                    slab = self.ws.next((key, mg, k0), kc * mcnt * P,
                                        lambda w, mg=mg, mcnt=mcnt, kc=kc, k0=k0: slab_fm(wfn(w), k0, kc, mg, mcnt))
                    for m in range(mcnt):
                        for k in range(kc):
                            kg = k0 + k
                            S.op("pe", lambda e, k=k, m=m, kg=kg, slab=slab: e.matmul(
                                pss[m][:, 0:N], lhsT=slab[:, (k * mcnt + m) * P:(k * mcnt + m + 1) * P], rhs=rhs_fn(kg),
                                start=(kg == 0), stop=(kg == nk - 1)),
                                reads=[slab.res, rhs_res], writes=[pss[m].res])
                for m in range(mcnt):
                    evac(mg + m, pss[m])

    def evac_y(self, N):
        yT, hy = self.yT, self.hy

        def f(m, ps):
            self.S.op("act", lambda e: e.activation(out=yT[:, m, 0:N], in_=ps[:, 0:N], func=AF.Copy),
                      reads=[ps.res], writes=[hy.res])
        return f

    def ffn(self, l, N):
        S = self.S
        hT, hy, act = self.hT, self.hy, self.act
        gtmp = self.tmpb
        for mg in range(0, KF, 4):
            mcnt = min(4, KF - mg)
            sg = self.ws.next(("ffg", l, mg), 8 * mcnt * P, lambda w, mg=mg, mcnt=mcnt: slab_fm(w["ffn_w_gate"][l], 0, 8, mg, mcnt))
            su = self.ws.next(("ffu", l, mg), 8 * mcnt * P, lambda w, mg=mg, mcnt=mcnt: slab_fm(w["ffn_w_up"][l], 0, 8, mg, mcnt))
            for m in range(mcnt):
                pg = self.psum()
                pu = self.psum()
                for (slab, ps) in ((sg, pg), (su, pu)):
                    for k in range(KD):
                        S.op("pe", lambda e, k=k, slab=slab, ps=ps: e.matmul(
                            ps[:, 0:N], lhsT=slab[:, (k * mcnt + m) * P:(k * mcnt + m + 1) * P], rhs=hT[:, k, 0:N],
                            start=(k == 0), stop=(k == KD - 1)), reads=[slab.res, hy.res], writes=[ps.res])
                j = (mg + m) % 2
                S.op("act", lambda e, j=j, pg=pg: e.activation(out=gtmp[:, j, 0:N], in_=pg[:, 0:N], func=AF.Silu),
                     reads=[pg.res], writes=[self.tmpbres[j]])
                S.op("dve", lambda e, j=j, pu=pu, c=mg + m: e.tensor_tensor(out=act[:, c, 0:N], in0=gtmp[:, j, 0:N], in1=pu[:, 0:N], op=ALU.mult),
                     reads=[self.tmpbres[j], pu.res], writes=[self.actres])
        self.mm_fm(("ffd", l), lambda w: w["ffn_w_down"][l], KF, KD, lambda k: act[:, k, 0:N], self.actres, N, self.evac_y(N))

    def gmlp(self, j, N, sample, vout=None):
        S = self.S
        hT, hy, act = self.hT, self.hy, self.act
        uT = act
        bu = self.gm_bu[j]

        def evac_u(m, ps):
            S.op("act", lambda e: e.activation(out=uT[:, m, 0:N], in_=ps[:, 0:N], func=AF.Gelu, bias=bu[:, m:m + 1], scale=1.0),
                 reads=[ps.res, self.cres], writes=[self.actres])
        self.mm_fm(("gmu", j), lambda w: w["gmlp_w_in"][j][:, 0:D], KD, KD, lambda k: hT[:, k, 0:N], hy.res, N, evac_u)
        sv = [self.ws.next(("gmv", j, hf), 8 * 512, lambda w, hf=hf: slab_tm(w["gmlp_w_in"][j][:, D + hf * 512:D + (hf + 1) * 512]))
              for hf in range(2)]
        nsub = 1 if sample else N // P
        TS = N if sample else P
        vtm, vln = self.vtm, self.vln
        g_bv = g_lng = g_lnb = self.gm_row

        def load_row(i3):
            S.dma("sp", [(self.gm_row[:, :], self.gm_rows_d[j][i3][:, :])], writes=[self.gm_row.res])
        for s in range(nsub):
            load_row(0)
            for hf in range(2):
                ps = self.psum()
                for k in range(KD):
                    S.op("pe", lambda e, k=k, ps=ps, hf=hf: e.matmul(ps[0:TS, 0:512], lhsT=hT[:, k, s * P:s * P + TS],
                                                                     rhs=sv[hf][:, k * 512:(k + 1) * 512],
                                                                     start=(k == 0), stop=(k == KD - 1)),
                         reads=[sv[hf].res, hy.res], writes=[ps.res])
                S.op("dve", lambda e, ps=ps, hf=hf: e.tensor_tensor(out=vtm[0:TS, hf * 512:(hf + 1) * 512], in0=ps[0:TS, 0:512],
                                                                    in1=g_bv[0:TS, hf * 512:(hf + 1) * 512], op=ALU.add),
                     reads=[ps.res, g_bv.res], writes=[self.vtmres])
            S.op("act", lambda e: e.activation(out=vtm[0:TS, :], in_=vtm[0:TS, :], func=AF.Gelu), reads=[self.vtmres], writes=[self.vtmres])
            st = self.stat
            for hf in range(2):
                S.op("dve", lambda e, hf=hf: e.bn_stats(out=st[0:TS, hf * 6:(hf + 1) * 6], in_=vtm[0:TS, hf * 512:(hf + 1) * 512]),
                     reads=[self.vtmres], writes=[self.statres])
            S.op("dve", lambda e: e.bn_aggr(out=st[0:TS, 12:14], in_=st[0:TS, 0:12]), reads=[self.statres], writes=[self.statres])
            S.op("act", lambda e: e.activation(out=st[0:TS, 14:15], in_=st[0:TS, 13:14], func=AF.Sqrt, bias=self.lnepsb[0:TS, 0:1], scale=1.0),
                 reads=[self.statres, self.cres], writes=[self.statres])
            S.op("dve", lambda e: e.reciprocal(out=st[0:TS, 15:16], in_=st[0:TS, 14:15]), reads=[self.statres], writes=[self.statres])
            S.op("dve", lambda e: e.tensor_scalar(out=vtm[0:TS, :], in0=vtm[0:TS, :], scalar1=st[0:TS, 12:13], scalar2=st[0:TS, 15:16],
                                                  op0=ALU.subtract, op1=ALU.mult),
                 reads=[self.vtmres, self.statres], writes=[self.vtmres])
            load_row(1)
            S.op("dve", lambda e: e.tensor_tensor(out=vtm[0:TS, :], in0=vtm[0:TS, :], in1=g_lng[0:TS, :], op=ALU.mult),
                 reads=[self.vtmres, g_lng.res], writes=[self.vtmres])
            load_row(2)
            S.op("dve", lambda e: e.tensor_tensor(out=vtm[0:TS, :], in0=vtm[0:TS, :], in1=g_lnb[0:TS, :], op=ALU.add),
                 reads=[self.vtmres, g_lnb.res], writes=[self.vtmres])
            S.op("act", lambda e: e.activation(out=vln[0:TS, :], in_=vtm[0:TS, :], func=AF.Copy), reads=[self.vtmres], writes=[self.vlnres])
            if vout is not None:
                S.dma("sp", [(vout, vtm[0:TS, :])], reads=[self.vtmres])
            for g in range(8):
                ps = self.psum()
                if sample:
                    rhs_w = self.gm_wsS[j][0:TS, g, 0:TS]
                    rhs_b = self.gm_bsS[j][0:1, g, 0:TS]
                else:
                    rhs_w = self.gm_ws[j][:, g, :]
                    rhs_b = self.gm_bs[j][0:1, g, :]
                S.op("pe", lambda e, g=g, ps=ps, rhs_w=rhs_w: e.matmul(ps[:, 0:TS], lhsT=vln[0:TS, g * P:(g + 1) * P], rhs=rhs_w, start=True, stop=False),
                     reads=[self.vlnres, self.cres], writes=[ps.res])
                S.op("pe", lambda e, g=g, ps=ps, rhs_b=rhs_b: e.matmul(ps[:, 0:TS], lhsT=self.ones_bf[0:1, 0:P], rhs=rhs_b, start=False, stop=True),
                     reads=[self.cres, self.ones_bf.res], writes=[ps.res])
                S.op("dve", lambda e, g=g, ps=ps: e.tensor_tensor(out=uT[:, g, s * P:s * P + TS], in0=uT[:, g, s * P:s * P + TS], in1=ps[:, 0:TS], op=ALU.mult),
                     reads=[ps.res, self.actres], writes=[self.actres])
        self.mm_fm(("gmo", j), lambda w: w["gmlp_w_out"][j], KD, KD, lambda k: uT[:, k, 0:N], self.actres, N, self.evac_y(N))

    def build(self):
        cfg = self.cfg
        nc = self.nc
        es = self.es
        S = self.S = Sched(nc, es)
        TT, NT, NS = cfg.TT, cfg.NT, cfg.NSAMP
        NC = 1 + NS
        self.NC = NC
        wdram = self.din("wts", [cfg.NSLAB, P, SLOTW], None)
        xp = self.din("xp", [cfg.SEQ, D], lambda I, c: I["x_prompt"][c % 2])
        xs = self.din("xs", [NS, D], lambda I, c: I["x_sample"][c * NS:(c + 1) * NS, 0])
        yp = self.dout("yp", [cfg.SEQ, D])
        ys = self.dout("ys", [NS, D])
        gv = self.dout("gv", [2, NS, D])
        self.ws = WStream(S, wdram, 4, cfg.NSLAB)
        self.xT = self.sb("xT", [P, KD, TT], F32)
        self.hy = self.sb("hy", [P, KD, TT], F32)
        self.yT = self.hy.t
        self.hT = self.hy.t[:].rearrange("p k t -> p (k t)")[:, 0:KD * TT // 2].bitcast(BF16).rearrange("p (k t) -> p k t", k=KD)
        self.act = self.sb("act", [P, KF, TT], BF16)
        self.actres = self.act.res
        self.rstd = self.sb("rstd", [P, TT], F32)
        self.tmpf = self.sb("tmpf", [P, 2, TT], F32)
        self.tmpres = [Res("tmpf0"), Res("tmpf1")]
        self.tmpb = self.sb("tmpb", [P, 2, TT], BF16)
        self.tmpbres = [Res("tmpb0"), Res("tmpb1")]
        self.vtm = self.act.t[:, 8:12, :].rearrange("p k t -> p (k t)").bitcast(F32)
        self.vtmres = self.actres
        self.vln = self.act.t[:, 12:14, :].rearrange("p k t -> p (k t)")
        self.vlnres = self.actres
        self.stat = self.sb("stat", [P, 16], F32)
        self.statres = self.stat.res
        self.xin = Buf.__new__(Buf)
        self.xin.t = self.hy.t[:].rearrange("p k t -> p (k t)").rearrange("p (s d) -> p s d", s=4)
        self.xin.res = self.hy.res
        self.pbanks = [Buf(S, "ps%d" % i, [P, 512], F32, psum=True) for i in range(6)]
        self.pidx = 0
        self.pacc = [Buf(S, "pacc%d" % i, [P, 512], F32, psum=True) for i in range(2)]
        self.paidx = 0
        self.vout_anchor = Res("vout")
        S.out_anchors.append(self.vout_anchor)
        self.cres = Res("consts")
        self.cbar = self.sb("cbar", [P, 2], F32)
        self.modres = Res("mod")

        self.cres_hw = Res("consts_hw")
        self.cres_sw = Res("consts_sw")
        cload = self._cload
        self.ident = cload("ident", [P, P], lambda I, c: np.eye(P, dtype=np.float32))
        self.ones_bf = cload("ones", [P, P], lambda I, c: np.ones((P, P), np.float32), BF16)
        self.epsb = cload("epsb", [P, 1], lambda I, c: np.full((P, 1), RMS_EPS, np.float32))
        self.lnepsb = cload("lnepsb", [P, 1], lambda I, c: np.full((P, 1), LN_EPS, np.float32))
        nG = 2
        self.gm_bu = [cload("gm_bu%d" % j, [P, KD], lambda I, c, j=j: I["gmlp_b_in"][j][0:D].reshape(KD, P).T) for j in range(nG)]
        self.gm_rows_d = [[self.din("gm_bv%d" % j, [P, D], lambda I, c, j=j: np.broadcast_to(I["gmlp_b_in"][j][D:2 * D], (P, D))),
                           self.din("gm_lng%d" % j, [P, D], lambda I, c, j=j: np.broadcast_to(I["gmlp_ln_g"][j], (P, D))),
                           self.din("gm_lnb%d" % j, [P, D], lambda I, c, j=j: np.broadcast_to(I["gmlp_ln_b"][j], (P, D)))] for j in range(nG)]
        self.gm_row = self.sb("gm_row", [P, D], F32)
        self.gm_ws = [cload("gm_ws%d" % j, [P, 8, P], lambda I, c, j=j: np.where(np.tril(np.ones((P, P), bool))[None], I["gmlp_w_s"][j], 0.0).transpose(2, 0, 1), BF16) for j in range(nG)]
        self.gm_bs = [cload("gm_bs%d" % j, [1, 8, P], lambda I, c, j=j: I["gmlp_b_s"][j][None], BF16) for j in range(nG)]

        def wsS(I, c, j):
            a = np.zeros((NS, 8, NS), np.float32)
            for g in range(8):
                for q in range(NS):
                    a[q, g, q] = I["gmlp_w_s"][j][g, 0, 0]
            return a
        self.gm_wsS = [cload("gm_wsS%d" % j, [NS, 8, NS], lambda I, c, j=j: wsS(I, c, j), BF16) for j in range(nG)]
        self.gm_bsS = [cload("gm_bsS%d" % j, [1, 8, NS], lambda I, c, j=j: np.broadcast_to(I["gmlp_b_s"][j][:, 0][None, :, None], (1, 8, NS)), BF16) for j in range(nG)]
        scT = cload("scT", [P, KD, NC], lambda I, c: np.concatenate([I["c_prompt"][c % 2][None], I["c_sample"][c * NS:(c + 1) * NS]], 0).T.reshape(KD, P, NC).transpose(1, 0, 2))
        bmod = cload("bmod", [P, 4, 6, KD], lambda I, c: I["b_mod"].reshape(4, 6, KD, P).transpose(3, 0, 1, 2))
        ng = cload("ng", [P, 4, 4, KD], lambda I, c: I["norm_g"].reshape(4, 4, KD, P).transpose(3, 0, 1, 2))
        if 2 in cfg.layers:
            self.s5d = cload("s5d", [P, KD], lambda I, c: I["ssm_d"][0].reshape(KD, P).T)
            self.s5b1 = cload("s5b1", [P, KD], lambda I, c: I["ssm_b_glu1"][0].reshape(KD, P).T)
            self.s5b2 = cload("s5b2", [P, KD], lambda I, c: I["ssm_b_glu2"][0].reshape(KD, P).T)
        if 1 in cfg.layers:
            self.nsa_consts()
        self.consts_barrier()
        if 2 in cfg.layers:
            self.s5_setup()
        if 1 in cfg.layers:
            self.nsa_setup()
        scb = self.sb("scb", [P, KD, NC], BF16)
        S.op("act", lambda e: e.activation(out=scb[:, :, :], in_=scT[:, :, :], func=AF.Silu), reads=[self.cres], writes=[scb.res])
        self.mod = self.sb("mod", [P, 4, 6, KD, NC], F32)
        self.mod.res = self.modres
        for l in cfg.layers:
            for jj in range(6):
                def ev(m, ps, l=l, jj=jj):
                    S.op("act", lambda e: e.activation(out=self.mod[:, l, jj, m, :], in_=ps[:, 0:NC], func=AF.Identity,
                                                       bias=bmod[:, l, jj, m:m + 1], scale=1.0),
                         reads=[ps.res, self.cres], writes=[self.modres])
                self.mm_fm(("mod", l, jj), lambda w, l=l, jj=jj: w["w_mod"][l][:, jj * D:(jj + 1) * D], KD, KD,
                           lambda k: scb[:, k, :], scb.res, NC, ev)
            for (sc_i, g_i) in ((1, 0), (4, 2)):
                for col in range(NC):
                    S.op("dve", lambda e, col=col, sc_i=sc_i, g_i=g_i: e.scalar_tensor_tensor(
                        out=self.mod[:, l, sc_i, :, col], in0=self.mod[:, l, sc_i, :, col], scalar=1.0, in1=ng[:, l, g_i, :],
                        op0=ALU.add, op1=ALU.mult), reads=[self.modres, self.cres], writes=[self.modres])
            for (ga_i, g_i) in ((2, 1), (5, 3)):
                for col in range(NC):
                    S.op("dve", lambda e, col=col, ga_i=ga_i, g_i=g_i: e.tensor_tensor(
                        out=self.mod[:, l, ga_i, :, col], in0=self.mod[:, l, ga_i, :, col], in1=ng[:, l, g_i, :], op=ALU.mult),
                        reads=[self.modres, self.cres], writes=[self.modres])
        yout_anchor = Res("yout")
        S.out_anchors.append(yout_anchor)
        for it in range(NT + 1):
            sample = (it == NT)
            N = NS if sample else TT
            nsub = 1 if sample else 4
            TS = NS if sample else P
            xin = self.xin
            if sample:
                S.dma("sp", [(xin[0:NS, 0, :], xs[:, :])], writes=[xin.res])
            else:
                S.dma("sp", [(xin[:, s, :], xp[it * TT + s * P:it * TT + (s + 1) * P, :]) for s in range(4)], writes=[xin.res])
            for k in range(KD):
                ps = self.psum()
                for s in range(nsub):
                    S.op("pe", lambda e, k=k, s=s, ps=ps: e.transpose(ps[:, s * P:s * P + TS], xin[0:TS, s, k * P:(k + 1) * P], self.ident[0:TS, 0:TS]),
                         reads=[xin.res, self.cres], writes=[ps.res])
                S.op("act", lambda e, k=k, ps=ps: e.activation(out=self.xT[:, k, 0:N], in_=ps[:, 0:N], func=AF.Copy),
                     reads=[ps.res], writes=[self.xT.res])
            for l in cfg.layers:
                mod = self.mod
                if sample:
                    A1, B1, G1 = mod[:, l, 1, :, 1:NC], mod[:, l, 0, :, 1:NC], mod[:, l, 2, :, 1:NC]
                    A2, B2, G2 = mod[:, l, 4, :, 1:NC], mod[:, l, 3, :, 1:NC], mod[:, l, 5, :, 1:NC]
                    col = None
                else:
                    A1, B1, G1 = mod[:, l, 1], mod[:, l, 0], mod[:, l, 2]
                    A2, B2, G2 = mod[:, l, 4], mod[:, l, 3], mod[:, l, 5]
                    col = 0
                self.prenorm(N, A1, B1, col)
                kind = l % 3
                if kind == 0:
                    self.gmlp(l // 3, N, sample, vout=(gv[l // 3, :, :] if sample else None))
                elif kind == 1:
                    self.nsa(it, N, sample)
                else:
                    self.s5(it, N, sample)
                self.postnorm_residual(N, G1, col)
                self.prenorm(N, A2, B2, col)
                self.ffn(l, N)
                self.postnorm_residual(N, G2, col)
            xo = self.xin
            for s in range(nsub):
                for kq in range(2):
                    ps = self.psum()
                    for kk in range(4):
                        k = kq * 4 + kk
                        S.op("pe", lambda e, k=k, kk=kk, s=s, ps=ps: e.transpose(ps[0:TS, kk * P:(kk + 1) * P], self.xT[:, k, s * P:s * P + TS], self.ident[:, :]),
                             reads=[self.xT.res, self.cres], writes=[ps.res])
                    S.op("act", lambda e, s=s, kq=kq, ps=ps: e.activation(out=xo[0:TS, s, kq * 512:(kq + 1) * 512], in_=ps[0:TS, 0:512], func=AF.Copy),
                         reads=[ps.res], writes=[xo.res])
            if sample:
                S.dma("sp", [(ys[:, :], xo[0:NS, 0, :])], reads=[xo.res])
            else:
                S.dma("sp", [(yp[it * TT + s * P:it * TT + (s + 1) * P, :], xo[:, s, :]) for s in range(4)], reads=[xo.res])
        S.finish()
        return nc

    def nsa_setup(self):
        S = self.S
        cfg = self.cfg
        NT, NS, SEQ = cfg.NT, cfg.NSAMP, cfg.SEQ
        self.HE = [0, 1, 2, 3, 8, 9, 10, 11]
        self.HO = [4, 5, 6, 7, 12, 13, 14, 15]
        self.KTs = self.sb("KTs", [P, 2, SEQ], BF16)
        NKT = SEQ // P
        self.Vs = self.sb("Vg", [P, NKT, 66], BF16)
        self.Vst = self.sb("Vst", [P, 4, 66], BF16)
        self.vscr = self.nc.dram_tensor("vscr", [4, P, NKT, 66], BF16, kind="Internal").ap()
        self.vscr_res = Res("vscr")
        self.KTw = self.sb("KTw", [P, 2, 2, 512], BF16)
        self.Vw = self.sb("Vw", [P, 2, 4, 4, 66], BF16)
        self.KcT = self.sb("KcT", [P, 2, P], BF16)
        self.Vc = self.sb("Vc", [P, 4, 66], BF16)
        S.op("pool", lambda e: e.memset(self.Vst[:, :, :], 1.0), writes=[self.Vst.res])
        S.op("pool", lambda e: e.memset(self.Vw[:, :, :, :, :], 1.0), writes=[self.Vw.res])
        S.op("pool", lambda e: e.memset(self.KcT[:, :, :], 0.0), writes=[self.KcT.res])
        S.op("pool", lambda e: e.memset(self.Vc[:, :, 0:64], 0.0), writes=[self.Vc.res])
        self.Vc32 = self.sb("Vc32", [P, 256], F32)
        S.op("pool", lambda e: e.memset(self.Vc32[:, :], 0.0), writes=[self.Vc32.res])
        S.op("pool", lambda e: e.memset(self.Vc[:, :, 64:66], 1.0), writes=[self.Vc.res])
        self.XT = Buf.__new__(Buf)
        self.XT.t = self.act.t[:, 16:20, :]
        self.XT.res = Res("XT")
        self.gates = self.sb("gates", [P, 4, 48], F32)
        self.negmT = self.sb("negmT", [P, 2, 5, 512], BF16)
        S.op("pool", lambda e: e.memset(self.negmT[:, :, :, :], 0.0), writes=[self.negmT.res])
        self.tk1 = Buf.__new__(Buf)
        self.tk1.t = self.tmpf.t[:, 0, :].rearrange("p (a b) -> p a b", a=4)
        self.tk1.res = self.tmpres[0]
        self.tk2 = Buf.__new__(Buf)
        self.tk2.t = self.tmpf.t[:, 1, 0:384].rearrange("p (a b) -> p a b", a=3)
        self.tk2.res = self.tmpres[1]
        self.tks = self.sb("tks", [P, 32], F32)
        self.negm = self.sb("negm", [P, P], BF16)
        self.hidK = self.sb("hidK", [P, 2, 2, 16], BF16)
        self.hidV = self.sb("hidV", [P, 4, P], BF16)
        self.pebias = self.sb("pebias", [P, 2], F32)
        self.mk = self.sb("mk", [P, 3, 4, P], BF16)
        self.f4 = self.sb("f4", [P, 8], F32)


        def masks(I, c):
            a = np.zeros((NT, P, 3, 4, P), np.float32)
            n = np.arange(P)[None, :]
            for it in range(NT):
                for s_ in range(4):
                    t = (it * 512 + s_ * 128 + np.arange(P))[:, None]
                    jt = t // 64
                    a[it, :, 0, s_, :] = np.where((n + 1) * 64 <= t + 1, 0.0, -30000.0)
                    vd = n <= jt
                    f = vd & ((n == 0) | (n == jt) | (n == jt - 1))
                    a[it, :, 1, s_, :] = (vd & ~f).astype(np.float32)
                    a[it, :, 2, s_, :] = np.where(f, 1.0e4, np.where(vd, 0.0, -1.0))
            return a
        self.mk_d = self.din("nsa_masks", [NT, P, 3, 4, P], masks)

        def cmpbt(I, c):
            a = np.zeros((NT, P, 512), np.float32)
            n = np.arange(P)[:, None]
            for it in range(NT):
                t = (it * 512 + np.arange(512))[None, :]
                a[it] = np.where((n + 1) * 64 <= t + 1, 0.0, -30000.0)
            return a
        self.cmpbt_d = self.din("nsa_cmpbt", [NT, P, 512], cmpbt)
        self.cmp_p = self.dout("cmp_p", [SEQ, 512])
        self.slc_p = self.dout("slc_p", [SEQ, 512])
        self.win_p = self.dout("win_p", [512, 512])
        self.kv_anchor = Res("kvout")
        S.out_anchors.append(self.kv_anchor)
        NPG = cfg.PAST // P
        self.NPG = NPG
        self.cmp_s = self.dout("cmp_s", [NS, 512])
        self.slc_s = self.dout("slc_s", [NS, 512])
        self.win_s = self.dout("win_s", [NS, 512, 512])
        self.cache_c = self.din("cache_c", [2560 * P, 512], lambda I, c: I["cache_nsa_cmp"][0].reshape(-1, 512))
        self.cache_s = self.din("cache_s", [2560 * P, 512], lambda I, c: I["cache_nsa_slc"][0].reshape(-1, 512))
        self.cache_w = self.din("cache_w", [NS, 512, 512], lambda I, c: I["cache_nsa_win"][0][c * NS:(c + 1) * NS].reshape(NS, 512, 512))
        pt_d = self.din("ptab", [1, NS * NPG], lambda I, c: I["page_table"][c * NS:(c + 1) * NS].reshape(1, NS * NPG), I32)
        hyflat = self.hy.t[:].rearrange("p k t -> p (k t)")
        self.PGn = Buf.__new__(Buf)
        self.PGn.t = hyflat[:, 2048:3072].rearrange("p (a b) -> p a b", a=2)
        self.pgn_res = Res("PGn")
        self.PGn.res = self.pgn_res
        self.XTs = Buf.__new__(Buf)
        self.XTs.t = hyflat[:, 0:2048].bitcast(BF16).rearrange("p (a b) -> p a b", a=4)
        self.XTs.res = self.hy.res
        self.KTp = [self.sb("KTp%d" % i, [P, 2, P], BF16) for i in range(2)]
        self.Vp = [self.sb("Vp%d" % i, [P, 4, 2, P], BF16) for i in range(1)]
        for i in range(1):
            S.op("pool", lambda e, i=i: e.memset(self.Vp[i][:, :, :, :], 0.0), writes=[self.Vp[i].res])
        self.pTs = [self.sb("pTs%d" % i, [P, 16], BF16) for i in range(2)]
        self.KcS = self.sb("KcS", [P, 2, P], BF16)
        self.hidVs = self.sb("hidVs", [P, 4, P], BF16)
        self.impT = self.sb("impT", [P, 4 * NS], F32)
        self.sw = self.sb("sampw", [P, 64], F32)
        self.sc16 = self.sb("sc16", [4 * NS, 2, P], F32)
        self.sk16 = self.sb("sk16", [4 * NS, 32], F32)
        self.ng16 = self.sb("ng16", [4 * NS, P], BF16)
        self.negS = self.sb("negS", [P, 2, 4 * NS], BF16)
        S.op("pool", lambda e: e.memset(self.negS[:, :, :], 0.0), writes=[self.negS.res])
        self.res_o = self.sb("res_o", [P, 3, NS, 8], F32)
        self.res_s = self.sb("res_s", [P, 3, NS, 8], F32)
        self.idxf = self.sb("idxf", [P, NS * NPG], F32)
        self.idxi = self.sb("idxi", [P, NS * NPG], I32)
        pti = self.sb("pti", [P, NS * NPG], I32)
        S.dma("sp", [(pti[:, :], pt_d[0:1, :].to_broadcast([P, NS * NPG]))], writes=[pti.res])
        S.op("dve", lambda e: e.tensor_copy(out=self.idxf[:, :], in_=pti[:, :]), reads=[pti.res], writes=[self.idxf.res])
        S.op("dve", lambda e: e.tensor_scalar(out=self.idxf[:, :], in0=self.idxf[:, :], scalar1=float(P), scalar2=self.iota_p[:, 0:1], op0=ALU.mult, op1=ALU.add),
             reads=[self.idxf.res, self.cres], writes=[self.idxf.res])
        S.op("dve", lambda e: e.tensor_copy(out=self.idxi[:, :], in_=self.idxf[:, :]), reads=[self.idxf.res], writes=[self.idxi.res])
        self.kti = 0
        self.wcopy_res = Res("wcopy")
        pb = self.psum()
        for z in range(2):
            for lh in range(2):
                slab = self.w1slab(z, lh, 0)
                for l in range(32):
                    la = lh * 32 + l
                    S.op("pe", lambda e, z=z, l=l, la=la, slab=slab: e.matmul(pb[:, z:z + 1], lhsT=slab[:, l * P:(l + 1) * P], rhs=self.peT[:, z, la:la + 1],
                                                                              start=(la == 0), stop=(la == 63)),
                         reads=[slab.res, self.cres], writes=[pb.res])
        S.op("act", lambda e: e.activation(out=self.pebias[:, :], in_=pb[:, 0:2], func=AF.Copy), reads=[pb.res], writes=[self.pebias.res])

    def nsa_consts(self):
        self.ident_bf = self._cload("ident_b", [P, P], lambda I, c: np.eye(P, dtype=np.float32), BF16)

        def w2k(I, c):
            a = np.zeros((P, 2, P), np.float32)
            a[:, 0, 0:64] = I["nsa_w_cmp2"][0][0]
            a[:, 1, 64:128] = I["nsa_w_cmp2"][0][0]
            return a
        self.W2K = self._cload("w2k", [P, 2, P], w2k, BF16)
        self.W2V = self._cload("w2v", [P, 64], lambda I, c: I["nsa_w_cmp2"][0][1], BF16)
        self.peT = self._cload("peT", [P, 2, 64], lambda I, c: np.concatenate([I["nsa_pe_cmp"][0].transpose(2, 0, 1), np.zeros((64, 2, 64), np.float32)], 0), BF16)

        NS = self.cfg.NSAMP
        self.iota_p = self._cload("iota_p", [P, 1], lambda I, c: np.arange(P, dtype=np.float32)[:, None])

        def sel(I, c):
            a = np.zeros((NS, NS, 2, P), np.float32)
            for b in range(NS):
                a[b, b, 0, 0:64] = 1.0
                a[b, b, 1, 64:128] = 1.0
            return a
        self.Sel = self._cload("sel", [NS, NS, 2, P], sel)

        def newmask(I, c):
            a = np.full((P, NS, 4), -30000.0, np.float32)
            for b in range(NS):
                a[b, b, :] = 0.0
            return a
        self.newmask = self._cload("newmask", [P, NS, 4], newmask, BF16)

        def onesv(I, c):
            a = np.zeros((P, 2, P), np.float32)
            a[:, 0, 0:64] = 1.0
            a[:, 1, 64:128] = 1.0
            return a
        self.OnesV = self._cload("onesv", [P, 2, P], onesv, BF16)

    def _cload(self, name, shape, fn, sdt=F32):
        d = self.din(name, shape, fn)
        b = Buf(self.S, "c_" + name, list(shape), sdt)
        b.res = self.cres
        idx = tuple(slice(None) for _ in shape)
        if sdt != F32:
            self.S.dma("pool", [(b.t[idx], d[idx])], writes=[self.cres_sw], anchor=self.cres_sw, serialize=False)
        else:
            self.S.dma("sp", [(b.t[idx], d[idx])], writes=[self.cres_hw], anchor=self.cres_hw, serialize=False)
        return b

    def consts_barrier(self):
        self.S.op("dve", lambda e: e.memset(self.cbar[:, 0:1], 0.0), reads=[self.cres_hw, self.cres_sw], writes=[self.cres, self.cbar.res])

    def w1slab(self, z, lh, g2):
        def fn(w, z=z, lh=lh, g2=g2):
            w1 = np.asarray(w["nsa_w_cmp1"][0][z])[lh * 32:(lh + 1) * 32]
            a = np.zeros((P, 32 * P), np.float32)
            a[g2 * 64:(g2 + 1) * 64] = w1.transpose(1, 0, 2).reshape(64, 32 * P)
            return a
        return self.ws.next(("w1", z, lh, g2), 32 * P, fn)

    def head_loc(self, h):
        g, r = h // 4, h % 4
        return 4 * (g // 2) + r, g % 2

    def nsa_project(self, N, sample, t0, slot):
        S = self.S
        cfg = self.cfg
        hT, hy, act = self.hT, self.hy, self.act
        qperm = np.concatenate([np.concatenate([np.arange(self.HE[j] * 64, self.HE[j] * 64 + 64), np.arange(self.HO[j] * 64, self.HO[j] * 64 + 64)]) for j in range(8)])

        S.op("pool", lambda e: e.memset(act[64:128, 0:8, 0:N], 0.0), writes=[self.actres])
        S.op("pool", lambda e: e.memset(act[0:64, 8:16, 0:N], 0.0), writes=[self.actres])

        def evq(m, ps):
            S.op("act", lambda e: e.activation(out=act[0:64, m, 0:N], in_=ps[0:64, 0:N], func=AF.Copy, scale=0.125), reads=[ps.res], writes=[self.actres])
            S.op("act", lambda e: e.activation(out=act[64:128, 8 + m, 0:N], in_=ps[64:128, 0:N], func=AF.Copy, scale=0.125), reads=[ps.res], writes=[self.actres])
        self.mm_fm(("nq",), lambda w: w["nsa_w_in"][0][:, qperm], KD, KD, lambda k: hT[:, k, 0:N], hy.res, N, evq)
        sub = int(os.environ.get("NSA_SUB", "9"))
        if sub < 2:
            return
        kcols = np.concatenate([np.arange(1024, 1280), np.arange(1280, 1536), np.arange(1536, 1792), np.arange(2048, 2304)])

        def evk(m, ps):
            if m < 4:
                S.op("act", lambda e: e.activation(out=self.XT[:, m, 0:N], in_=ps[:, 0:N], func=AF.Copy), reads=[ps.res], writes=[self.XT.res])
            elif m < 6:
                S.op("act", lambda e: e.activation(out=self.KTs[:, m - 4, t0:t0 + N], in_=ps[:, 0:N], func=AF.Copy), reads=[ps.res], writes=[self.KTs.res])
            else:
                S.op("act", lambda e: e.activation(out=self.KTw[:, slot, m - 6, 0:N], in_=ps[:, 0:N], func=AF.Copy), reads=[ps.res], writes=[self.KTw.res])
        if not sample:
            self.mm_fm(("nk",), lambda w: w["nsa_w_in"][0][:, kcols], KD, KD, lambda k: hT[:, k, 0:N], hy.res, N, evk)
        if sub < 3:
            return
        skv = [self.ws.next(("nkv", b), 8 * 512, lambda w, b=b: slab_tm(w["nsa_w_in"][0][:, 1024 + 512 * b:1536 + 512 * b])) for b in range(3)]
        sg = self.ws.next(("ngate",), 8 * 48, lambda w: np.ascontiguousarray(w["nsa_w_in"][0][:, 2560:2608].reshape(8, P, 48).transpose(1, 0, 2)).reshape(P, 8 * 48))
        nsub = 1 if sample else 4
        TS = N if sample else P
        tf = self.tmpf
        cnt = 0
        for s_ in range(nsub):
            for b in range(3):
                ps = self.psum()
                for k in range(KD):
                    S.op("pe", lambda e, k=k, ps=ps, b=b: e.matmul(ps[0:TS, 0:512], lhsT=hT[:, k, s_ * P:s_ * P + TS], rhs=skv[b][:, k * 512:(k + 1) * 512],
                                                                   start=(k == 0), stop=(k == KD - 1)), reads=[skv[b].res, hy.res], writes=[ps.res])
                j = cnt % 2
                cnt += 1
                S.op("act", lambda e, ps=ps, j=j: e.activation(out=tf[0:TS, j, :], in_=ps[0:TS, 0:512], func=AF.Copy), reads=[ps.res], writes=[self.tmpres[j]])
                if sample:
                    outs = [self.cmp_s[:, :], self.slc_s[:, :], self.win_s[:, 511, :]][b]
                    S.dma("sp", [(outs, tf[0:TS, j, :])], reads=[self.tmpres[j]])
                    if b > 0:
                        S.op("act", lambda e, ps=ps, b=b: e.activation(out=self.PGn[0:TS, b - 1, :], in_=ps[0:TS, 0:512], func=AF.Copy), reads=[ps.res], writes=[self.PGn.res])
                else:
                    r0 = t0 + s_ * P
                    if b == 0:
                        if sub >= 4:
                            S.dma("sp", [(self.cmp_p[r0:r0 + P, :], tf[:, j, :])], reads=[self.tmpres[j]])
                    elif b == 1:
                        if sub >= 4:
                            S.dma("sp", [(self.slc_p[r0:r0 + P, :], tf[:, j, :])], reads=[self.tmpres[j]])
                        kt = r0 // P
                        if sub >= 5:
                            S.op("act", lambda e, ps=ps, kt=kt: e.activation(out=self.Vst[:, :, 0:64], in_=ps[:, 256:512].rearrange("p (g d) -> p g d", g=4), func=AF.Copy),
                                 reads=[ps.res], writes=[self.Vst.res])
                            S.dma("sp", [(self.vscr[g_, :, kt, :], self.Vst[:, g_, :]) for g_ in range(4)], reads=[self.Vst.res], writes=[self.vscr_res])
                    else:
                        if t0 == cfg.SEQ - 512 and sub >= 4:
                            S.dma("sp", [(self.win_p[s_ * P:(s_ + 1) * P, :], tf[:, j, :])], reads=[self.tmpres[j]])
                        if sub >= 5:
                            S.op("act", lambda e, ps=ps: e.activation(out=self.Vw[:, slot, s_, :, 0:64], in_=ps[:, 256:512].rearrange("p (g d) -> p g d", g=4), func=AF.Copy),
                                 reads=[ps.res], writes=[self.Vw.res])
            if sub < 6:
                continue
            ps = self.psum()
            for k in range(KD):
                S.op("pe", lambda e, k=k, ps=ps: e.matmul(ps[0:TS, 0:48], lhsT=hT[:, k, s_ * P:s_ * P + TS], rhs=sg[:, k * 48:(k + 1) * 48],
                                                          start=(k == 0), stop=(k == KD - 1)), reads=[sg.res, hy.res], writes=[ps.res])
            S.op("act", lambda e, ps=ps: e.activation(out=self.gates[0:TS, s_, :], in_=ps[0:TS, 0:48], func=AF.Sigmoid), reads=[ps.res], writes=[self.gates.res])

    def nsa_compress(self, nblk, xt_fn, xt_res, k_out, v_out_hid, hid_res=None):
        S = self.S
        pc = self.psum()
        nb2 = 2 * nblk
        for z in range(2):
            for lh in range(2):
                for g2 in range(2):
                    slab = self.w1slab(z, lh, g2)
                    for l in range(32):
                        la = lh * 32 + l
                        S.op("pe", lambda e, z=z, l=l, la=la, g2=g2, slab=slab: e.matmul(
                            pc[:, (z * 2 + g2) * nb2:(z * 2 + g2 + 1) * nb2], lhsT=slab[:, l * P:(l + 1) * P], rhs=xt_fn(g2, z, la),
                            start=(la == 0 and z == 0 and g2 == 0), stop=(la == 63), skip_group_check=True), reads=[slab.res, xt_res], writes=[pc.res])
        for g2 in range(2):
            S.op("act", lambda e, g2=g2: e.activation(out=self.hidK[:, g2, :, 0:nblk], in_=pc[:, g2 * nb2:(g2 + 1) * nb2].rearrange("p (a b) -> p a b", a=2),
                                                      func=AF.Silu, bias=self.pebias[:, 0:1], scale=1.0), reads=[pc.res, self.pebias.res], writes=[self.hidK.res])
            S.op("act", lambda e, g2=g2: e.activation(out=v_out_hid(g2), in_=pc[:, (2 + g2) * nb2:(3 + g2) * nb2].rearrange("p (a b) -> p a b", a=2),
                                                      func=AF.Silu, bias=self.pebias[:, 1:2], scale=1.0), reads=[pc.res, self.pebias.res], writes=[hid_res or self.hidV.res])
        for gp in range(2):
            ps = self.psum()
            for g2 in range(2):
                S.op("pe", lambda e, g2=g2, gp=gp, ps=ps: e.matmul(ps[:, 0:nblk], lhsT=self.W2K[:, g2, :], rhs=self.hidK[:, g2, gp, 0:nblk], start=(g2 == 0), stop=(g2 == 1)),
                     reads=[self.cres, self.hidK.res], writes=[ps.res])
            k_out(gp, ps)

    def nsa_attend(self, N, groups):
        raise NotImplementedError

    def nsa(self, it, N, sample):
        if sample:
            return self.nsa_sample(N)
        S = self.S
        cfg = self.cfg
        t0 = it * 512
        slot = it % 2
        hT, hy, act = self.hT, self.hy, self.act
        qT = act
        S.dma("pool", [(self.mk[:, :, :, :], self.mk_d[it])], writes=[self.mk.res])
        S.dma("pool", [(self.negmT[:, 0, 4, :], self.cmpbt_d[it])], writes=[self.negmT.res])
        stage = int(os.environ.get("NSA_STAGE", "9"))
        if stage < 1:
            return self.nsa_sample(N)
        self.nsa_project(N, False, t0, slot)
        if stage < 2:
            return self.nsa_sample(N)
        S.op("pool", lambda e: e.memset(self.hidV[:, :, :], 0.0), writes=[self.hidV.res])
        XT = self.XT

        def xt_fn(half, z, l):
            return XT[:, z * 2:z * 2 + 2, l:512:64]

        def k_out(gp, ps):
            S.op("act", lambda e: e.activation(out=self.KcT[:, gp, it * 8:it * 8 + 8], in_=ps[:, 0:8], func=AF.Copy), reads=[ps.res], writes=[self.KcT.res])
        self.nsa_compress(8, xt_fn, XT.res, k_out, lambda g2: self.hidV[:, g2:4:2, it * 8:it * 8 + 8])
        ps = self.psum()
        for g in range(4):
            S.op("pe", lambda e, g=g: e.matmul(ps[:, g * 64:(g + 1) * 64], lhsT=self.hidV[:, g, :], rhs=self.W2V[:, :], start=True, stop=True),
                 reads=[self.hidV.res, self.cres], writes=[ps.res])
        S.op("dve", lambda e: e.tensor_tensor(out=self.Vc32[:, :], in0=self.Vc32[:, :], in1=ps[:, 0:256], op=ALU.add),
             reads=[ps.res, self.Vc32.res], writes=[self.Vc32.res])
        S.op("act", lambda e: e.activation(out=self.Vc[:, :, 0:64], in_=self.Vc32[:, :].rearrange("p (g d) -> p g d", g=4), func=AF.Copy),
             reads=[self.Vc32.res], writes=[self.Vc.res])
        if stage < 3:
            return self.nsa_sample(N)
        tk1, tk2, tks = self.tk1, self.tk2, self.tks
        mk = self.mk
        for s_ in range(4):
            for g in range(4):
                half, gp = g % 2, g // 2
                ps = self.psum()
                for r in range(4):
                    ch = 4 * gp + r
                    S.op("pe", lambda e, r=r, ch=ch, ps=ps: e.matmul(ps[:, r * P:(r + 1) * P], lhsT=qT[:, 8 * half + ch, s_ * P:(s_ + 1) * P],
                                                                     rhs=self.KcT[:, gp, :], start=True, stop=True),
                         reads=[self.actres, self.KcT.res], writes=[ps.res])
                S.op("dve", lambda e, ps=ps: e.tensor_tensor(out=tk1[:, :, :], in0=ps[:, 0:512].rearrange("p (r n) -> p r n", r=4),
                                                             in1=mk[:, 0, s_, :].unsqueeze(1).to_broadcast([P, 4, P]), op=ALU.add),
                     reads=[ps.res, mk.res], writes=[tk1.res])
                for r in range(4):
                    S.op("act", lambda e, r=r: e.activation(out=tk1[:, r, :], in_=tk1[:, r, :], func=AF.Exp, accum_out=tks[:, r:r + 1]),
                         reads=[tk1.res], writes=[tk1.res, tks.res])
                S.op("dve", lambda e: e.tensor_scalar(out=tks[:, 4:8], in0=tks[:, 0:4], scalar1=1e-30, scalar2=None, op0=ALU.max), reads=[tks.res], writes=[tks.res])
                S.op("dve", lambda e: e.reciprocal(out=tks[:, 4:8], in_=tks[:, 4:8]), reads=[tks.res], writes=[tks.res])
                S.op("dve", lambda e: e.tensor_scalar(out=tk2[:, 0, :], in0=tk1[:, 0, :], scalar1=tks[:, 4:5], scalar2=None, op0=ALU.mult),
                     reads=[tk1.res, tks.res], writes=[tk2.res])
                for r in range(1, 4):
                    S.op("dve", lambda e, r=r: e.scalar_tensor_tensor(out=tk2[:, 0, :], in0=tk1[:, r, :], scalar=tks[:, 4 + r:5 + r], in1=tk2[:, 0, :], op0=ALU.mult, op1=ALU.add),
                         reads=[tk1.res, tks.res, tk2.res], writes=[tk2.res])
                S.op("dve", lambda e: e.tensor_tensor(out=tk2[:, 0, :], in0=tk2[:, 0, :], in1=mk[:, 1, s_, :], op=ALU.mult), reads=[tk2.res, mk.res], writes=[tk2.res])
                S.op("dve", lambda e: e.tensor_tensor(out=tk2[:, 0, :], in0=tk2[:, 0, :], in1=mk[:, 2, s_, :], op=ALU.add), reads=[tk2.res, mk.res], writes=[tk2.res])
                S.op("dve", lambda e: e.max(out=tks[:, 8:16], in_=tk2[:, 0, :]), reads=[tk2.res], writes=[tks.res])
                S.op("dve", lambda e: e.match_replace(out=tk2[:, 1, :], in_to_replace=tks[:, 8:16], in_values=tk2[:, 0, :], imm_value=-2.0),
                     reads=[tk2.res, tks.res], writes=[tk2.res])
                S.op("dve", lambda e: e.max(out=tks[:, 16:24], in_=tk2[:, 1, :]), reads=[tk2.res], writes=[tks.res])
                S.op("dve", lambda e: e.tensor_scalar(out=tks[:, 24:25], in0=tks[:, 23:24], scalar1=-0.5, scalar2=None, op0=ALU.max), reads=[tks.res], writes=[tks.res])
                S.op("dve", lambda e: e.tensor_scalar(out=self.negm[:, :], in0=tk2[:, 0, :], scalar1=tks[:, 24:25], scalar2=1.0, op0=ALU.is_ge, op1=ALU.subtract),
                     reads=[tk2.res, tks.res], writes=[self.negm.res])
                pst = self.psum()
                pv = pst.t[:].bitcast(BF16)
                S.op("pe", lambda e, pv=pv, pst=pst: e.transpose(pv[:, 0:P], self.negm[:, :], self.ident_bf[:, :]), reads=[self.negm.res, self.cres], writes=[pst.res])
                S.op("act", lambda e, pv=pv, pst=pst, g=g: e.activation(out=self.negmT[0:64, 0, g, s_ * P:(s_ + 1) * P], in_=pv[0:64, 0:P], func=AF.Copy),
                     reads=[pst.res], writes=[self.negmT.res])
                S.op("act", lambda e, pv=pv, pst=pst, g=g: e.activation(out=self.negmT[64:128, 1, g, s_ * P:(s_ + 1) * P], in_=pv[64:128, 0:P], func=AF.Copy),
                     reads=[pst.res], writes=[self.negmT.res])
        if stage < 4:
            return self.nsa_sample(N)
        sE = self.ws.next(("E64",), 4096, lambda w: e64_host())
        sC = self.ws.next(("caus",), 4096, lambda w: caus_host())
        o_tm = self.xin
        pT = self.tmpb
        npt = 0
        for h in range(16):
            g = h // 4
            ch, half = self.head_loc(h)
            gp = g // 2
            hs = slice(0, P)
            qh = qT[:, 8 * half + ch, 0:512]
            for br in range(3):
                tiles = []
                if br == 0:
                    tiles.append((self.KcT[hs, gp, :], self.KcT.res, self.Vc[:, g, 0:65], self.Vc.res,
                                  [(self.ident_bf[:, :], self.negmT[:, 0, 4, :])]))
                elif br == 1:
                    if h % 4 == 0:
                        nkt = 4 * it + 4
                        S.dma("sp", [(self.Vs[:, 0:nkt, :], self.vscr[g, :, 0:nkt, :])], reads=[self.vscr_res], writes=[self.Vs.res])
                    for kt in range(4 * it + 4):
                        b64, v = kt // 32, kt % 32
                        ms = [(sE[:, v * P:(v + 1) * P], self.negmT[:, b64, g, :])]
                        if kt >= 4 * it:
                            j = kt - 4 * it
                            ms.append((self.ident_bf[:, :], sC[:, j * 512:(j + 1) * 512]))
                        tiles.append((self.KTs[hs, gp, kt * P:(kt + 1) * P], self.KTs.res, self.Vs[:, kt, 0:65], self.Vs.res, ms))
                else:
                    if it > 0:
                        for j in range(4):
                            tiles.append((self.KTw[hs, 1 - slot, gp, j * P:(j + 1) * P], self.KTw.res, self.Vw[:, 1 - slot, j, g, 0:65], self.Vw.res,
                                          [(self.ident_bf[:, :], sC[:, (4 + j) * 512:(5 + j) * 512])]))
                    for j in range(4):
                        tiles.append((self.KTw[hs, slot, gp, j * P:(j + 1) * P], self.KTw.res, self.Vw[:, slot, j, g, 0:65], self.Vw.res,
                                      [(self.ident_bf[:, :], sC[:, j * 512:(j + 1) * 512])]))
                po = self.pacc[self.paidx % 2]
                self.paidx += 1
                nt_ = len(tiles)
                for ti, (kt_ap, kres, v_ap, vres, ms) in enumerate(tiles):
                    ps = self.psum()
                    S.op("pe", lambda e, ps=ps, kt_ap=kt_ap: e.matmul(ps[:, 0:512], lhsT=kt_ap, rhs=qh, start=True, stop=False),
                         reads=[kres, self.actres], writes=[ps.res])
                    for mi, (ml, mr) in enumerate(ms):
                        S.op("pe", lambda e, ps=ps, ml=ml, mr=mr, mi=mi, nm=len(ms): e.matmul(ps[:, 0:512], lhsT=ml, rhs=mr, start=False, stop=(mi == nm - 1)),
                             reads=[self.cres, self.negmT.res, sE.res, sC.res], writes=[ps.res])
                    pj = npt % 2
                    npt += 1
                    S.op("act", lambda e, ps=ps, pj=pj: e.activation(out=pT[:, pj, :], in_=ps[:, 0:512], func=AF.Exp), reads=[ps.res], writes=[self.tmpbres[pj]])
                    for qs in range(4):
                        S.op("pe", lambda e, qs=qs, pj=pj, v_ap=v_ap, ti=ti: e.matmul(po[:, qs * 65:(qs + 1) * 65], lhsT=pT[:, pj, qs * P:(qs + 1) * P], rhs=v_ap,
                                                                                     start=(ti == 0 and qs == 0), stop=(ti == nt_ - 1), skip_group_check=True),
                             reads=[self.tmpbres[pj], vres], writes=[po.res])
                f4 = self.f4
                pov = po[:, 0:260].rearrange("p (q c) -> p q c", q=4)
                S.op("dve", lambda e, pov=pov: e.tensor_scalar(out=f4[:, 0:4], in0=pov[:, :, 64], scalar1=1e-30, scalar2=None, op0=ALU.max), reads=[po.res], writes=[f4.res])
                S.op("dve", lambda e: e.reciprocal(out=f4[:, 0:4], in_=f4[:, 0:4]), reads=[f4.res], writes=[f4.res])
                S.op("dve", lambda e, br=br, h=h: e.tensor_tensor(out=f4[:, 4:8], in0=f4[:, 0:4], in1=self.gates[:, :, br * 16 + h], op=ALU.mult),
                     reads=[f4.res, self.gates.res], writes=[f4.res])
                for qs in range(4):
                    if br == 0:
                        S.op("dve", lambda e, qs=qs, pov=pov, h=h: e.tensor_scalar(out=o_tm[:, qs, h * 64:(h + 1) * 64], in0=pov[:, qs, 0:64], scalar1=f4[:, 4 + qs:5 + qs],
                                                                                 scalar2=None, op0=ALU.mult), reads=[po.res, f4.res], writes=[hy.res])
                    else:
                        S.op("dve", lambda e, qs=qs, pov=pov, h=h: e.scalar_tensor_tensor(out=o_tm[:, qs, h * 64:(h + 1) * 64], in0=pov[:, qs, 0:64], scalar=f4[:, 4 + qs:5 + qs],
                                                                                        in1=o_tm[:, qs, h * 64:(h + 1) * 64], op0=ALU.mult, op1=ALU.add),
                             reads=[po.res, f4.res, hy.res], writes=[hy.res])
        for k in range(KD):
            ps = self.psum()
            for qs in range(4):
                S.op("pe", lambda e, k=k, qs=qs, ps=ps: e.transpose(ps[:, qs * P:(qs + 1) * P], o_tm[:, qs, k * P:(k + 1) * P], self.ident[:, :]),
                     reads=[hy.res, self.cres], writes=[ps.res])
            S.op("act", lambda e, k=k, ps=ps: e.activation(out=act[:, k, 0:512], in_=ps[:, 0:512], func=AF.Copy), reads=[ps.res], writes=[self.actres])
        self.mm_fm(("nwo",), lambda w: w["nsa_w_out"][0], KD, KD, lambda k: act[:, k, 0:N], self.actres, N, self.evac_y(N))

    def samp_page(self, b, v_src, v_res, kt_buf, masks, first, last, po, psm, pg_ap=None, pg_res=None):
        S = self.S
        act = self.act
        if kt_buf is None:
            kt_buf = self.KTp[self.kti % 2]
            for gp in range(2):
                ps = self.psum()
                S.op("pe", lambda e, gp=gp, ps=ps: e.transpose(ps[:, 0:P], pg_ap[:, gp * P:(gp + 1) * P], self.ident[:, :]), reads=[pg_res, self.cres], writes=[ps.res])
                S.op("act", lambda e, gp=gp, ps=ps: e.activation(out=kt_buf[:, gp, :], in_=ps[:, 0:P], func=AF.Copy), reads=[ps.res], writes=[kt_buf.res])
        Vp = self.Vp[0]
        pT = self.pTs[self.kti % 2]
        self.kti += 1
        vv = v_src.rearrange("p (g d) -> p g d", g=4)
        S.op("act", lambda e: e.activation(out=Vp[:, :, 0, 0:64], in_=vv, func=AF.Copy), reads=[v_res], writes=[Vp.res])
        S.op("act", lambda e: e.activation(out=Vp[:, :, 1, 64:128], in_=vv, func=AF.Copy), reads=[v_res], writes=[Vp.res])
        ps = self.psum()
        for g in range(4):
            half, gp = g % 2, g // 2
            q_ap = act[:, 8 * half + 4 * gp:8 * half + 4 * gp + 4, b]
            S.op("pe", lambda e, g=g, gp=gp, q_ap=q_ap: e.matmul(ps[:, g * 4:(g + 1) * 4], lhsT=kt_buf[:, gp, :], rhs=q_ap, start=True, stop=(len(masks) == 0)),
                 reads=[kt_buf.res, self.actres], writes=[ps.res])
            for mi, (ml, mr, mres) in enumerate(masks):
                S.op("pe", lambda e, g=g, ml=ml, mr=mr, mi=mi: e.matmul(ps[:, g * 4:(g + 1) * 4], lhsT=ml, rhs=mr(g), start=False, stop=(mi == len(masks) - 1)),
                     reads=[self.cres] + mres, writes=[ps.res])
        S.op("act", lambda e: e.activation(out=pT[:, :], in_=ps[:, 0:16], func=AF.Exp), reads=[ps.res], writes=[pT.res])
        for g in range(4):
            for var in range(2):
                rhs = pT[:, g * 4 + var:g * 4 + 4:2]
                S.op("pe", lambda e, g=g, var=var, rhs=rhs: e.matmul(po[:, 2 * g:2 * g + 2], lhsT=Vp[:, g, var, :], rhs=rhs,
                                                                     start=(first and g == 0 and var == 0), stop=last, skip_group_check=True),
                     reads=[Vp.res, pT.res], writes=[po.res])
                S.op("pe", lambda e, g=g, var=var, rhs=rhs: e.matmul(psm[:, 2 * g:2 * g + 2], lhsT=self.OnesV[:, var, :], rhs=rhs,
                                                                     start=(first and g == 0 and var == 0), stop=last, skip_group_check=True),
                     reads=[self.cres, pT.res], writes=[psm.res])
        return pT

    def nsa_sample(self, N):
        S = self.S
        cfg = self.cfg
        NS, NPG = cfg.NSAMP, self.NPG
        hT, hy, act = self.hT, self.hy, self.act
        stage = int(os.environ.get("NSA_SSTAGE", "9"))
        if stage < 1 or 1 not in cfg.layers or not hasattr(self, "cmp_s"):
            for k in range(KD):
                S.op("act", lambda e, k=k: e.activation(out=self.yT[:, k, 0:N], in_=self.xT[:, k, 0:N], func=AF.Copy),
                     reads=[self.xT.res], writes=[self.hy.res])
            return
        S.op("pool", lambda e: e.memset(self.PGn[:, :, :], 0.0), reads=[self.hy.res], writes=[self.pgn_res])
        self.nsa_project(N, True, 0, 0)
        tf = self.tmpf
        pgi = 0
        res_o, res_s = self.res_o, self.res_s

        def fin(b, br, po, psm):
            S.op("act", lambda e: e.activation(out=res_o[:, br, b, :], in_=po[:, 0:8], func=AF.Copy), reads=[po.res], writes=[res_o.res])
            S.op("act", lambda e: e.activation(out=res_s[:, br, b, :], in_=psm[:, 0:8], func=AF.Copy), reads=[psm.res], writes=[res_s.res])
        newm = lambda b: [(self.ident_bf[:, :], (lambda g, b=b: self.newmask[:, b, :]), [])]
        XTs = self.XTs
        for b in range(NS):
            for grp in range(NPG // 8):
                for pi in range(8):
                    page = grp * 8 + pi
                    j = pgi % 2
                    pgi += 1
                    col = b * NPG + page
                    S.dma_gather(tf[:, j, :], self.cache_c[:, :], self.idxi[:, col:col + 1], reads=[self.idxi.res], writes=[self.tmpres[j]])
                    for c in range(4):
                        ps = self.psum()
                        S.op("pe", lambda e, c=c, ps=ps, j=j: e.transpose(ps[:, 0:P], tf[:, j, c * P:(c + 1) * P], self.ident[:, :]),
                             reads=[self.tmpres[j], self.cres], writes=[ps.res])
                        S.op("act", lambda e, c=c, ps=ps, pi=pi: e.activation(out=XTs[:, c, pi * P:(pi + 1) * P], in_=ps[:, 0:P], func=AF.Copy),
                             reads=[ps.res], writes=[XTs.res])

                def k_out(gp, ps, grp=grp):
                    S.op("act", lambda e: e.activation(out=self.KcS[:, gp, grp * 16:(grp + 1) * 16], in_=ps[:, 0:16], func=AF.Copy), reads=[ps.res], writes=[self.KcS.res])
                self.nsa_compress(16, lambda half, z, l: XTs[:, z * 2:z * 2 + 2, l:1024:64], XTs.res, k_out,
                                  lambda g2, grp=grp: self.hidVs[:, g2:4:2, grp * 16:(grp + 1) * 16], hid_res=self.hidVs.res)
            psv = self.psum()
            for g in range(4):
                S.op("pe", lambda e, g=g: e.matmul(psv[:, g * 64:(g + 1) * 64], lhsT=self.hidVs[:, g, :], rhs=self.W2V[:, :], start=True, stop=True),
                     reads=[self.hidVs.res, self.cres], writes=[psv.res])
            po, psm = self.pacc[0], self.pacc[1]
            pT = self.samp_page(b, psv[:, 0:256], psv.res, self.KcS, [], True, True, po, psm)
            fin(b, 0, po, psm)
            psr = self.psum()
            S.op("pe", lambda e: e.matmul(psr[:, 0:16], lhsT=self.ones_bf[:, 0:P], rhs=pT[:, :], start=True, stop=True), reads=[self.cres, pT.res], writes=[psr.res])
            sw = self.sw
            S.op("dve", lambda e: e.tensor_scalar(out=sw[:, 0:16], in0=psr[:, 0:16], scalar1=1e-30, scalar2=None, op0=ALU.max), reads=[psr.res], writes=[sw.res])
            S.op("dve", lambda e: e.reciprocal(out=sw[:, 0:16], in_=sw[:, 0:16]), reads=[sw.res], writes=[sw.res])
            S.op("dve", lambda e: e.tensor_tensor(out=sw[:, 16:32], in0=sw[:, 0:16], in1=pT[:, :], op=ALU.mult), reads=[sw.res, pT.res], writes=[sw.res])
            S.op("dve", lambda e, b=b: e.tensor_reduce(out=self.impT[:, b * 4:(b + 1) * 4], in_=sw[:, 16:32].rearrange("p (g r) -> p g r", g=4),
                                                       axis=mybir.AxisListType.X, op=ALU.add), reads=[sw.res], writes=[self.impT.res])
        R16 = 4 * NS
        pst = self.psum()
        S.op("pe", lambda e: e.transpose(pst[0:R16, 0:P], self.impT[:, :], self.ident[:, :]), reads=[self.impT.res, self.cres], writes=[pst.res])
        sc, sk = self.sc16, self.sk16
        S.op("act", lambda e: e.activation(out=sc[:, 0, :], in_=pst[0:R16, 0:P], func=AF.Copy), reads=[pst.res], writes=[sc.res])
        S.op("dve", lambda e: e.memset(sc[:, 0, 0:1], 1.0e4), reads=[sc.res], writes=[sc.res])
        S.op("dve", lambda e: e.memset(sc[:, 0, P - 1:P], 1.0e4), reads=[sc.res], writes=[sc.res])
        S.op("dve", lambda e: e.max(out=sk[:, 0:8], in_=sc[:, 0, :]), reads=[sc.res], writes=[sk.res])
        S.op("dve", lambda e: e.match_replace(out=sc[:, 1, :], in_to_replace=sk[:, 0:8], in_values=sc[:, 0, :], imm_value=-2.0), reads=[sc.res, sk.res], writes=[sc.res])
        S.op("dve", lambda e: e.max(out=sk[:, 8:16], in_=sc[:, 1, :]), reads=[sc.res], writes=[sk.res])
        S.op("dve", lambda e: e.tensor_scalar(out=self.ng16[:, :], in0=sc[:, 0, :], scalar1=sk[:, 14:15], scalar2=1.0, op0=ALU.is_ge, op1=ALU.subtract),
             reads=[sc.res, sk.res], writes=[self.ng16.res])
        pst2 = self.psum()
        pv = pst2.t[:].bitcast(BF16)
        S.op("pe", lambda e: e.transpose(pv[:, 0:R16], self.ng16[:, :], self.ident_bf[0:R16, 0:R16]), reads=[self.ng16.res, self.cres], writes=[pst2.res])
        S.op("act", lambda e: e.activation(out=self.negS[0:64, 0, :], in_=pv[0:64, 0:R16], func=AF.Copy), reads=[pst2.res], writes=[self.negS.res])
        S.op("act", lambda e: e.activation(out=self.negS[64:128, 1, :], in_=pv[64:128, 0:R16], func=AF.Copy), reads=[pst2.res], writes=[self.negS.res])
        sE = self.ws.next(("E64",), 4096, lambda w: e64_host())
        for b in range(NS):
            po, psm = self.pacc[0], self.pacc[1]
            for page in range(NPG):
                j = pgi % 2
                pgi += 1
                col = b * NPG + page
                S.dma_gather(tf[:, j, :], self.cache_s[:, :], self.idxi[:, col:col + 1], reads=[self.idxi.res], writes=[self.tmpres[j]])
                b64, v = page // 32, page % 32
                ms = [(sE[:, v * P:(v + 1) * P], (lambda g, b=b, b64=b64: self.negS[:, b64, b * 4 + g:b * 4 + g + 1].to_broadcast([P, 4])), [sE.res, self.negS.res])]
                self.samp_page(b, tf[:, j, 256:512], self.tmpres[j], None, ms, page == 0, False, po, psm, pg_ap=tf[:, j, :], pg_res=self.tmpres[j])
            self.samp_page(b, self.PGn[:, 0, 256:512], self.PGn.res, None, newm(b), False, True, po, psm, pg_ap=self.PGn[:, 0, :], pg_res=self.PGn.res)
            fin(b, 1, po, psm)
        for b in range(NS):
            po, psm = self.pacc[0], self.pacc[1]
            for page in range(4):
                j = pgi % 2
                pgi += 1
                S.dma("pool", [(tf[:, j, :], self.cache_w[b, page * P:(page + 1) * P, :])], writes=[self.tmpres[j]])
                self.samp_page(b, tf[:, j, 256:512], self.tmpres[j], None, [], page == 0, False, po, psm, pg_ap=tf[:, j, :], pg_res=self.tmpres[j])
            self.samp_page(b, self.PGn[:, 1, 256:512], self.PGn.res, None, newm(b), False, True, po, psm, pg_ap=self.PGn[:, 1, :], pg_res=self.PGn.res)
            fin(b, 2, po, psm)
            S.dma("sp", [(self.win_s[b, 0:511, :], self.cache_w[b, 1:512, :])], reads=[self.wcopy_res])
        S.op("dve", lambda e: e.tensor_scalar(out=res_s[:, :, :, :], in0=res_s[:, :, :, :], scalar1=1e-30, scalar2=None, op0=ALU.max), reads=[res_s.res], writes=[res_s.res])
        S.op("dve", lambda e: e.reciprocal(out=res_s[:, :, :, :], in_=res_s[:, :, :, :]), reads=[res_s.res], writes=[res_s.res])
        S.op("dve", lambda e: e.tensor_tensor(out=res_o[:, :, :, :], in0=res_o[:, :, :, :], in1=res_s[:, :, :, :], op=ALU.mult), reads=[res_s.res, res_o.res], writes=[res_o.res])
        gv = self.gates[0:NS, 0, :].rearrange("p (a c h) -> p a c h", a=3, c=8)
        sw = self.sw
        for b in range(NS):
            pg_ = self.psum()
            for hf in range(2):
                S.op("pe", lambda e, b=b, hf=hf: e.matmul(pg_[:, 0:24], lhsT=self.Sel[0:NS, b, hf, :], rhs=gv[:, :, :, hf], start=(hf == 0), stop=(hf == 1)),
                     reads=[self.cres, self.gates.res], writes=[pg_.res])
            S.op("dve", lambda e, b=b: e.tensor_tensor(out=sw[:, 32:56].rearrange("p (a c) -> p a c", a=3), in0=res_o[:, :, b, :],
                                                       in1=pg_[:, 0:24].rearrange("p (a c) -> p a c", a=3), op=ALU.mult),
                 reads=[res_o.res, pg_.res], writes=[sw.res])
            S.op("dve", lambda e: e.tensor_tensor(out=sw[:, 32:40], in0=sw[:, 32:40], in1=sw[:, 40:48], op=ALU.add), reads=[sw.res], writes=[sw.res])
            S.op("dve", lambda e: e.tensor_tensor(out=sw[:, 32:40], in0=sw[:, 32:40], in1=sw[:, 48:56], op=ALU.add), reads=[sw.res], writes=[sw.res])
            S.op("act", lambda e, b=b: e.activation(out=act[:, 0:8, b], in_=sw[:, 32:40], func=AF.Copy), reads=[sw.res], writes=[self.actres])
        self.mm_fm(("nwo",), lambda w: w["nsa_w_out"][0], KD, KD, lambda k: act[:, k, 0:N], self.actres, N, self.evac_y(N))

    def s5_setup(self):
        S = self.S
        cfg = self.cfg
        NS = cfg.NSAMP
        TC = 16
        self.TC = TC

        def st_layout(a):
            return np.asarray(a).reshape(32, 2, 64).transpose(1, 2, 0).reshape(P, 32)
        d_lr = self.din("s5_lr", [P, 32], lambda I, c: st_layout(I["ssm_lambda_re"][0]))
        d_li = self.din("s5_li", [P, 32], lambda I, c: st_layout(I["ssm_lambda_im"][0]))
        d_ls = self.din("s5_ls", [P, 32], lambda I, c: st_layout(np.broadcast_to(I["ssm_log_step"][0][:, None], (64, 64))))
        W = self.sb("s5w", [P, 24, 32], F32)
        self.s5res = Res("s5const")
        W.res = self.s5res
        S.dma("sp", [(W[:, 0, :], d_lr[:, :]), (W[:, 1, :], d_li[:, :]), (W[:, 2, :], d_ls[:, :])], writes=[W.res])
        LR, LI, DT, A_, TH, X, X2, SN, CS, T1, T2, MAG, ABR, ABI, DEN, FR, FI = range(17)
        LR, LI, LS = 0, 1, 2
        DT, A_, TH, X, X2, SN, CS, T1, T2, MAG, ABR, ABI, DEN, FR, FI = range(3, 18)

        def dve(fn):
            S.op("dve", fn, reads=[W.res], writes=[W.res])

        def tt(o, a, b, op):
            dve(lambda e: e.tensor_tensor(out=W[:, o, :], in0=W[:, a, :], in1=W[:, b, :], op=op))

        def ts(o, a, s1, op0, s2=None, op1=None):
            if op1 is None:
                dve(lambda e: e.tensor_scalar(out=W[:, o, :], in0=W[:, a, :], scalar1=s1, scalar2=None, op0=op0))
            else:
                dve(lambda e: e.tensor_scalar(out=W[:, o, :], in0=W[:, a, :], scalar1=s1, scalar2=s2, op0=op0, op1=op1))
        S.op("act", lambda e: e.activation(out=W[:, DT, :], in_=W[:, LS, :], func=AF.Exp), reads=[W.res], writes=[W.res])
        tt(A_, LR, DT, ALU.mult)
        tt(TH, LI, DT, ALU.mult)
        ts(MAG, A_, 1.0 / 5, ALU.mult, 1.0, ALU.add)
        for dd in (4.0, 3.0, 2.0, 1.0):
            tt(MAG, MAG, A_, ALU.mult)
            ts(MAG, MAG, 1.0 / dd, ALU.mult, 1.0, ALU.add)
        ts(X, TH, 1.0 / 16, ALU.mult)
        tt(X2, X, X, ALU.mult)
        ts(SN, X2, -1.0 / 156, ALU.mult, 1.0, ALU.add)
        for dd in (110.0, 72.0, 42.0, 20.0, 6.0):
            tt(SN, SN, X2, ALU.mult)
            ts(SN, SN, -1.0 / dd, ALU.mult, 1.0, ALU.add)
        tt(SN, SN, X, ALU.mult)
        ts(CS, X2, -1.0 / 182, ALU.mult, 1.0, ALU.add)
        for dd in (132.0, 90.0, 56.0, 30.0, 12.0, 2.0):
            tt(CS, CS, X2, ALU.mult)
            ts(CS, CS, -1.0 / dd, ALU.mult, 1.0, ALU.add)

        def double(c, s_):
            tt(T1, s_, c, ALU.mult)
            tt(T2, s_, s_, ALU.mult)
            ts(s_, T1, 2.0, ALU.mult)
            ts(c, T2, -2.0, ALU.mult, 1.0, ALU.add)
        for _ in range(4):
            double(CS, SN)
        tt(ABR, MAG, CS, ALU.mult)
        tt(ABI, MAG, SN, ALU.mult)
        tt(T1, LR, LR, ALU.mult)
        tt(T2, LI, LI, ALU.mult)
        tt(DEN, T1, T2, ALU.add)
        dve(lambda e: e.reciprocal(out=W[:, DEN, :], in_=W[:, DEN, :]))
        ts(T1, ABR, -1.0, ALU.add)
        tt(FR, T1, LR, ALU.mult)
        tt(T2, ABI, LI, ALU.mult)
        tt(FR, FR, T2, ALU.add)
        tt(FR, FR, DEN, ALU.mult)
        tt(FI, ABI, LR, ALU.mult)
        tt(T2, T1, LI, ALU.mult)
        tt(FI, FI, T2, ALU.subtract)
        tt(FI, FI, DEN, ALU.mult)
        self.s5W = W
        self.s5i = dict(MAG=MAG, CS=CS, SN=SN, ABR=ABR, ABI=ABI, FR=FR, FI=FI)
        self.cosT = self.sb("cosT", [P, 32, TC], F32)
        self.sinT = self.sb("sinT", [P, 32, TC], F32)
        self.rhoT = self.sb("rhoT", [P, 32, TC], F32)
        for bb in (self.cosT, self.sinT, self.rhoT):
            bb.res = self.s5res
        cT, sT, rT = self.cosT, self.sinT, self.rhoT
        dve(lambda e: e.memset(cT[:, :, 0:1], 1.0))
        dve(lambda e: e.memset(sT[:, :, 0:1], 0.0))
        dve(lambda e: e.tensor_copy(out=cT[:, :, 1], in_=W[:, CS, :]))
        dve(lambda e: e.tensor_copy(out=sT[:, :, 1], in_=W[:, SN, :]))
        for t in range(TC):
            dve(lambda e, t=t: e.tensor_copy(out=rT[:, :, t], in_=W[:, MAG, :]))
        WC, WS_, T3, T4 = 18, 19, 20, 21
        dve(lambda e: e.tensor_copy(out=W[:, WC, :], in_=W[:, CS, :]))
        dve(lambda e: e.tensor_copy(out=W[:, WS_, :], in_=W[:, SN, :]))
        tmpc = Buf.__new__(Buf)
        tmpc.t = self.act.t[:, 8:10, :].rearrange("p k t -> p (k t)").bitcast(F32).rearrange("p (c t) -> p c t", t=TC)
        tmpc.res = self.s5res
        n = 2
        while n < TC:
            double(WC, WS_)
            wc = W[:, WC, :].unsqueeze(2).to_broadcast([P, 32, n])
            wsn = W[:, WS_, :].unsqueeze(2).to_broadcast([P, 32, n])
            dve(lambda e, n=n, wc=wc: e.tensor_tensor(out=cT[:, :, n:2 * n], in0=cT[:, :, 0:n], in1=wc, op=ALU.mult))
            dve(lambda e, n=n, wsn=wsn: e.tensor_tensor(out=tmpc[:, :, 0:n], in0=sT[:, :, 0:n], in1=wsn, op=ALU.mult))
            dve(lambda e, n=n, wsn=wsn: e.tensor_tensor(out=sT[:, :, n:2 * n], in0=cT[:, :, 0:n], in1=wsn, op=ALU.mult))
            dve(lambda e, n=n: e.tensor_tensor(out=cT[:, :, n:2 * n], in0=cT[:, :, n:2 * n], in1=tmpc[:, :, 0:n], op=ALU.subtract))
            dve(lambda e, n=n, wc=wc: e.tensor_tensor(out=tmpc[:, :, 0:n], in0=sT[:, :, 0:n], in1=wc, op=ALU.mult))
            dve(lambda e, n=n: e.tensor_tensor(out=sT[:, :, n:2 * n], in0=sT[:, :, n:2 * n], in1=tmpc[:, :, 0:n], op=ALU.add))
            n *= 2
        self.s5tmp = tmpc
        self.s5x = self.sb("s5x", [P, 32, 2], F32)
        S.op("dve", lambda e: e.memset(self.s5x[:, :, :], 0.0), writes=[self.s5x.res])
        self.s5xs = self.sb("s5xs", [P, 32, NS, 2], F32)
        d_st = self.din("s5_st", [P, 32, NS, 2], lambda I, c: I["state_ssm"][0][c * NS:(c + 1) * NS].reshape(NS, 32, 2, 64, 2).transpose(2, 3, 1, 0, 4).reshape(P, 32, NS, 2))
        S.dma("sp", [(self.s5xs[:, :, :, :], d_st[:, :, :, :])], writes=[self.s5xs.res])
        self.s5_dv = None
        self.ssm_p = self.dout("ssm_p", [P, 32, 2])
        self.ssm_s = self.dout("ssm_s", [P, 32, NS, 2])
        self.ssm_anchor = Res("ssm_out")
        S.out_anchors.append(self.ssm_anchor)
        self.s5work = Buf.__new__(Buf)
        self.s5work.t = self.act.t[:, 8:20, :].rearrange("p k t -> p (k t)").bitcast(F32).rearrange("p (r c t) -> p r c t", r=6, c=32)
        self.s5work.res = self.actres
        self.s5xb = Buf.__new__(Buf)
        self.s5xb.t = self.act.t[:, 20:22, :].rearrange("p k t -> p (k t)").rearrange("p (z c t) -> p z c t", z=2, c=32)
        self.s5xb.res = self.actres
        self.s5zi = self.sb("s5zi", [P, 32, 2], F32)
        S.op("dve", lambda e: e.memset(self.rhoT[:, :, 0:1], 0.0), reads=[self.s5res], writes=[self.s5res])
        self.s5wkres = Res("s5wk")
        self.s5xbres = Res("s5xb")
        self.s5xb0 = self.act.t[:, 20:22, :].rearrange("p k t -> p (k t)").rearrange("p (c t) -> p c t", c=32)
        self.s5xb1 = self.sb("s5xb1", [P, 32, 32], BF16)

    def s5_prompt(self, it, N, sBr, sBi, sCr, sCi):
        S = self.S
        TC = self.TC
        hT, hy, act = self.hT, self.hy, self.act
        W, ix, wk = self.s5W, self.s5i, self.s5work
        wkres, xbres = self.s5wkres, self.s5xbres
        xb = [self.s5xb0, self.s5xb1.t]
        gT = act
        S.op("dve", lambda e: e.memset(self.cbar[:, 1:2], 0.0), reads=[self.actres, self.s5xb1.res], writes=[self.actres, wkres, xbres, self.cbar.res])
        banksets = [[self.pbanks[0], self.pbanks[1], self.pbanks[2], self.pbanks[3]],
                    [self.pbanks[4], self.pbanks[5], self.pacc[0], self.pacc[1]]]
        cT, sT, rT = self.cosT, self.sinT, self.rhoT
        c1, s1, mag = W[:, ix["CS"], :], W[:, ix["SN"], :], W[:, ix["MAG"], :]
        xp_, zi_ = self.s5x, self.s5zi
        R = lambda j: wk[:, j, :, :]
        flat = lambda j: wk[:, j, :, :].rearrange("p c t -> p (c t)")
        nsc = N // 32

        def emitB(sc):
            bs = banksets[sc % 2]
            t0 = sc * 32
            for z, slab in ((0, sBr), (1, sBi)):
                for c in range(32):
                    ps = bs[z * 2 + c // 16]
                    S.op("pe", lambda e, c=c, ps=ps, slab=slab: e.matmul(ps[:, (c % 16) * 32:(c % 16 + 1) * 32], lhsT=slab[:, c * P:(c + 1) * P],
                                                                         rhs=hT[:, c // 4, t0:t0 + 32], start=True, stop=True),
                         reads=[slab.res, hy.res], writes=[ps.res])
            return bs

        def dv(fn, extra=()):
            S.op("dve", fn, reads=[wkres, self.s5res] + list(extra), writes=[wkres])

        def mul(o, a, b):
            dv(lambda e: e.tensor_tensor(out=o, in0=a, in1=b, op=ALU.mult))

        def emitDVE(sc, bs):
            for sub in range(2):
                for hb in range(2):
                    Rh = lambda j, hb=hb: wk[:, j, hb * 16:(hb + 1) * 16, :]
                    PSv = lambda z, hb=hb, sub=sub: bs[z * 2 + hb][:, 0:512].rearrange("p (c t) -> p c t", t=32)[:, :, sub * 16:(sub + 1) * 16]
                    frh = W[:, ix["FR"], hb * 16:(hb + 1) * 16].unsqueeze(2).to_broadcast([P, 16, TC])
                    fih = W[:, ix["FI"], hb * 16:(hb + 1) * 16].unsqueeze(2).to_broadcast([P, 16, TC])
                    ex = [bs[hb].res, bs[2 + hb].res]
                    dv(lambda e, Rh=Rh, PSv=PSv, frh=frh: e.tensor_tensor(out=Rh(4), in0=PSv(0), in1=frh, op=ALU.mult), ex)
                    dv(lambda e, Rh=Rh, PSv=PSv, fih=fih: e.tensor_tensor(out=Rh(5), in0=PSv(1), in1=fih, op=ALU.mult), ex)
                    dv(lambda e, Rh=Rh: e.tensor_tensor(out=Rh(0), in0=Rh(4), in1=Rh(5), op=ALU.subtract))
                    dv(lambda e, Rh=Rh, PSv=PSv, frh=frh: e.tensor_tensor(out=Rh(4), in0=PSv(1), in1=frh, op=ALU.mult), ex)
                    dv(lambda e, Rh=Rh, PSv=PSv, fih=fih: e.tensor_tensor(out=Rh(5), in0=PSv(0), in1=fih, op=ALU.mult), ex)
                    dv(lambda e, Rh=Rh: e.tensor_tensor(out=Rh(1), in0=Rh(4), in1=Rh(5), op=ALU.add))
                mul(R(4), R(0), cT[:, :, :])
                mul(R(5), R(1), sT[:, :, :])
                dv(lambda e: e.tensor_tensor(out=R(2), in0=R(4), in1=R(5), op=ALU.add))
                mul(R(4), R(1), cT[:, :, :])
                mul(R(5), R(0), sT[:, :, :])
                dv(lambda e: e.tensor_tensor(out=R(3), in0=R(4), in1=R(5), op=ALU.subtract))
                def dz(fn):
                    S.op("dve", fn, reads=[xp_.res, self.s5res, zi_.res, wkres], writes=[zi_.res, wkres])
                dz(lambda e: e.tensor_tensor(out=wk[:, 4, :, 0], in0=xp_[:, :, 0], in1=c1, op=ALU.mult))
                dz(lambda e: e.tensor_tensor(out=wk[:, 4, :, 1], in0=xp_[:, :, 1], in1=s1, op=ALU.mult))
                dz(lambda e: e.tensor_tensor(out=zi_[:, :, 0], in0=wk[:, 4, :, 0], in1=wk[:, 4, :, 1], op=ALU.subtract))
                dz(lambda e: e.tensor_tensor(out=wk[:, 4, :, 2], in0=xp_[:, :, 1], in1=c1, op=ALU.mult))
                dz(lambda e: e.tensor_tensor(out=wk[:, 4, :, 3], in0=xp_[:, :, 0], in1=s1, op=ALU.mult))
                dz(lambda e: e.tensor_tensor(out=zi_[:, :, 1], in0=wk[:, 4, :, 2], in1=wk[:, 4, :, 3], op=ALU.add))
                dz(lambda e: e.tensor_tensor(out=wk[:, 4, :, 4], in0=zi_[:, :, 0], in1=mag, op=ALU.mult))
                dz(lambda e: e.tensor_tensor(out=wk[:, 4, :, 5], in0=zi_[:, :, 1], in1=mag, op=ALU.mult))
                dz(lambda e: e.tensor_tensor(out=wk[:, 2, :, 0], in0=wk[:, 2, :, 0], in1=wk[:, 4, :, 4], op=ALU.add))
                dz(lambda e: e.tensor_tensor(out=wk[:, 3, :, 0], in0=wk[:, 3, :, 0], in1=wk[:, 4, :, 5], op=ALU.add))
                rflat = rT[:, :, :].rearrange("p c t -> p (c t)")
                dv(lambda e: e.tensor_tensor_scan(out=flat(0), data0=rflat, data1=flat(2), initial=0.0, op0=ALU.mult, op1=ALU.add))
                dv(lambda e: e.tensor_tensor_scan(out=flat(1), data0=rflat, data1=flat(3), initial=0.0, op0=ALU.mult, op1=ALU.add))
                mul(R(4), R(0), cT[:, :, :])
                mul(R(5), R(1), sT[:, :, :])
                dv(lambda e: e.tensor_tensor(out=R(2), in0=R(4), in1=R(5), op=ALU.subtract))
                mul(R(4), R(0), sT[:, :, :])
                mul(R(5), R(1), cT[:, :, :])
                dv(lambda e: e.tensor_tensor(out=R(3), in0=R(4), in1=R(5), op=ALU.add))
                S.op("dve", lambda e: e.tensor_copy(out=xp_[:, :, 0], in_=wk[:, 2, :, TC - 1]), reads=[wkres], writes=[xp_.res])
                S.op("dve", lambda e: e.tensor_copy(out=xp_[:, :, 1], in_=wk[:, 3, :, TC - 1]), reads=[wkres], writes=[xp_.res])
                S.op("act", lambda e, sub=sub: e.activation(out=xb[0][:, :, sub * 16:(sub + 1) * 16], in_=R(2), func=AF.Copy), reads=[wkres], writes=[xbres])
                S.op("act", lambda e, sub=sub: e.activation(out=xb[1][:, :, sub * 16:(sub + 1) * 16], in_=R(3), func=AF.Copy, scale=-1.0), reads=[wkres], writes=[xbres])

        def emitC(sc, bs):
            py = bs[0]
            t0 = sc * 32
            for k in range(KD):
                n_ = 0
                for j in range(4):
                    c = 4 * k + j
                    for z, slab in ((0, sCr), (1, sCi)):
                        S.op("pe", lambda e, k=k, c=c, z=z, slab=slab, n_=n_: e.matmul(py[:, k * 32:(k + 1) * 32], lhsT=slab[:, c * P:(c + 1) * P],
                                                                                       rhs=xb[z][:, c, :], start=(n_ == 0), stop=(n_ == 7)),
                             reads=[slab.res, xbres], writes=[py.res])
                        n_ += 1
            tf = self.tmpf
            S.op("dve", lambda e: e.tensor_tensor(out=tf[:, 0, 0:KD * 32].rearrange("p (k t) -> p k t", t=32), in0=hT[:, :, t0:t0 + 32],
                                                  in1=self.s5d[:, :].unsqueeze(2).to_broadcast([P, KD, 32]), op=ALU.mult),
                 reads=[hy.res, self.cres], writes=[self.tmpres[0]])
            S.op("dve", lambda e: e.tensor_tensor(out=tf[:, 0, 0:KD * 32], in0=tf[:, 0, 0:KD * 32], in1=py[:, 0:KD * 32], op=ALU.add),
                 reads=[self.tmpres[0], py.res], writes=[self.tmpres[0]])
            S.op("act", lambda e: e.activation(out=gT[:, 0:KD, t0:t0 + 32], in_=tf[:, 0, 0:KD * 32].rearrange("p (k t) -> p k t", t=32), func=AF.Gelu),
                 reads=[self.tmpres[0]], writes=[self.actres])

        bs = emitB(0)
        for sc in range(nsc):
            bs_next = emitB(sc + 1) if sc + 1 < nsc else None
            emitDVE(sc, bs)
            emitC(sc, bs)
            bs = bs_next
        S.op("dve", lambda e: e.memset(self.cbar[:, 1:2], 0.0), reads=[wkres, xbres], writes=[self.actres, wkres, xbres, self.cbar.res])

    def s5(self, it, N, sample):
        S = self.S
        cfg = self.cfg
        NS = cfg.NSAMP
        TC = self.TC
        hT, hy, act = self.hT, self.hy, self.act
        W = self.s5W
        ix = self.s5i
        wk = self.s5work

        def padB(w, key):
            b = np.asarray(w[key][0])
            out = np.zeros((P, 32, P), np.float32)
            for c in range(32):
                for g2 in range(2):
                    g = 2 * c + g2
                    r0 = (c % 4) * 32 + g2 * 16
                    out[r0:r0 + 16, c, g2 * 64:(g2 + 1) * 64] = b[g].T
            return out.reshape(P, 32 * P)

        def padC(w, key):
            cm = np.asarray(w[key][0])
            out = np.zeros((P, 32, P), np.float32)
            for c in range(32):
                for g2 in range(2):
                    g = 2 * c + g2
                    f0 = (c % 4) * 32 + g2 * 16
                    out[g2 * 64:(g2 + 1) * 64, c, f0:f0 + 16] = cm[g].T
            return out.reshape(P, 32 * P)
        sBr = self.ws.next(("s5", "br"), 4096, lambda w: padB(w, "ssm_b_re"))
        sBi = self.ws.next(("s5", "bi"), 4096, lambda w: padB(w, "ssm_b_im"))
        sCr = self.ws.next(("s5", "cr"), 4096, lambda w: padC(w, "ssm_c_re"))
        sCi = self.ws.next(("s5", "ci"), 4096, lambda w: padC(w, "ssm_c_im"))
        nch = 1 if sample else 0
        T = N if sample else TC
        gT = act
        if not sample:
            self.s5_prompt(it, N, sBr, sBi, sCr, sCi)
        fr = W[:, ix["FR"], :].unsqueeze(2).to_broadcast([P, 32, T])
        fi = W[:, ix["FI"], :].unsqueeze(2).to_broadcast([P, 32, T])
        for ch in range(nch):
            t0 = ch * TC
            pb = [[self.psum(), self.psum()], [self.psum(), self.psum()]]
            for z, slab in ((0, sBr), (1, sBi)):
                for c in range(32):
                    ps = pb[z][c // 16]
                    S.op("pe", lambda e, c=c, ps=ps, slab=slab: e.matmul(ps[:, (c % 16) * T:(c % 16 + 1) * T], lhsT=slab[:, c * P:(c + 1) * P],
                                                                         rhs=hT[:, c // 4, t0:t0 + T], start=True, stop=True),
                         reads=[slab.res, hy.res], writes=[ps.res])
            def dv(fn):
                S.op("dve", fn, reads=[wk.res, self.s5res], writes=[wk.res])

            def mul(o, a, b):
                dv(lambda e: e.tensor_tensor(out=o, in0=a, in1=b, op=ALU.mult))
            R = lambda j: wk[:, j, :, 0:T]
            for hb in range(2):
                Rh = lambda j, hb=hb: wk[:, j, hb * 16:(hb + 1) * 16, 0:T]
                PSv = lambda z, hb=hb: pb[z][hb][:, 0:16 * T].rearrange("p (c t) -> p c t", t=T)
                frh = W[:, ix["FR"], hb * 16:(hb + 1) * 16].unsqueeze(2).to_broadcast([P, 16, T])
                fih = W[:, ix["FI"], hb * 16:(hb + 1) * 16].unsqueeze(2).to_broadcast([P, 16, T])

                def dp(fn, hb=hb):
                    S.op("dve", fn, reads=[wk.res, self.s5res, pb[0][hb].res, pb[1][hb].res], writes=[wk.res])
                dp(lambda e, Rh=Rh, PSv=PSv, frh=frh: e.tensor_tensor(out=Rh(4), in0=PSv(0), in1=frh, op=ALU.mult))
                dp(lambda e, Rh=Rh, PSv=PSv, fih=fih: e.tensor_tensor(out=Rh(5), in0=PSv(1), in1=fih, op=ALU.mult))
                dp(lambda e, Rh=Rh: e.tensor_tensor(out=Rh(0), in0=Rh(4), in1=Rh(5), op=ALU.subtract))
                dp(lambda e, Rh=Rh, PSv=PSv, frh=frh: e.tensor_tensor(out=Rh(4), in0=PSv(1), in1=frh, op=ALU.mult))
                dp(lambda e, Rh=Rh, PSv=PSv, fih=fih: e.tensor_tensor(out=Rh(5), in0=PSv(0), in1=fih, op=ALU.mult))
                dp(lambda e, Rh=Rh: e.tensor_tensor(out=Rh(1), in0=Rh(4), in1=Rh(5), op=ALU.add))
            if sample:
                xs_ = self.s5xs
                abr = W[:, ix["ABR"], :].unsqueeze(2).to_broadcast([P, 32, NS])
                abi = W[:, ix["ABI"], :].unsqueeze(2).to_broadcast([P, 32, NS])

                def dx(fn):
                    S.op("dve", fn, reads=[wk.res, self.s5res, xs_.res], writes=[wk.res])
                dx(lambda e: e.tensor_tensor(out=R(4), in0=xs_[:, :, :, 0], in1=abr, op=ALU.mult))
                dx(lambda e: e.tensor_tensor(out=R(5), in0=xs_[:, :, :, 1], in1=abi, op=ALU.mult))
                dv(lambda e: e.tensor_tensor(out=R(4), in0=R(4), in1=R(5), op=ALU.subtract))
                dv(lambda e: e.tensor_tensor(out=R(2), in0=R(4), in1=R(0), op=ALU.add))
                dx(lambda e: e.tensor_tensor(out=R(4), in0=xs_[:, :, :, 1], in1=abr, op=ALU.mult))
                dx(lambda e: e.tensor_tensor(out=R(5), in0=xs_[:, :, :, 0], in1=abi, op=ALU.mult))
                dv(lambda e: e.tensor_tensor(out=R(4), in0=R(4), in1=R(5), op=ALU.add))
                dv(lambda e: e.tensor_tensor(out=R(3), in0=R(4), in1=R(1), op=ALU.add))
                xo = self.s5xs
                S.op("dve", lambda e: e.tensor_copy(out=xo[:, :, :, 0], in_=R(2)), reads=[wk.res], writes=[xo.res])
                S.op("dve", lambda e: e.tensor_copy(out=xo[:, :, :, 1], in_=R(3)), reads=[wk.res], writes=[xo.res])
                S.dma("sp", [(self.ssm_s[:, :, :, :], xo[:, :, :, :])], reads=[xo.res])
                xb = self.s5xb
                S.op("dve", lambda e: e.tensor_copy(out=xb[:, 0, :, 0:T], in_=R(2)), reads=[wk.res], writes=[xb.res])
                S.op("dve", lambda e: e.tensor_scalar(out=xb[:, 1, :, 0:T], in0=R(3), scalar1=-1.0, scalar2=None, op0=ALU.mult), reads=[wk.res], writes=[xb.res])
            else:
                cT, sT, rT = self.cosT, self.sinT, self.rhoT
                mul(R(4), R(0), cT[:, :, :])
                mul(R(5), R(1), sT[:, :, :])
                dv(lambda e: e.tensor_tensor(out=R(2), in0=R(4), in1=R(5), op=ALU.add))
                mul(R(4), R(1), cT[:, :, :])
                mul(R(5), R(0), sT[:, :, :])
                dv(lambda e: e.tensor_tensor(out=R(3), in0=R(4), in1=R(5), op=ALU.subtract))
                xp_, zi_ = self.s5x, self.s5zi
                c1, s1 = W[:, ix["CS"], :], W[:, ix["SN"], :]

                def dz(fn):
                    S.op("dve", fn, reads=[xp_.res, self.s5res, zi_.res, wk.res], writes=[zi_.res, wk.res])
                dz(lambda e: e.tensor_tensor(out=wk[:, 4, :, 0], in0=xp_[:, :, 0], in1=c1, op=ALU.mult))
                dz(lambda e: e.tensor_tensor(out=wk[:, 4, :, 1], in0=xp_[:, :, 1], in1=s1, op=ALU.mult))
                dz(lambda e: e.tensor_tensor(out=zi_[:, :, 0], in0=wk[:, 4, :, 0], in1=wk[:, 4, :, 1], op=ALU.subtract))
                dz(lambda e: e.tensor_tensor(out=wk[:, 4, :, 2], in0=xp_[:, :, 1], in1=c1, op=ALU.mult))
                dz(lambda e: e.tensor_tensor(out=wk[:, 4, :, 3], in0=xp_[:, :, 0], in1=s1, op=ALU.mult))
                dz(lambda e: e.tensor_tensor(out=zi_[:, :, 1], in0=wk[:, 4, :, 2], in1=wk[:, 4, :, 3], op=ALU.add))
                zres = [Res("zr"), Res("zi")]
                for z in range(2):
                    for c in range(32):
                        S.op("dve", lambda e, z=z, c=c: e.tensor_tensor_scan(out=wk[:, 0 + z, c, :], data0=rT[:, c, :], data1=wk[:, 2 + z, c, :],
                                                                             initial=zi_[:, c, z:z + 1], op0=ALU.mult, op1=ALU.add),
                             reads=[wk.res, self.s5res, zi_.res], writes=[zres[z]])
                S.op("dve", lambda e: e.tensor_copy(out=wk[:, 5, 0, 0:1], in_=wk[:, 5, 0, 0:1]), reads=[zres[0], zres[1]], writes=[wk.res])
                xb = self.s5xb
                mul(R(4), R(0), cT[:, :, :])
                mul(R(5), R(1), sT[:, :, :])
                dv(lambda e: e.tensor_tensor(out=R(2), in0=R(4), in1=R(5), op=ALU.subtract))
                mul(R(4), R(0), sT[:, :, :])
                mul(R(5), R(1), cT[:, :, :])
                dv(lambda e: e.tensor_tensor(out=R(3), in0=R(4), in1=R(5), op=ALU.add))
                S.op("dve", lambda e: e.tensor_copy(out=xp_[:, :, 0], in_=wk[:, 2, :, T - 1]), reads=[wk.res], writes=[xp_.res])
                S.op("dve", lambda e: e.tensor_copy(out=xp_[:, :, 1], in_=wk[:, 3, :, T - 1]), reads=[wk.res], writes=[xp_.res])
                S.op("act", lambda e: e.activation(out=xb[:, 0, :, 0:T], in_=R(2), func=AF.Copy), reads=[wk.res], writes=[xb.res])
                S.op("act", lambda e: e.activation(out=xb[:, 1, :, 0:T], in_=R(3), func=AF.Copy, scale=-1.0), reads=[wk.res], writes=[xb.res])
            xb = self.s5xb
            py = self.psum()
            for k in range(KD):
                n_ = 0
                for j in range(4):
                    c = 4 * k + j
                    for z, slab in ((0, sCr), (1, sCi)):
                        S.op("pe", lambda e, k=k, c=c, z=z, slab=slab, n_=n_: e.matmul(py[:, k * T:(k + 1) * T], lhsT=slab[:, c * P:(c + 1) * P],
                                                                                       rhs=xb[:, z, c, 0:T], start=(n_ == 0), stop=(n_ == 7)),
                             reads=[slab.res, xb.res], writes=[py.res])
                        n_ += 1
            tf = self.tmpf
            S.op("dve", lambda e: e.tensor_tensor(out=tf[:, 0, 0:KD * T].rearrange("p (k t) -> p k t", t=T), in0=hT[:, :, t0:t0 + T],
                                                  in1=self.s5d[:, :].unsqueeze(2).to_broadcast([P, KD, T]), op=ALU.mult),
                 reads=[hy.res, self.cres], writes=[self.tmpres[0]])
            S.op("dve", lambda e: e.tensor_tensor(out=tf[:, 0, 0:KD * T], in0=tf[:, 0, 0:KD * T], in1=py[:, 0:KD * T], op=ALU.add),
                 reads=[self.tmpres[0], py.res], writes=[self.tmpres[0]])
            S.op("act", lambda e: e.activation(out=gT[:, 0:KD, t0:t0 + T], in_=tf[:, 0, 0:KD * T].rearrange("p (k t) -> p k t", t=T), func=AF.Gelu),
                 reads=[self.tmpres[0]], writes=[self.actres])
        if (not sample) and it == cfg.NT - 1:
            S.dma("sp", [(self.ssm_p[:, :, :], self.s5x[:, :, :])], reads=[self.s5x.res])
        yT = self.yT
        sg = self.tmpf
        for mg in range(0, KD, 4):
            s1_ = self.ws.next(("glu1", mg), 8 * 4 * P, lambda w, mg=mg: slab_fm(w["ssm_w_glu1"][0], 0, 8, mg, 4))
            s2_ = self.ws.next(("glu2", mg), 8 * 4 * P, lambda w, mg=mg: slab_fm(w["ssm_w_glu2"][0], 0, 8, mg, 4))
            for m in range(4):
                p1, p2 = self.psum(), self.psum()
                for (slab, ps) in ((s1_, p1), (s2_, p2)):
                    for k in range(KD):
                        S.op("pe", lambda e, k=k, slab=slab, ps=ps, m=m: e.matmul(ps[:, 0:N], lhsT=slab[:, (k * 4 + m) * P:(k * 4 + m + 1) * P], rhs=gT[:, k, 0:N],
                                                                                 start=(k == 0), stop=(k == KD - 1)), reads=[slab.res, self.actres], writes=[ps.res])
                mm = mg + m
                j = mm % 2
                S.op("act", lambda e, p2=p2, mm=mm, j=j: e.activation(out=sg[:, j, 0:N], in_=p2[:, 0:N], func=AF.Sigmoid, bias=self.s5b2[:, mm:mm + 1], scale=1.0),
                     reads=[p2.res, self.cres], writes=[self.tmpres[j]])
                S.op("dve", lambda e, p1=p1, mm=mm, j=j: e.scalar_tensor_tensor(out=yT[:, mm, 0:N], in0=p1[:, 0:N], scalar=self.s5b1[:, mm:mm + 1], in1=sg[:, j, 0:N],
                                                                               op0=ALU.add, op1=ALU.mult),
                     reads=[p1.res, self.tmpres[j], self.cres], writes=[hy.res])


def e64_host():
    a = np.zeros((P, 32, P), np.float32)
    for m in range(P):
        for v in range(32):
            for j in range(2):
                if (m % 64) == 2 * v + j:
                    a[m, v, j * 64:(j + 1) * 64] = 30000.0
    return a.reshape(P, 32 * P)


def caus_host():
    a = np.zeros((P, 8, 512), np.float32)
    key = np.arange(P)[:, None]
    q = np.arange(512)[None, :]
    for j in range(4):
        a[:, j, :] = np.where(j * P + key <= q, 0.0, -30000.0)
        a[:, 4 + j, :] = np.where(j * P + key >= q, 0.0, -30000.0)
    return a.reshape(P, 8 * 512)


def slab_fm(W, k0, kc, mg, mcnt):
    W = np.asarray(W)
    sub = W[k0 * P:(k0 + kc) * P, mg * P:(mg + mcnt) * P].reshape(kc, P, mcnt * P)
    return np.ascontiguousarray(sub.transpose(1, 0, 2)).reshape(P, kc * mcnt * P)


def slab_tm(W):
    W = np.asarray(W)
    return np.ascontiguousarray(W.reshape(8, P, 512).transpose(1, 0, 2)).reshape(P, 8 * 512)


_CACHE = {}


def get_prog(cfg_key=(8192, 4, 8192, (0, 1, 2, 3))):
    if cfg_key not in _CACHE:
        cfg = Cfg(seq=cfg_key[0], nsamp=cfg_key[1], past=cfg_key[2], layers=cfg_key[3])
        pr = Prog(cfg)
        pr.build()
        n = len(pr.ws.recipes)
        cfg2 = Cfg(seq=cfg_key[0], nsamp=cfg_key[1], past=cfg_key[2], layers=cfg_key[3], nslab_max=n)
        pr = Prog(cfg2)
        pr.build()
        _CACHE[cfg_key] = pr
    return _CACHE[cfg_key]


def run_prog(pr, inputs, ncores):
    I = {k: np.asarray(v) for k, v in inputs.items()}
    wts = pr.ws.host_array(I)
    in_maps = []
    for c in range(ncores):
        m = {}
        for name, (shape, fn, npdt) in pr.host.items():
            if name == "wts":
                m[name] = wts
            else:
                key = (name, c)
                a = np.ascontiguousarray(np.asarray(fn(I, c), dtype=npdt)).reshape(shape)
                m[name] = a
        in_maps.append(m)
    res = run_bass_kernel_spmd(pr.nc, in_maps, core_ids=list(range(ncores)))
    return res.results


def kernel(**inputs):
    pr = get_prog()
    NS = 4
    res = run_prog(pr, inputs, 8)
    f32 = np.float32
    y_prompt = np.stack([res[0]["yp"], res[1]["yp"]]).astype(f32)
    y_sample = np.concatenate([res[c]["ys"] for c in range(8)], 0)[:, None, :].astype(f32)
    kv = (2, 4, 64)
    new_cmp_p = np.stack([res[0]["cmp_p"], res[1]["cmp_p"]]).reshape((1, 2, 8192) + kv).astype(f32)
    new_slc_p = np.stack([res[0]["slc_p"], res[1]["slc_p"]]).reshape((1, 2, 8192) + kv).astype(f32)
    new_win_p = np.stack([res[0]["win_p"], res[1]["win_p"]]).reshape((1, 2, 512) + kv).astype(f32)
    new_cmp_s = np.concatenate([res[c]["cmp_s"] for c in range(8)], 0).reshape((1, 32, 1) + kv).astype(f32)
    new_slc_s = np.concatenate([res[c]["slc_s"] for c in range(8)], 0).reshape((1, 32, 1) + kv).astype(f32)
    new_win_s = np.concatenate([res[c]["win_s"] for c in range(8)], 0).reshape((1, 32, 512) + kv).astype(f32)

    def st_p(a):
        return a.reshape(2, 64, 32, 2).transpose(2, 0, 1, 3).reshape(64, 64, 2)

    def st_s(a):
        return a.reshape(2, 64, 32, NS, 2).transpose(3, 2, 0, 1, 4).reshape(NS, 64, 64, 2)
    new_ssm_p = np.stack([st_p(res[0]["ssm_p"]), st_p(res[1]["ssm_p"])])[None].astype(f32)
    new_ssm_s = np.concatenate([st_s(res[c]["ssm_s"]) for c in range(8)], 0)[None].astype(f32)
    gvs = np.concatenate([res[c]["gv"] for c in range(8)], 1)[:, :, None, :].astype(f32)
    return (y_prompt, y_sample, new_cmp_p, new_cmp_s, new_slc_p, new_slc_s, new_win_p, new_win_s, new_ssm_p, new_ssm_s, gvs)
```

```python
import os
import numpy as np
from contextlib import ExitStack
import concourse.bass as bass
import concourse.mybir as mybir
from concourse.bass_utils import run_bass_kernel_spmd

F32 = mybir.dt.float32
BF16 = mybir.dt.bfloat16
I32 = mybir.dt.int32
AF = mybir.ActivationFunctionType
ALU = mybir.AluOpType
P = 128
D = 1024
KD = 8
DFF = 2816
KF = 22
SLOTW = 4096
RMS_EPS = 1e-6
LN_EPS = 1e-5


class Res:
    __slots__ = ("name", "w", "r", "dsem", "dcnt", "outanchor")

    def __init__(self, name):
        self.name = name
        self.w = None
        self.r = {}
        self.dsem = None
        self.dcnt = 0
        self.outanchor = None


class Sched:
    def __init__(self, nc, es):
        self.nc = nc
        self.es = es
        self.eng = {"pe": nc.tensor, "act": nc.scalar, "dve": nc.vector, "pool": nc.gpsimd, "sp": nc.sync}
        self.sem = {k: es.enter_context(nc.semaphore("S_" + k)) for k in self.eng}
        self.cnt = {k: 0 for k in self.eng}
        self.known = {k: {} for k in self.eng}
        self.nsem = 0
        self.out_anchors = []

    def _wait(self, e, tok):
        sem, val, src = tok[0], tok[1], tok[2]
        if src == "dma":
            val = tok[3].dcnt
        key = id(sem)
        if self.known[e].get(key, 0) >= val:
            return
        self.eng[e].wait_ge(sem, val)
        self.known[e][key] = val

    def _deps(self, e, reads, writes):
        toks = []
        for r in reads:
            if r.w is not None:
                toks.append(r.w)
        for w in writes:
            if w.w is not None:
                toks.append(w.w)
            toks.extend(w.r.values())
        for t in toks:
            if t[2] == "pe" and e == "pe":
                continue
            self._wait(e, t)

    def op(self, e, fn, reads=(), writes=()):
        self._deps(e, reads, writes)
        ins = fn(self.eng[e])
        self.cnt[e] += 1
        ins.then_inc(self.sem[e], 1)
        tok = (self.sem[e], self.cnt[e], e)
        for r in reads:
            r.r[e] = tok
        for w in writes:
            w.w = tok
            w.r = {}
        return ins

    def dma(self, q, pairs, reads=(), writes=(), anchor=None, serialize=True, **kw):
        self._deps(q, reads, writes)
        if anchor is None:
            if writes:
                anchor = writes[0]
            else:
                if reads[0].outanchor is None:
                    reads[0].outanchor = Res("out_" + reads[0].name)
                    self.out_anchors.append(reads[0].outanchor)
                anchor = reads[0].outanchor
        if anchor.dsem is None:
            anchor.dsem = self.es.enter_context(self.nc.semaphore("D%d" % self.nsem))
            self.nsem += 1
        elif serialize and anchor.dcnt > 0:
            self._wait(q, (anchor.dsem, anchor.dcnt, "dma", anchor))
        for (o, i) in pairs:
            self.eng[q].dma_start(out=o, in_=i, **kw).then_inc(anchor.dsem, 16)
            anchor.dcnt += 16
        tok = (anchor.dsem, anchor.dcnt, "dma", anchor)
        for r in reads:
            r.r[("dma", id(anchor.dsem))] = tok
        for w in writes:
            w.w = tok
            w.r = {}
        return anchor

    def dma_gather(self, out_ap, in_ap, idx_ap, reads=(), writes=()):
        self._deps("pool", reads, writes)
        anchor = writes[0]
        if anchor.dsem is None:
            anchor.dsem = self.es.enter_context(self.nc.semaphore("D%d" % self.nsem))
            self.nsem += 1
        elif anchor.dcnt > 0:
            self._wait("pool", (anchor.dsem, anchor.dcnt, "dma", anchor))
        self.eng["pool"].indirect_dma_start(out=out_ap, out_offset=None, in_=in_ap,
                                            in_offset=bass.IndirectOffsetOnAxis(ap=idx_ap, axis=0)).then_inc(anchor.dsem, 16)
        anchor.dcnt += 16
        tok = (anchor.dsem, anchor.dcnt, "dma", anchor)
        for r in reads:
            r.r[("dma", id(anchor.dsem))] = tok
        for w in writes:
            w.w = tok
            w.r = {}

    def finish(self):
        for a in self.out_anchors:
            if a.dsem is not None:
                self._wait("sp", (a.dsem, a.dcnt, "dma", a))


class Buf:
    def __init__(self, S, name, shape, dt, psum=False):
        nc = S.nc
        if psum:
            self.t = S.es.enter_context(nc.psum_tensor(name, shape, dt))
        else:
            self.t = S.es.enter_context(nc.sbuf_tensor(name, shape, dt))
        self.res = Res(name)
        self.shape = shape

    def __getitem__(self, idx):
        return self.t[idx]


class Cfg:
    def __init__(self, seq=8192, nsamp=4, past=8192, layers=(0, 1, 2, 3), nslab_max=176):
        self.SEQ = seq
        self.TT = 512
        self.NT = seq // 512
        self.NSAMP = nsamp
        self.PAST = past
        self.layers = tuple(layers)
        self.NSLAB = nslab_max


class WStream:
    def __init__(self, S, wdram, nslot, nslab):
        self.S = S
        self.wdram = wdram
        self.nslot = nslot
        self.slots = [Buf(S, "wslot%d" % i, [P, SLOTW], BF16) for i in range(nslot)]
        self.recipes = []
        self.index = {}
        self.pos = 0
        self.nslab = nslab

    def next(self, key, ncols, fn, npart=P):
        if key not in self.index:
            self.index[key] = len(self.recipes)
            self.recipes.append((ncols, npart, fn))
            assert len(self.recipes) <= self.nslab, "too many slabs"
        j = self.index[key]
        slot = self.slots[self.pos % self.nslot]
        self.pos += 1
        half = ncols // 2
        if ncols >= 1024:
            pairs = [(slot.t[0:npart, 0:half], self.wdram[j, 0:npart, 0:half]),
                     (slot.t[0:npart, half:ncols], self.wdram[j, 0:npart, half:ncols])]
        else:
            pairs = [(slot.t[0:npart, 0:ncols], self.wdram[j, 0:npart, 0:ncols])]
        if os.environ.get("NO_WDMA") == "1" and self.pos > self.nslot:
            return slot
        self.S.dma("pool", pairs, writes=[slot.res])
        return slot

    def host_array(self, weights):
        arr = np.zeros((self.nslab, P, SLOTW), np.float32)
        for j, (ncols, npart, fn) in enumerate(self.recipes):
            a = np.asarray(fn(weights), np.float32)
            assert a.shape == (npart, ncols), (a.shape, npart, ncols)
            arr[j, :npart, :ncols] = a
        return arr


class Prog:
    def __init__(self, cfg):
        self.cfg = cfg
        self.nc = bass.Bass("TRN2", target_bir_lowering=False)
        self.host = {}
        self.outs = {}
        self.es = ExitStack()

    def din(self, name, shape, fn, dt=F32):
        t = self.nc.dram_tensor(name, list(shape), dt, kind="ExternalInput").ap()
        self.host[name] = (tuple(shape), fn, np.int32 if dt == I32 else np.float32)
        return t

    def dout(self, name, shape, dt=F32):
        t = self.nc.dram_tensor(name, list(shape), dt, kind="ExternalOutput").ap()
        self.outs[name] = tuple(shape)
        return t

    def sb(self, name, shape, dt):
        return Buf(self.S, name, shape, dt)

    def const(self, name, shape, fn, dt=F32, sdt=None, q="sp"):
        d = self.din(name, shape, fn, dt)
        b = self.sb("c_" + name, list(shape), sdt or dt)
        idx = tuple(slice(None) for _ in shape)
        self.S.dma(q if (sdt is None or sdt == dt) else "pool", [(b.t[idx], d[idx])], writes=[b.res])
        return b

    def psum(self):
        b = self.pbanks[self.pidx % len(self.pbanks)]
        self.pidx += 1
        return b

    def sumsq_rstd(self, src_ap, src_res, N, eps, tag):
        S = self.S
        sqb = self.tmpb
        ps = self.psum()
        for k in range(KD):
            j = k % 2
            S.op("act", lambda e, k=k, j=j: e.activation(out=sqb[:, j, 0:N], in_=src_ap[:, k, :], func=AF.Square),
                 reads=[src_res], writes=[self.tmpbres[j]])
            S.op("pe", lambda e, k=k, j=j: e.matmul(ps[:, 0:N], lhsT=self.ones_bf[:, 0:P], rhs=sqb[:, j, 0:N],
                                                    start=(k == 0), stop=(k == KD - 1)),
                 reads=[self.tmpbres[j], self.cres], writes=[ps.res])
        rs = self.rstd
        S.op("act", lambda e: e.activation(out=rs[:, 0:N], in_=ps[:, 0:N], func=AF.Sqrt,
                                           bias=self.epsb[:, 0:1], scale=1.0 / D),
             reads=[ps.res, self.epsb.res], writes=[rs.res])
        S.op("dve", lambda e: e.reciprocal(out=rs[:, 0:N], in_=rs[:, 0:N]), reads=[rs.res], writes=[rs.res])
        return rs

    def prenorm(self, N, A, Bv, col=None):
        S = self.S
        xT, hy = self.xT, self.hy
        rs = self.sumsq_rstd(xT[:, :, 0:N], xT.res, N, RMS_EPS, "pre")
        hT = self.hT
        tmp = self.tmpf
        if col is not None:
            for k in range(KD):
                S.op("dve", lambda e, k=k: e.scalar_tensor_tensor(out=tmp[:, k % 2, 0:N], in0=xT[:, k, 0:N],
                                                                   scalar=A[:, k, col:col + 1], in1=rs[:, 0:N],
                                                                   op0=ALU.mult, op1=ALU.mult),
                     reads=[xT.res, rs.res, self.modres], writes=[self.tmpres[k % 2]])
                S.op("act", lambda e, k=k: e.activation(out=hT[:, k, 0:N], in_=tmp[:, k % 2, 0:N], func=AF.Identity,
                                                        bias=Bv[:, k, col:col + 1], scale=1.0),
                     reads=[self.tmpres[k % 2], self.modres], writes=[hy.res])
        else:
            for k in range(KD):
                S.op("dve", lambda e, k=k: e.tensor_tensor(out=tmp[:, 0, 0:N], in0=xT[:, k, 0:N], in1=rs[:, 0:N], op=ALU.mult),
                     reads=[xT.res, rs.res], writes=[self.tmpres[0]])
                S.op("dve", lambda e, k=k: e.tensor_tensor(out=tmp[:, 0, 0:N], in0=tmp[:, 0, 0:N], in1=A[:, k, :], op=ALU.mult),
                     reads=[self.tmpres[0], self.modres], writes=[self.tmpres[0]])
                S.op("dve", lambda e, k=k: e.tensor_tensor(out=hT[:, k, 0:N], in0=tmp[:, 0, 0:N], in1=Bv[:, k, :], op=ALU.add),
                     reads=[self.tmpres[0], self.modres], writes=[hy.res])

    def postnorm_residual(self, N, G, col=None):
        S = self.S
        xT, yT, hy = self.xT, self.yT, self.hy
        rs = self.sumsq_rstd(yT[:, :, 0:N], hy.res, N, RMS_EPS, "post")
        tmp = self.tmpf
        for k in range(KD):
            S.op("dve", lambda e, k=k: e.tensor_tensor(out=tmp[:, k % 2, 0:N], in0=yT[:, k, 0:N], in1=rs[:, 0:N], op=ALU.mult),
                 reads=[hy.res, rs.res], writes=[self.tmpres[k % 2]])
            if col is not None:
                S.op("dve", lambda e, k=k: e.scalar_tensor_tensor(out=xT[:, k, 0:N], in0=tmp[:, k % 2, 0:N],
                                                                   scalar=G[:, k, col:col + 1], in1=xT[:, k, 0:N],
                                                                   op0=ALU.mult, op1=ALU.add),
                     reads=[self.tmpres[k % 2], xT.res, self.modres], writes=[xT.res])
            else:
                S.op("dve", lambda e, k=k: e.tensor_tensor(out=tmp[:, k % 2, 0:N], in0=tmp[:, k % 2, 0:N], in1=G[:, k, :], op=ALU.mult),
                     reads=[self.tmpres[k % 2], self.modres], writes=[self.tmpres[k % 2]])
                S.op("dve", lambda e, k=k: e.tensor_tensor(out=xT[:, k, 0:N], in0=tmp[:, k % 2, 0:N], in1=xT[:, k, 0:N], op=ALU.add),
                     reads=[self.tmpres[k % 2], xT.res], writes=[xT.res])

    def mm_fm(self, key, wfn, nk, nm, rhs_fn, rhs_res, N, evac):
        S = self.S
        for mg in range(0, nm, 4):
            mcnt = min(4, nm - mg)
            kgroups = [(k0, min(8, nk - k0)) for k0 in range(0, nk, 8)]
            if len(kgroups) == 1:
                k0, kc = kgroups[0]
                slab = self.ws.next((key, mg, 0), kc * mcnt * P,
                                    lambda w, mg=mg, mcnt=mcnt, kc=kc: slab_fm(wfn(w), 0, kc, mg, mcnt))
                for m in range(mcnt):
                    ps = self.psum()
                    for k in range(kc):
                        S.op("pe", lambda e, k=k, m=m, ps=ps, slab=slab: e.matmul(
                            ps[:, 0:N], lhsT=slab[:, (k * mcnt + m) * P:(k * mcnt + m + 1) * P], rhs=rhs_fn(k),
                            start=(k == 0), stop=(k == kc - 1)),
                            reads=[slab.res, rhs_res], writes=[ps.res])
                    evac(mg + m, ps)
            else:
                pss = [self.psum() for _ in range(mcnt)]
                for (k0, kc) in kgroups:
                    slab = self.ws.next((key, mg, k0), kc * mcnt * P,
                                        lambda w, mg=mg, mcnt=mcnt, kc=kc, k0=k0: slab_fm(wfn(w), k0, kc, mg, mcnt))
                    for m in range(mcnt):
                        for k in range(kc):
                            kg = k0 + k
                            S.op("pe", lambda e, k=k, m=m, kg=kg, slab=slab: e.matmul(
                                pss[m][:, 0:N], lhsT=slab[:, (k * mcnt + m) * P:(k * mcnt + m + 1) * P], rhs=rhs_fn(kg),
                                start=(kg == 0), stop=(kg == nk - 1)),
                                reads=[slab.res, rhs_res], writes=[pss[m].res])
                for m in range(mcnt):
                    evac(mg + m, pss[m])

    def evac_y(self, N):
        yT, hy = self.yT, self.hy

        def f(m, ps):
            self.S.op("act", lambda e: e.activation(out=yT[:, m, 0:N], in_=ps[:, 0:N], func=AF.Copy),
                      reads=[ps.res], writes=[hy.res])
        return f

    def ffn(self, l, N):
        S = self.S
        hT, hy, act = self.hT, self.hy, self.act
        gtmp = self.tmpb
        for mg in range(0, KF, 4):
            mcnt = min(4, KF - mg)
            sg = self.ws.next(("ffg", l, mg), 8 * mcnt * P, lambda w, mg=mg, mcnt=mcnt: slab_fm(w["ffn_w_gate"][l], 0, 8, mg, mcnt))
            su = self.ws.next(("ffu", l, mg), 8 * mcnt * P, lambda w, mg=mg, mcnt=mcnt: slab_fm(w["ffn_w_up"][l], 0, 8, mg, mcnt))
            for m in range(mcnt):
                pg = self.psum()
                pu = self.psum()
                for (slab, ps) in ((sg, pg), (su, pu)):
                    for k in range(KD):
                        S.op("pe", lambda e, k=k, slab=slab, ps=ps: e.matmul(
                            ps[:, 0:N], lhsT=slab[:, (k * mcnt + m) * P:(k * mcnt + m + 1) * P], rhs=hT[:, k, 0:N],
                            start=(k == 0), stop=(k == KD - 1)), reads=[slab.res, hy.res], writes=[ps.res])
                j = (mg + m) % 2
                S.op("act", lambda e, j=j, pg=pg: e.activation(out=gtmp[:, j, 0:N], in_=pg[:, 0:N], func=AF.Silu),
                     reads=[pg.res], writes=[self.tmpbres[j]])
                S.op("dve", lambda e, j=j, pu=pu, c=mg + m: e.tensor_tensor(out=act[:, c, 0:N], in0=gtmp[:, j, 0:N], in1=pu[:, 0:N], op=ALU.mult),
                     reads=[self.tmpbres[j], pu.res], writes=[self.actres])
        self.mm_fm(("ffd", l), lambda w: w["ffn_w_down"][l], KF, KD, lambda k: act[:, k, 0:N], self.actres, N, self.evac_y(N))

    def gmlp(self, j, N, sample, vout=None):
        S = self.S
        hT, hy, act = self.hT, self.hy, self.act
        uT = act
        bu = self.gm_bu[j]

        def evac_u(m, ps):
            S.op("act", lambda e: e.activation(out=uT[:, m, 0:N], in_=ps[:, 0:N], func=AF.Gelu, bias=bu[:, m:m + 1], scale=1.0),
                 reads=[ps.res, self.cres], writes=[self.actres])
        self.mm_fm(("gmu", j), lambda w: w["gmlp_w_in"][j][:, 0:D], KD, KD, lambda k: hT[:, k, 0:N], hy.res, N, evac_u)
        sv = [self.ws.next(("gmv", j, hf), 8 * 512, lambda w, hf=hf: slab_tm(w["gmlp_w_in"][j][:, D + hf * 512:D + (hf + 1) * 512]))
              for hf in range(2)]
        nsub = 1 if sample else N // P
        TS = N if sample else P
        vtm, vln = self.vtm, self.vln
        g_bv = g_lng = g_lnb = self.gm_row

        def load_row(i3):
            S.dma("sp", [(self.gm_row[:, :], self.gm_rows_d[j][i3][:, :])], writes=[self.gm_row.res])
        for s in range(nsub):
            load_row(0)
            for hf in range(2):
                ps = self.psum()
                for k in range(KD):
                    S.op("pe", lambda e, k=k, ps=ps, hf=hf: e.matmul(ps[0:TS, 0:512], lhsT=hT[:, k, s * P:s * P + TS],
                                                                     rhs=sv[hf][:, k * 512:(k + 1) * 512],
                                                                     start=(k == 0), stop=(k == KD - 1)),
                         reads=[sv[hf].res, hy.res], writes=[ps.res])
                S.op("dve", lambda e, ps=ps, hf=hf: e.tensor_tensor(out=vtm[0:TS, hf * 512:(hf + 1) * 512], in0=ps[0:TS, 0:512],
                                                                    in1=g_bv[0:TS, hf * 512:(hf + 1) * 512], op=ALU.add),
                     reads=[ps.res, g_bv.res], writes=[self.vtmres])
            S.op("act", lambda e: e.activation(out=vtm[0:TS, :], in_=vtm[0:TS, :], func=AF.Gelu), reads=[self.vtmres], writes=[self.vtmres])
            st = self.stat
            for hf in range(2):
                S.op("dve", lambda e, hf=hf: e.bn_stats(out=st[0:TS, hf * 6:(hf + 1) * 6], in_=vtm[0:TS, hf * 512:(hf + 1) * 512]),
                     reads=[self.vtmres], writes=[self.statres])
            S.op("dve", lambda e: e.bn_aggr(out=st[0:TS, 12:14], in_=st[0:TS, 0:12]), reads=[self.statres], writes=[self.statres])
            S.op("act", lambda e: e.activation(out=st[0:TS, 14:15], in_=st[0:TS, 13:14], func=AF.Sqrt, bias=self.lnepsb[0:TS, 0:1], scale=1.0),
                 reads=[self.statres, self.cres], writes=[self.statres])
            S.op("dve", lambda e: e.reciprocal(out=st[0:TS, 15:16], in_=st[0:TS, 14:15]), reads=[self.statres], writes=[self.statres])
            S.op("dve", lambda e: e.tensor_scalar(out=vtm[0:TS, :], in0=vtm[0:TS, :], scalar1=st[0:TS, 12:13], scalar2=st[0:TS, 15:16],
                                                  op0=ALU.subtract, op1=ALU.mult),
                 reads=[self.vtmres, self.statres], writes=[self.vtmres])
            load_row(1)
            S.op("dve", lambda e: e.tensor_tensor(out=vtm[0:TS, :], in0=vtm[0:TS, :], in1=g_lng[0:TS, :], op=ALU.mult),
                 reads=[self.vtmres, g_lng.res], writes=[self.vtmres])
            load_row(2)
            S.op("dve", lambda e: e.tensor_tensor(out=vtm[0:TS, :], in0=vtm[0:TS, :], in1=g_lnb[0:TS, :], op=ALU.add),
                 reads=[self.vtmres, g_lnb.res], writes=[self.vtmres])
            S.op("act", lambda e: e.activation(out=vln[0:TS, :], in_=vtm[0:TS, :], func=AF.Copy), reads=[self.vtmres], writes=[self.vlnres])
            if vout is not None:
                S.dma("sp", [(vout, vtm[0:TS, :])], reads=[self.vtmres])
            for g in range(8):
                ps = self.psum()
                if sample:
                    rhs_w = self.gm_wsS[j][0:TS, g, 0:TS]
                    rhs_b = self.gm_bsS[j][0:1, g, 0:TS]
                else:
                    rhs_w = self.gm_ws[j][:, g, :]
                    rhs_b = self.gm_bs[j][0:1, g, :]
                S.op("pe", lambda e, g=g, ps=ps, rhs_w=rhs_w: e.matmul(ps[:, 0:TS], lhsT=vln[0:TS, g * P:(g + 1) * P], rhs=rhs_w, start=True, stop=False),
                     reads=[self.vlnres, self.cres], writes=[ps.res])
                S.op("pe", lambda e, g=g, ps=ps, rhs_b=rhs_b: e.matmul(ps[:, 0:TS], lhsT=self.ones_bf[0:1, 0:P], rhs=rhs_b, start=False, stop=True),
                     reads=[self.cres, self.ones_bf.res], writes=[ps.res])
                S.op("dve", lambda e, g=g, ps=ps: e.tensor_tensor(out=uT[:, g, s * P:s * P + TS], in0=uT[:, g, s * P:s * P + TS], in1=ps[:, 0:TS], op=ALU.mult),
                     reads=[ps.res, self.actres], writes=[self.actres])
        self.mm_fm(("gmo", j), lambda w: w["gmlp_w_out"][j], KD, KD, lambda k: uT[:, k, 0:N], self.actres, N, self.evac_y(N))

    def build(self):
        cfg = self.cfg
        nc = self.nc
        es = self.es
        S = self.S = Sched(nc, es)
        TT, NT, NS = cfg.TT, cfg.NT, cfg.NSAMP
        NC = 1 + NS
        self.NC = NC
        wdram = self.din("wts", [cfg.NSLAB, P, SLOTW], None)
        xp = self.din("xp", [cfg.SEQ, D], lambda I, c: I["x_prompt"][c % 2])
        xs = self.din("xs", [NS, D], lambda I, c: I["x_sample"][c * NS:(c + 1) * NS, 0])
        yp = self.dout("yp", [cfg.SEQ, D])
        ys = self.dout("ys", [NS, D])
        gv = self.dout("gv", [2, NS, D])
        self.ws = WStream(S, wdram, 4, cfg.NSLAB)
        self.xT = self.sb("xT", [P, KD, TT], F32)
        self.hy = self.sb("hy", [P, KD, TT], F32)
        self.yT = self.hy.t
        self.hT = self.hy.t[:].rearrange("p k t -> p (k t)")[:, 0:KD * TT // 2].bitcast(BF16).rearrange("p (k t) -> p k t", k=KD)
        self.act = self.sb("act", [P, KF, TT], BF16)
        self.actres = self.act.res
        self.rstd = self.sb("rstd", [P, TT], F32)
        self.tmpf = self.sb("tmpf", [P, 2, TT], F32)
        self.tmpres = [Res("tmpf0"), Res("tmpf1")]
        self.tmpb = self.sb("tmpb", [P, 2, TT], BF16)
        self.tmpbres = [Res("tmpb0"), Res("tmpb1")]
        self.vtm = self.act.t[:, 8:12, :].rearrange("p k t -> p (k t)").bitcast(F32)
        self.vtmres = self.actres
        self.vln = self.act.t[:, 12:14, :].rearrange("p k t -> p (k t)")
        self.vlnres = self.actres
        self.stat = self.sb("stat", [P, 16], F32)
        self.statres = self.stat.res
        self.xin = Buf.__new__(Buf)
        self.xin.t = self.hy.t[:].rearrange("p k t -> p (k t)").rearrange("p (s d) -> p s d", s=4)
        self.xin.res = self.hy.res
        self.pbanks = [Buf(S, "ps%d" % i, [P, 512], F32, psum=True) for i in range(6)]
        self.pidx = 0
        self.pacc = [Buf(S, "pacc%d" % i, [P, 512], F32, psum=True) for i in range(2)]
        self.paidx = 0
        self.vout_anchor = Res("vout")
        S.out_anchors.append(self.vout_anchor)
        self.cres = Res("consts")
        self.cbar = self.sb("cbar", [P, 2], F32)
        self.modres = Res("mod")

        self.cres_hw = Res("consts_hw")
        self.cres_sw = Res("consts_sw")
        cload = self._cload
        self.ident = cload("ident", [P, P], lambda I, c: np.eye(P, dtype=np.float32))
        self.ones_bf = cload("ones", [P, P], lambda I, c: np.ones((P, P), np.float32), BF16)
        self.epsb = cload("epsb", [P, 1], lambda I, c: np.full((P, 1), RMS_EPS, np.float32))
        self.lnepsb = cload("lnepsb", [P, 1], lambda I, c: np.full((P, 1), LN_EPS, np.float32))
        nG = 2
        self.gm_bu = [cload("gm_bu%d" % j, [P, KD], lambda I, c, j=j: I["gmlp_b_in"][j][0:D].reshape(KD, P).T) for j in range(nG)]
        self.gm_rows_d = [[self.din("gm_bv%d" % j, [P, D], lambda I, c, j=j: np.broadcast_to(I["gmlp_b_in"][j][D:2 * D], (P, D))),
                           self.din("gm_lng%d" % j, [P, D], lambda I, c, j=j: np.broadcast_to(I["gmlp_ln_g"][j], (P, D))),
                           self.din("gm_lnb%d" % j, [P, D], lambda I, c, j=j: np.broadcast_to(I["gmlp_ln_b"][j], (P, D)))] for j in range(nG)]
        self.gm_row = self.sb("gm_row", [P, D], F32)
        self.gm_ws = [cload("gm_ws%d" % j, [P, 8, P], lambda I, c, j=j: np.where(np.tril(np.ones((P, P), bool))[None], I["gmlp_w_s"][j], 0.0).transpose(2, 0, 1), BF16) for j in range(nG)]
        self.gm_bs = [cload("gm_bs%d" % j, [1, 8, P], lambda I, c, j=j: I["gmlp_b_s"][j][None], BF16) for j in range(nG)]

        def wsS(I, c, j):
            a = np.zeros((NS, 8, NS), np.float32)
            for g in range(8):
                for q in range(NS):
                    a[q, g, q] = I["gmlp_w_s"][j][g, 0, 0]
            return a
        self.gm_wsS = [cload("gm_wsS%d" % j, [NS, 8, NS], lambda I, c, j=j: wsS(I, c, j), BF16) for j in range(nG)]
        self.gm_bsS = [cload("gm_bsS%d" % j, [1, 8, NS], lambda I, c, j=j: np.broadcast_to(I["gmlp_b_s"][j][:, 0][None, :, None], (1, 8, NS)), BF16) for j in range(nG)]
        scT = cload("scT", [P, KD, NC], lambda I, c: np.concatenate([I["c_prompt"][c % 2][None], I["c_sample"][c * NS:(c + 1) * NS]], 0).T.reshape(KD, P, NC).transpose(1, 0, 2))
        bmod = cload("bmod", [P, 4, 6, KD], lambda I, c: I["b_mod"].reshape(4, 6, KD, P).transpose(3, 0, 1, 2))
        ng = cload("ng", [P, 4, 4, KD], lambda I, c: I["norm_g"].reshape(4, 4, KD, P).transpose(3, 0, 1, 2))
        if 2 in cfg.layers:
            self.s5d = cload("s5d", [P, KD], lambda I, c: I["ssm_d"][0].reshape(KD, P).T)
            self.s5b1 = cload("s5b1", [P, KD], lambda I, c: I["ssm_b_glu1"][0].reshape(KD, P).T)
            self.s5b2 = cload("s5b2", [P, KD], lambda I, c: I["ssm_b_glu2"][0].reshape(KD, P).T)
        if 1 in cfg.layers:
            self.nsa_consts()
        self.consts_barrier()
        if 2 in cfg.layers:
            self.s5_setup()
        if 1 in cfg.layers:
            self.nsa_setup()
        scb = self.sb("scb", [P, KD, NC], BF16)
        S.op("act", lambda e: e.activation(out=scb[:, :, :], in_=scT[:, :, :], func=AF.Silu), reads=[self.cres], writes=[scb.res])
        self.mod = self.sb("mod", [P, 4, 6, KD, NC], F32)
        self.mod.res = self.modres
        for l in cfg.layers:
            for jj in range(6):
                def ev(m, ps, l=l, jj=jj):
                    S.op("act", lambda e: e.activation(out=self.mod[:, l, jj, m, :], in_=ps[:, 0:NC], func=AF.Identity,
                                                       bias=bmod[:, l, jj, m:m + 1], scale=1.0),
                         reads=[ps.res, self.cres], writes=[self.modres])
                self.mm_fm(("mod", l, jj), lambda w, l=l, jj=jj: w["w_mod"][l][:, jj * D:(jj + 1) * D], KD, KD,
                           lambda k: scb[:, k, :], scb.res, NC, ev)
            for (sc_i, g_i) in ((1, 0), (4, 2)):
                for col in range(NC):
                    S.op("dve", lambda e, col=col, sc_i=sc_i, g_i=g_i: e.scalar_tensor_tensor(
                        out=self.mod[:, l, sc_i, :, col], in0=self.mod[:, l, sc_i, :, col], scalar=1.0, in1=ng[:, l, g_i, :],
                        op0=ALU.add, op1=ALU.mult), reads=[self.modres, self.cres], writes=[self.modres])
            for (ga_i, g_i) in ((2, 1), (5, 3)):
                for col in range(NC):
                    S.op("dve", lambda e, col=col, ga_i=ga_i, g_i=g_i: e.tensor_tensor(
                        out=self.mod[:, l, ga_i, :, col], in0=self.mod[:, l, ga_i, :, col], in1=ng[:, l, g_i, :], op=ALU.mult),
                        reads=[self.modres, self.cres], writes=[self.modres])
        yout_anchor = Res("yout")
        S.out_anchors.append(yout_anchor)
        for it in range(NT + 1):
            sample = (it == NT)
            N = NS if sample else TT
            nsub = 1 if sample else 4
            TS = NS if sample else P
            xin = self.xin
            if sample:
                S.dma("sp", [(xin[0:NS, 0, :], xs[:, :])], writes=[xin.res])
            else:
                S.dma("sp", [(xin[:, s, :], xp[it * TT + s * P:it * TT + (s + 1) * P, :]) for s in range(4)], writes=[xin.res])
            for k in range(KD):
                ps = self.psum()
                for s in range(nsub):
                    S.op("pe", lambda e, k=k, s=s, ps=ps: e.transpose(ps[:, s * P:s * P + TS], xin[0:TS, s, k * P:(k + 1) * P], self.ident[0:TS, 0:TS]),
                         reads=[xin.res, self.cres], writes=[ps.res])
                S.op("act", lambda e, k=k, ps=ps: e.activation(out=self.xT[:, k, 0:N], in_=ps[:, 0:N], func=AF.Copy),
                     reads=[ps.res], writes=[self.xT.res])
            for l in cfg.layers:
                mod = self.mod
                if sample:
                    A1, B1, G1 = mod[:, l, 1, :, 1:NC], mod[:, l, 0, :, 1:NC], mod[:, l, 2, :, 1:NC]
                    A2, B2, G2 = mod[:, l, 4, :, 1:NC], mod[:, l, 3, :, 1:NC], mod[:, l, 5, :, 1:NC]
                    col = None
                else:
                    A1, B1, G1 = mod[:, l, 1], mod[:, l, 0], mod[:, l, 2]
                    A2, B2, G2 = mod[:, l, 4], mod[:, l, 3], mod[:, l, 5]
                    col = 0
                self.prenorm(N, A1, B1, col)
                kind = l % 3
                if kind == 0:
                    self.gmlp(l // 3, N, sample, vout=(gv[l // 3, :, :] if sample else None))
                elif kind == 1:
                    self.nsa(it, N, sample)
                else:
                    self.s5(it, N, sample)
                self.postnorm_residual(N, G1, col)
                self.prenorm(N, A2, B2, col)
                self.ffn(l, N)
                self.postnorm_residual(N, G2, col)
            xo = self.xin
            for s in range(nsub):
                for kq in range(2):
                    ps = self.psum()
                    for kk in range(4):
                        k = kq * 4 + kk
                        S.op("pe", lambda e, k=k, kk=kk, s=s, ps=ps: e.transpose(ps[0:TS, kk * P:(kk + 1) * P], self.xT[:, k, s * P:s * P + TS], self.ident[:, :]),
                             reads=[self.xT.res, self.cres], writes=[ps.res])
                    S.op("act", lambda e, s=s, kq=kq, ps=ps: e.activation(out=xo[0:TS, s, kq * 512:(kq + 1) * 512], in_=ps[0:TS, 0:512], func=AF.Copy),
                         reads=[ps.res], writes=[xo.res])
            if sample:
                S.dma("sp", [(ys[:, :], xo[0:NS, 0, :])], reads=[xo.res])
            else:
                S.dma("sp", [(yp[it * TT + s * P:it * TT + (s + 1) * P, :], xo[:, s, :]) for s in range(4)], reads=[xo.res])
        S.finish()
        return nc

    def nsa_setup(self):
        S = self.S
        cfg = self.cfg
        NT, NS, SEQ = cfg.NT, cfg.NSAMP, cfg.SEQ
        self.HE = [0, 1, 2, 3, 8, 9, 10, 11]
        self.HO = [4, 5, 6, 7, 12, 13, 14, 15]
        self.KTs = self.sb("KTs", [P, 2, SEQ], BF16)
        NKT = SEQ // P
        self.Vs = self.sb("Vg", [P, NKT, 66], BF16)
        self.Vst = self.sb("Vst", [P, 4, 66], BF16)
        self.vscr = self.nc.dram_tensor("vscr", [4, P, NKT, 66], BF16, kind="Internal").ap()
        self.vscr_res = Res("vscr")
        self.KTw = self.sb("KTw", [P, 2, 2, 512], BF16)
        self.Vw = self.sb("Vw", [P, 2, 4, 4, 66], BF16)
        self.KcT = self.sb("KcT", [P, 2, P], BF16)
        self.Vc = self.sb("Vc", [P, 4, 66], BF16)
        S.op("pool", lambda e: e.memset(self.Vst[:, :, :], 1.0), writes=[self.Vst.res])
        S.op("pool", lambda e: e.memset(self.Vw[:, :, :, :, :], 1.0), writes=[self.Vw.res])
        S.op("pool", lambda e: e.memset(self.KcT[:, :, :], 0.0), writes=[self.KcT.res])
        S.op("pool", lambda e: e.memset(self.Vc[:, :, 0:64], 0.0), writes=[self.Vc.res])
        self.Vc32 = self.sb("Vc32", [P, 256], F32)
        S.op("pool", lambda e: e.memset(self.Vc32[:, :], 0.0), writes=[self.Vc32.res])
        S.op("pool", lambda e: e.memset(self.Vc[:, :, 64:66], 1.0), writes=[self.Vc.res])
        self.XT = Buf.__new__(Buf)
        self.XT.t = self.act.t[:, 16:20, :]
        self.XT.res = Res("XT")
        self.gates = self.sb("gates", [P, 4, 48], F32)
        self.negmT = self.sb("negmT", [P, 2, 5, 512], BF16)
        S.op("pool", lambda e: e.memset(self.negmT[:, :, :, :], 0.0), writes=[self.negmT.res])
        self.tk1 = Buf.__new__(Buf)
        self.tk1.t = self.tmpf.t[:, 0, :].rearrange("p (a b) -> p a b", a=4)
        self.tk1.res = self.tmpres[0]
        self.tk2 = Buf.__new__(Buf)
        self.tk2.t = self.tmpf.t[:, 1, 0:384].rearrange("p (a b) -> p a b", a=3)
        self.tk2.res = self.tmpres[1]
        self.tks = self.sb("tks", [P, 32], F32)
        self.negm = self.sb("negm", [P, P], BF16)
        self.hidK = self.sb("hidK", [P, 2, 2, 16], BF16)
        self.hidV = self.sb("hidV", [P, 4, P], BF16)
        self.pebias = self.sb("pebias", [P, 2], F32)
        self.mk = self.sb("mk", [P, 3, 4, P], BF16)
        self.f4 = self.sb("f4", [P, 8], F32)


        def masks(I, c):
            a = np.zeros((NT, P, 3, 4, P), np.float32)
            n = np.arange(P)[None, :]
            for it in range(NT):
                for s_ in range(4):
                    t = (it * 512 + s_ * 128 + np.arange(P))[:, None]
                    jt = t // 64
                    a[it, :, 0, s_, :] = np.where((n + 1) * 64 <= t + 1, 0.0, -30000.0)
                    vd = n <= jt
                    f = vd & ((n == 0) | (n == jt) | (n == jt - 1))
                    a[it, :, 1, s_, :] = (vd & ~f).astype(np.float32)
                    a[it, :, 2, s_, :] = np.where(f, 1.0e4, np.where(vd, 0.0, -1.0))
            return a
        self.mk_d = self.din("nsa_masks", [NT, P, 3, 4, P], masks)

        def cmpbt(I, c):
            a = np.zeros((NT, P, 512), np.float32)
            n = np.arange(P)[:, None]
            for it in range(NT):
                t = (it * 512 + np.arange(512))[None, :]
                a[it] = np.where((n + 1) * 64 <= t + 1, 0.0, -30000.0)
            return a
        self.cmpbt_d = self.din("nsa_cmpbt", [NT, P, 512], cmpbt)
        self.cmp_p = self.dout("cmp_p", [SEQ, 512])
        self.slc_p = self.dout("slc_p", [SEQ, 512])
        self.win_p = self.dout("win_p", [512, 512])
        self.kv_anchor = Res("kvout")
        S.out_anchors.append(self.kv_anchor)
        NPG = cfg.PAST // P
        self.NPG = NPG
        self.cmp_s = self.dout("cmp_s", [NS, 512])
        self.slc_s = self.dout("slc_s", [NS, 512])
        self.win_s = self.dout("win_s", [NS, 512, 512])
        self.cache_c = self.din("cache_c", [2560 * P, 512], lambda I, c: I["cache_nsa_cmp"][0].reshape(-1, 512))
        self.cache_s = self.din("cache_s", [2560 * P, 512], lambda I, c: I["cache_nsa_slc"][0].reshape(-1, 512))
        self.cache_w = self.din("cache_w", [NS, 512, 512], lambda I, c: I["cache_nsa_win"][0][c * NS:(c + 1) * NS].reshape(NS, 512, 512))
        pt_d = self.din("ptab", [1, NS * NPG], lambda I, c: I["page_table"][c * NS:(c + 1) * NS].reshape(1, NS * NPG), I32)
        hyflat = self.hy.t[:].rearrange("p k t -> p (k t)")
        self.PGn = Buf.__new__(Buf)
        self.PGn.t = hyflat[:, 2048:3072].rearrange("p (a b) -> p a b", a=2)
        self.pgn_res = Res("PGn")
        self.PGn.res = self.pgn_res
        self.XTs = Buf.__new__(Buf)
        self.XTs.t = hyflat[:, 0:2048].bitcast(BF16).rearrange("p (a b) -> p a b", a=4)
        self.XTs.res = self.hy.res
        self.KTp = [self.sb("KTp%d" % i, [P, 2, P], BF16) for i in range(2)]
        self.Vp = [self.sb("Vp%d" % i, [P, 4, 2, P], BF16) for i in range(1)]
        for i in range(1):
            S.op("pool", lambda e, i=i: e.memset(self.Vp[i][:, :, :, :], 0.0), writes=[self.Vp[i].res])
        self.pTs = [self.sb("pTs%d" % i, [P, 16], BF16) for i in range(2)]
        self.KcS = self.sb("KcS", [P, 2, P], BF16)
        self.hidVs = self.sb("hidVs", [P, 4, P], BF16)
        self.impT = self.sb("impT", [P, 4 * NS], F32)
        self.sw = self.sb("sampw", [P, 64], F32)
        self.sc16 = self.sb("sc16", [4 * NS, 2, P], F32)
        self.sk16 = self.sb("sk16", [4 * NS, 32], F32)
        self.ng16 = self.sb("ng16", [4 * NS, P], BF16)
        self.negS = self.sb("negS", [P, 2, 4 * NS], BF16)
        S.op("pool", lambda e: e.memset(self.negS[:, :, :], 0.0), writes=[self.negS.res])
        self.res_o = self.sb("res_o", [P, 3, NS, 8], F32)
        self.res_s = self.sb("res_s", [P, 3, NS, 8], F32)
        self.idxf = self.sb("idxf", [P, NS * NPG], F32)
        self.idxi = self.sb("idxi", [P, NS * NPG], I32)
        pti = self.sb("pti", [P, NS * NPG], I32)
        S.dma("sp", [(pti[:, :], pt_d[0:1, :].to_broadcast([P, NS * NPG]))], writes=[pti.res])
        S.op("dve", lambda e: e.tensor_copy(out=self.idxf[:, :], in_=pti[:, :]), reads=[pti.res], writes=[self.idxf.res])
        S.op("dve", lambda e: e.tensor_scalar(out=self.idxf[:, :], in0=self.idxf[:, :], scalar1=float(P), scalar2=self.iota_p[:, 0:1], op0=ALU.mult, op1=ALU.add),
             reads=[self.idxf.res, self.cres], writes=[self.idxf.res])
        S.op("dve", lambda e: e.tensor_copy(out=self.idxi[:, :], in_=self.idxf[:, :]), reads=[self.idxf.res], writes=[self.idxi.res])
        self.kti = 0
        self.wcopy_res = Res("wcopy")
        pb = self.psum()
        for z in range(2):
            for lh in range(2):
                slab = self.w1slab(z, lh, 0)
                for l in range(32):
                    la = lh * 32 + l
                    S.op("pe", lambda e, z=z, l=l, la=la, slab=slab: e.matmul(pb[:, z:z + 1], lhsT=slab[:, l * P:(l + 1) * P], rhs=self.peT[:, z, la:la + 1],
                                                                              start=(la == 0), stop=(la == 63)),
                         reads=[slab.res, self.cres], writes=[pb.res])
        S.op("act", lambda e: e.activation(out=self.pebias[:, :], in_=pb[:, 0:2], func=AF.Copy), reads=[pb.res], writes=[self.pebias.res])

    def nsa_consts(self):
        self.ident_bf = self._cload("ident_b", [P, P], lambda I, c: np.eye(P, dtype=np.float32), BF16)

        def w2k(I, c):
            a = np.zeros((P, 2, P), np.float32)
            a[:, 0, 0:64] = I["nsa_w_cmp2"][0][0]
            a[:, 1, 64:128] = I["nsa_w_cmp2"][0][0]
            return a
        self.W2K = self._cload("w2k", [P, 2, P], w2k, BF16)
        self.W2V = self._cload("w2v", [P, 64], lambda I, c: I["nsa_w_cmp2"][0][1], BF16)
        self.peT = self._cload("peT", [P, 2, 64], lambda I, c: np.concatenate([I["nsa_pe_cmp"][0].transpose(2, 0, 1), np.zeros((64, 2, 64), np.float32)], 0), BF16)

        NS = self.cfg.NSAMP
        self.iota_p = self._cload("iota_p", [P, 1], lambda I, c: np.arange(P, dtype=np.float32)[:, None])

        def sel(I, c):
            a = np.zeros((NS, NS, 2, P), np.float32)
            for b in range(NS):
                a[b, b, 0, 0:64] = 1.0
                a[b, b, 1, 64:128] = 1.0
            return a
        self.Sel = self._cload("sel", [NS, NS, 2, P], sel)

        def newmask(I, c):
            a = np.full((P, NS, 4), -30000.0, np.float32)
            for b in range(NS):
                a[b, b, :] = 0.0
            return a
        self.newmask = self._cload("newmask", [P, NS, 4], newmask, BF16)

        def onesv(I, c):
            a = np.zeros((P, 2, P), np.float32)
            a[:, 0, 0:64] = 1.0
            a[:, 1, 64:128] = 1.0
            return a
        self.OnesV = self._cload("onesv", [P, 2, P], onesv, BF16)

    def _cload(self, name, shape, fn, sdt=F32):
        d = self.din(name, shape, fn)
        b = Buf(self.S, "c_" + name, list(shape), sdt)
        b.res = self.cres
        idx = tuple(slice(None) for _ in shape)
        if sdt != F32:
            self.S.dma("pool", [(b.t[idx], d[idx])], writes=[self.cres_sw], anchor=self.cres_sw, serialize=False)
        else:
            self.S.dma("sp", [(b.t[idx], d[idx])], writes=[self.cres_hw], anchor=self.cres_hw, serialize=False)
        return b

    def consts_barrier(self):
        self.S.op("dve", lambda e: e.memset(self.cbar[:, 0:1], 0.0), reads=[self.cres_hw, self.cres_sw], writes=[self.cres, self.cbar.res])

    def w1slab(self, z, lh, g2):
        def fn(w, z=z, lh=lh, g2=g2):
            w1 = np.asarray(w["nsa_w_cmp1"][0][z])[lh * 32:(lh + 1) * 32]
            a = np.zeros((P, 32 * P), np.float32)
            a[g2 * 64:(g2 + 1) * 64] = w1.transpose(1, 0, 2).reshape(64, 32 * P)
            return a
        return self.ws.next(("w1", z, lh, g2), 32 * P, fn)

    def head_loc(self, h):
        g, r = h // 4, h % 4
        return 4 * (g // 2) + r, g % 2

    def nsa_project(self, N, sample, t0, slot):
        S = self.S
        cfg = self.cfg
        hT, hy, act = self.hT, self.hy, self.act
        qperm = np.concatenate([np.concatenate([np.arange(self.HE[j] * 64, self.HE[j] * 64 + 64), np.arange(self.HO[j] * 64, self.HO[j] * 64 + 64)]) for j in range(8)])

        S.op("pool", lambda e: e.memset(act[64:128, 0:8, 0:N], 0.0), writes=[self.actres])
        S.op("pool", lambda e: e.memset(act[0:64, 8:16, 0:N], 0.0), writes=[self.actres])

        def evq(m, ps):
            S.op("act", lambda e: e.activation(out=act[0:64, m, 0:N], in_=ps[0:64, 0:N], func=AF.Copy, scale=0.125), reads=[ps.res], writes=[self.actres])
            S.op("act", lambda e: e.activation(out=act[64:128, 8 + m, 0:N], in_=ps[64:128, 0:N], func=AF.Copy, scale=0.125), reads=[ps.res], writes=[self.actres])
        self.mm_fm(("nq",), lambda w: w["nsa_w_in"][0][:, qperm], KD, KD, lambda k: hT[:, k, 0:N], hy.res, N, evq)
        sub = int(os.environ.get("NSA_SUB", "9"))
        if sub < 2:
            return
        kcols = np.concatenate([np.arange(1024, 1280), np.arange(1280, 1536), np.arange(1536, 1792), np.arange(2048, 2304)])

        def evk(m, ps):
            if m < 4:
                S.op("act", lambda e: e.activation(out=self.XT[:, m, 0:N], in_=ps[:, 0:N], func=AF.Copy), reads=[ps.res], writes=[self.XT.res])
            elif m < 6:
                S.op("act", lambda e: e.activation(out=self.KTs[:, m - 4, t0:t0 + N], in_=ps[:, 0:N], func=AF.Copy), reads=[ps.res], writes=[self.KTs.res])
            else:
                S.op("act", lambda e: e.activation(out=self.KTw[:, slot, m - 6, 0:N], in_=ps[:, 0:N], func=AF.Copy), reads=[ps.res], writes=[self.KTw.res])
        if not sample:
            self.mm_fm(("nk",), lambda w: w["nsa_w_in"][0][:, kcols], KD, KD, lambda k: hT[:, k, 0:N], hy.res, N, evk)
        if sub < 3:
            return
        skv = [self.ws.next(("nkv", b), 8 * 512, lambda w, b=b: slab_tm(w["nsa_w_in"][0][:, 1024 + 512 * b:1536 + 512 * b])) for b in range(3)]
        sg = self.ws.next(("ngate",), 8 * 48, lambda w: np.ascontiguousarray(w["nsa_w_in"][0][:, 2560:2608].reshape(8, P, 48).transpose(1, 0, 2)).reshape(P, 8 * 48))
        nsub = 1 if sample else 4
        TS = N if sample else P
        tf = self.tmpf
        cnt = 0
        for s_ in range(nsub):
            for b in range(3):
                ps = self.psum()
                for k in range(KD):
                    S.op("pe", lambda e, k=k, ps=ps, b=b: e.matmul(ps[0:TS, 0:512], lhsT=hT[:, k, s_ * P:s_ * P + TS], rhs=skv[b][:, k * 512:(k + 1) * 512],
                                                                   start=(k == 0), stop=(k == KD - 1)), reads=[skv[b].res, hy.res], writes=[ps.res])
                j = cnt % 2
                cnt += 1
                S.op("act", lambda e, ps=ps, j=j: e.activation(out=tf[0:TS, j, :], in_=ps[0:TS, 0:512], func=AF.Copy), reads=[ps.res], writes=[self.tmpres[j]])
                if sample:
                    outs = [self.cmp_s[:, :], self.slc_s[:, :], self.win_s[:, 511, :]][b]
                    S.dma("sp", [(outs, tf[0:TS, j, :])], reads=[self.tmpres[j]])
                    if b > 0:
                        S.op("act", lambda e, ps=ps, b=b: e.activation(out=self.PGn[0:TS, b - 1, :], in_=ps[0:TS, 0:512], func=AF.Copy), reads=[ps.res], writes=[self.PGn.res])
                else:
                    r0 = t0 + s_ * P
                    if b == 0:
                        if sub >= 4:
                            S.dma("sp", [(self.cmp_p[r0:r0 + P, :], tf[:, j, :])], reads=[self.tmpres[j]])
                    elif b == 1:
                        if sub >= 4:
                            S.dma("sp", [(self.slc_p[r0:r0 + P, :], tf[:, j, :])], reads=[self.tmpres[j]])
                        kt = r0 // P
                        if sub >= 5:
                            S.op("act", lambda e, ps=ps, kt=kt: e.activation(out=self.Vst[:, :, 0:64], in_=ps[:, 256:512].rearrange("p (g d) -> p g d", g=4), func=AF.Copy),
                                 reads=[ps.res], writes=[self.Vst.res])
                            S.dma("sp", [(self.vscr[g_, :, kt, :], self.Vst[:, g_, :]) for g_ in range(4)], reads=[self.Vst.res], writes=[self.vscr_res])
                    else:
                        if t0 == cfg.SEQ - 512 and sub >= 4:
                            S.dma("sp", [(self.win_p[s_ * P:(s_ + 1) * P, :], tf[:, j, :])], reads=[self.tmpres[j]])
                        if sub >= 5:
                            S.op("act", lambda e, ps=ps: e.activation(out=self.Vw[:, slot, s_, :, 0:64], in_=ps[:, 256:512].rearrange("p (g d) -> p g d", g=4), func=AF.Copy),
                                 reads=[ps.res], writes=[self.Vw.res])
            if sub < 6:
                continue
            ps = self.psum()
            for k in range(KD):
                S.op("pe", lambda e, k=k, ps=ps: e.matmul(ps[0:TS, 0:48], lhsT=hT[:, k, s_ * P:s_ * P + TS], rhs=sg[:, k * 48:(k + 1) * 48],
                                                          start=(k == 0), stop=(k == KD - 1)), reads=[sg.res, hy.res], writes=[ps.res])
            S.op("act", lambda e, ps=ps: e.activation(out=self.gates[0:TS, s_, :], in_=ps[0:TS, 0:48], func=AF.Sigmoid), reads=[ps.res], writes=[self.gates.res])

    def nsa_compress(self, nblk, xt_fn, xt_res, k_out, v_out_hid, hid_res=None):
        S = self.S
        pc = self.psum()
        nb2 = 2 * nblk
        for z in range(2):
            for lh in range(2):
                for g2 in range(2):
                    slab = self.w1slab(z, lh, g2)
                    for l in range(32):
                        la = lh * 32 + l
                        S.op("pe", lambda e, z=z, l=l, la=la, g2=g2, slab=slab: e.matmul(
                            pc[:, (z * 2 + g2) * nb2:(z * 2 + g2 + 1) * nb2], lhsT=slab[:, l * P:(l + 1) * P], rhs=xt_fn(g2, z, la),
                            start=(la == 0 and z == 0 and g2 == 0), stop=(la == 63), skip_group_check=True), reads=[slab.res, xt_res], writes=[pc.res])
        for g2 in range(2):
            S.op("act", lambda e, g2=g2: e.activation(out=self.hidK[:, g2, :, 0:nblk], in_=pc[:, g2 * nb2:(g2 + 1) * nb2].rearrange("p (a b) -> p a b", a=2),
                                                      func=AF.Silu, bias=self.pebias[:, 0:1], scale=1.0), reads=[pc.res, self.pebias.res], writes=[self.hidK.res])
            S.op("act", lambda e, g2=g2: e.activation(out=v_out_hid(g2), in_=pc[:, (2 + g2) * nb2:(3 + g2) * nb2].rearrange("p (a b) -> p a b", a=2),
                                                      func=AF.Silu, bias=self.pebias[:, 1:2], scale=1.0), reads=[pc.res, self.pebias.res], writes=[hid_res or self.hidV.res])
        for gp in range(2):
            ps = self.psum()
            for g2 in range(2):
                S.op("pe", lambda e, g2=g2, gp=gp, ps=ps: e.matmul(ps[:, 0:nblk], lhsT=self.W2K[:, g2, :], rhs=self.hidK[:, g2, gp, 0:nblk], start=(g2 == 0), stop=(g2 == 1)),
                     reads=[self.cres, self.hidK.res], writes=[ps.res])
            k_out(gp, ps)

    def nsa_attend(self, N, groups):
        raise NotImplementedError

    def nsa(self, it, N, sample):
        if sample:
            return self.nsa_sample(N)
        S = self.S
        cfg = self.cfg
        t0 = it * 512
        slot = it % 2
        hT, hy, act = self.hT, self.hy, self.act
        qT = act
        S.dma("pool", [(self.mk[:, :, :, :], self.mk_d[it])], writes=[self.mk.res])
        S.dma("pool", [(self.negmT[:, 0, 4, :], self.cmpbt_d[it])], writes=[self.negmT.res])
        stage = int(os.environ.get("NSA_STAGE", "9"))
        if stage < 1:
            return self.nsa_sample(N)
        self.nsa_project(N, False, t0, slot)
        if stage < 2:
            return self.nsa_sample(N)
        S.op("pool", lambda e: e.memset(self.hidV[:, :, :], 0.0), writes=[self.hidV.res])
        XT = self.XT

        def xt_fn(half, z, l):
            return XT[:, z * 2:z * 2 + 2, l:512:64]

        def k_out(gp, ps):
            S.op("act", lambda e: e.activation(out=self.KcT[:, gp, it * 8:it * 8 + 8], in_=ps[:, 0:8], func=AF.Copy), reads=[ps.res], writes=[self.KcT.res])
        self.nsa_compress(8, xt_fn, XT.res, k_out, lambda g2: self.hidV[:, g2:4:2, it * 8:it * 8 + 8])
        ps = self.psum()
        for g in range(4):
            S.op("pe", lambda e, g=g: e.matmul(ps[:, g * 64:(g + 1) * 64], lhsT=self.hidV[:, g, :], rhs=self.W2V[:, :], start=True, stop=True),
                 reads=[self.hidV.res, self.cres], writes=[ps.res])
        S.op("dve", lambda e: e.tensor_tensor(out=self.Vc32[:, :], in0=self.Vc32[:, :], in1=ps[:, 0:256], op=ALU.add),
             reads=[ps.res, self.Vc32.res], writes=[self.Vc32.res])
        S.op("act", lambda e: e.activation(out=self.Vc[:, :, 0:64], in_=self.Vc32[:, :].rearrange("p (g d) -> p g d", g=4), func=AF.Copy),
             reads=[self.Vc32.res], writes=[self.Vc.res])
        if stage < 3:
            return self.nsa_sample(N)
        tk1, tk2, tks = self.tk1, self.tk2, self.tks
        mk = self.mk
        for s_ in range(4):
            for g in range(4):
                half, gp = g % 2, g // 2
                ps = self.psum()
                for r in range(4):
                    ch = 4 * gp + r
                    S.op("pe", lambda e, r=r, ch=ch, ps=ps: e.matmul(ps[:, r * P:(r + 1) * P], lhsT=qT[:, 8 * half + ch, s_ * P:(s_ + 1) * P],
                                                                     rhs=self.KcT[:, gp, :], start=True, stop=True),
                         reads=[self.actres, self.KcT.res], writes=[ps.res])
                S.op("dve", lambda e, ps=ps: e.tensor_tensor(out=tk1[:, :, :], in0=ps[:, 0:512].rearrange("p (r n) -> p r n", r=4),
                                                             in1=mk[:, 0, s_, :].unsqueeze(1).to_broadcast([P, 4, P]), op=ALU.add),
                     reads=[ps.res, mk.res], writes=[tk1.res])
                for r in range(4):
                    S.op("act", lambda e, r=r: e.activation(out=tk1[:, r, :], in_=tk1[:, r, :], func=AF.Exp, accum_out=tks[:, r:r + 1]),
                         reads=[tk1.res], writes=[tk1.res, tks.res])
                S.op("dve", lambda e: e.tensor_scalar(out=tks[:, 4:8], in0=tks[:, 0:4], scalar1=1e-30, scalar2=None, op0=ALU.max), reads=[tks.res], writes=[tks.res])
                S.op("dve", lambda e: e.reciprocal(out=tks[:, 4:8], in_=tks[:, 4:8]), reads=[tks.res], writes=[tks.res])
                S.op("dve", lambda e: e.tensor_scalar(out=tk2[:, 0, :], in0=tk1[:, 0, :], scalar1=tks[:, 4:5], scalar2=None, op0=ALU.mult),
                     reads=[tk1.res, tks.res], writes=[tk2.res])
                for r in range(1, 4):
                    S.op("dve", lambda e, r=r: e.scalar_tensor_tensor(out=tk2[:, 0, :], in0=tk1[:, r, :], scalar=tks[:, 4 + r:5 + r], in1=tk2[:, 0, :], op0=ALU.mult, op1=ALU.add),
                         reads=[tk1.res, tks.res, tk2.res], writes=[tk2.res])
                S.op("dve", lambda e: e.tensor_tensor(out=tk2[:, 0, :], in0=tk2[:, 0, :], in1=mk[:, 1, s_, :], op=ALU.mult), reads=[tk2.res, mk.res], writes=[tk2.res])
                S.op("dve", lambda e: e.tensor_tensor(out=tk2[:, 0, :], in0=tk2[:, 0, :], in1=mk[:, 2, s_, :], op=ALU.add), reads=[tk2.res, mk.res], writes=[tk2.res])
                S.op("dve", lambda e: e.max(out=tks[:, 8:16], in_=tk2[:, 0, :]), reads=[tk2.res], writes=[tks.res])
                S.op("dve", lambda e: e.match_replace(out=tk2[:, 1, :], in_to_replace=tks[:, 8:16], in_values=tk2[:, 0, :], imm_value=-2.0),
                     reads=[tk2.res, tks.res], writes=[tk2.res])
                S.op("dve", lambda e: e.max(out=tks[:, 16:24], in_=tk2[:, 1, :]), reads=[tk2.res], writes=[tks.res])
                S.op("dve", lambda e: e.tensor_scalar(out=tks[:, 24:25], in0=tks[:, 23:24], scalar1=-0.5, scalar2=None, op0=ALU.max), reads=[tks.res], writes=[tks.res])
                S.op("dve", lambda e: e.tensor_scalar(out=self.negm[:, :], in0=tk2[:, 0, :], scalar1=tks[:, 24:25], scalar2=1.0, op0=ALU.is_ge, op1=ALU.subtract),
                     reads=[tk2.res, tks.res], writes=[self.negm.res])
                pst = self.psum()
                pv = pst.t[:].bitcast(BF16)
                S.op("pe", lambda e, pv=pv, pst=pst: e.transpose(pv[:, 0:P], self.negm[:, :], self.ident_bf[:, :]), reads=[self.negm.res, self.cres], writes=[pst.res])
                S.op("act", lambda e, pv=pv, pst=pst, g=g: e.activation(out=self.negmT[0:64, 0, g, s_ * P:(s_ + 1) * P], in_=pv[0:64, 0:P], func=AF.Copy),
                     reads=[pst.res], writes=[self.negmT.res])
                S.op("act", lambda e, pv=pv, pst=pst, g=g: e.activation(out=self.negmT[64:128, 1, g, s_ * P:(s_ + 1) * P], in_=pv[64:128, 0:P], func=AF.Copy),
                     reads=[pst.res], writes=[self.negmT.res])
        if stage < 4:
            return self.nsa_sample(N)
        sE = self.ws.next(("E64",), 4096, lambda w: e64_host())
        sC = self.ws.next(("caus",), 4096, lambda w: caus_host())
        o_tm = self.xin
        pT = self.tmpb
        npt = 0
        for h in range(16):
            g = h // 4
            ch, half = self.head_loc(h)
            gp = g // 2
            hs = slice(0, P)
            qh = qT[:, 8 * half + ch, 0:512]
            for br in range(3):
                tiles = []
                if br == 0:
                    tiles.append((self.KcT[hs, gp, :], self.KcT.res, self.Vc[:, g, 0:65], self.Vc.res,
                                  [(self.ident_bf[:, :], self.negmT[:, 0, 4, :])]))
                elif br == 1:
                    if h % 4 == 0:
                        nkt = 4 * it + 4
                        S.dma("sp", [(self.Vs[:, 0:nkt, :], self.vscr[g, :, 0:nkt, :])], reads=[self.vscr_res], writes=[self.Vs.res])
                    for kt in range(4 * it + 4):
                        b64, v = kt // 32, kt % 32
                        ms = [(sE[:, v * P:(v + 1) * P], self.negmT[:, b64, g, :])]
                        if kt >= 4 * it:
                            j = kt - 4 * it
                            ms.append((self.ident_bf[:, :], sC[:, j * 512:(j + 1) * 512]))
                        tiles.append((self.KTs[hs, gp, kt * P:(kt + 1) * P], self.KTs.res, self.Vs[:, kt, 0:65], self.Vs.res, ms))
                else:
                    if it > 0:
                        for j in range(4):
                            tiles.append((self.KTw[hs, 1 - slot, gp, j * P:(j + 1) * P], self.KTw.res, self.Vw[:, 1 - slot, j, g, 0:65], self.Vw.res,
                                          [(self.ident_bf[:, :], sC[:, (4 + j) * 512:(5 + j) * 512])]))
                    for j in range(4):
                        tiles.append((self.KTw[hs, slot, gp, j * P:(j + 1) * P], self.KTw.res, self.Vw[:, slot, j, g, 0:65], self.Vw.res,
                                      [(self.ident_bf[:, :], sC[:, j * 512:(j + 1) * 512])]))
                po = self.pacc[self.paidx % 2]
                self.paidx += 1
                nt_ = len(tiles)

                def emit_pv(ti, pj, v_ap, vres, po=po, nt_=nt_):
                    for qs in range(4):
                        S.op("pe", lambda e, qs=qs, pj=pj, v_ap=v_ap, ti=ti: e.matmul(po[:, qs * 65:(qs + 1) * 65], lhsT=pT[:, pj, qs * P:(qs + 1) * P], rhs=v_ap,
                                                                                     start=(ti == 0 and qs == 0), stop=(ti == nt_ - 1), skip_group_check=True),
                             reads=[self.tmpbres[pj], vres], writes=[po.res])
                pend = None
                for ti, (kt_ap, kres, v_ap, vres, ms) in enumerate(tiles):
                    ps = self.psum()
                    S.op("pe", lambda e, ps=ps, kt_ap=kt_ap: e.matmul(ps[:, 0:512], lhsT=kt_ap, rhs=qh, start=True, stop=False),
                         reads=[kres, self.actres], writes=[ps.res])
                    for mi, (ml, mr) in enumerate(ms):
                        S.op("pe", lambda e, ps=ps, ml=ml, mr=mr, mi=mi, nm=len(ms): e.matmul(ps[:, 0:512], lhsT=ml, rhs=mr, start=False, stop=(mi == nm - 1)),
                             reads=[self.cres, self.negmT.res, sE.res, sC.res], writes=[ps.res])
                    pj = npt % 2
                    npt += 1
                    S.op("act", lambda e, ps=ps, pj=pj: e.activation(out=pT[:, pj, :], in_=ps[:, 0:512], func=AF.Exp), reads=[ps.res], writes=[self.tmpbres[pj]])
                    if pend is not None:
                        emit_pv(*pend)
                    pend = (ti, pj, v_ap, vres)
                emit_pv(*pend)
                f4 = self.f4
                pov = po[:, 0:260].rearrange("p (q c) -> p q c", q=4)
                S.op("dve", lambda e, pov=pov: e.tensor_scalar(out=f4[:, 0:4], in0=pov[:, :, 64], scalar1=1e-30, scalar2=None, op0=ALU.max), reads=[po.res], writes=[f4.res])
                S.op("dve", lambda e: e.reciprocal(out=f4[:, 0:4], in_=f4[:, 0:4]), reads=[f4.res], writes=[f4.res])
                S.op("dve", lambda e, br=br, h=h: e.tensor_tensor(out=f4[:, 4:8], in0=f4[:, 0:4], in1=self.gates[:, :, br * 16 + h], op=ALU.mult),
                     reads=[f4.res, self.gates.res], writes=[f4.res])
                for qs in range(4):
                    if br == 0:
                        S.op("dve", lambda e, qs=qs, pov=pov, h=h: e.tensor_scalar(out=o_tm[:, qs, h * 64:(h + 1) * 64], in0=pov[:, qs, 0:64], scalar1=f4[:, 4 + qs:5 + qs],
                                                                                 scalar2=None, op0=ALU.mult), reads=[po.res, f4.res], writes=[hy.res])
                    else:
                        S.op("dve", lambda e, qs=qs, pov=pov, h=h: e.scalar_tensor_tensor(out=o_tm[:, qs, h * 64:(h + 1) * 64], in0=pov[:, qs, 0:64], scalar=f4[:, 4 + qs:5 + qs],
                                                                                        in1=o_tm[:, qs, h * 64:(h + 1) * 64], op0=ALU.mult, op1=ALU.add),
                             reads=[po.res, f4.res, hy.res], writes=[hy.res])
        for k in range(KD):
            ps = self.psum()
            for qs in range(4):
                S.op("pe", lambda e, k=k, qs=qs, ps=ps: e.transpose(ps[:, qs * P:(qs + 1) * P], o_tm[:, qs, k * P:(k + 1) * P], self.ident[:, :]),
                     reads=[hy.res, self.cres], writes=[ps.res])
            S.op("act", lambda e, k=k, ps=ps: e.activation(out=act[:, k, 0:512], in_=ps[:, 0:512], func=AF.Copy), reads=[ps.res], writes=[self.actres])
        self.mm_fm(("nwo",), lambda w: w["nsa_w_out"][0], KD, KD, lambda k: act[:, k, 0:N], self.actres, N, self.evac_y(N))

    def samp_page(self, b, v_src, v_res, kt_buf, masks, first, last, po, psm, pg_ap=None, pg_res=None):
        S = self.S
        act = self.act
        if kt_buf is None:
            kt_buf = self.KTp[self.kti % 2]
            for gp in range(2):
                ps = self.psum()
                S.op("pe", lambda e, gp=gp, ps=ps: e.transpose(ps[:, 0:P], pg_ap[:, gp * P:(gp + 1) * P], self.ident[:, :]), reads=[pg_res, self.cres], writes=[ps.res])
                S.op("act", lambda e, gp=gp, ps=ps: e.activation(out=kt_buf[:, gp, :], in_=ps[:, 0:P], func=AF.Copy), reads=[ps.res], writes=[kt_buf.res])
        Vp = self.Vp[0]
        pT = self.pTs[self.kti % 2]
        self.kti += 1
        vv = v_src.rearrange("p (g d) -> p g d", g=4)
        S.op("act", lambda e: e.activation(out=Vp[:, :, 0, 0:64], in_=vv, func=AF.Copy), reads=[v_res], writes=[Vp.res])
        S.op("act", lambda e: e.activation(out=Vp[:, :, 1, 64:128], in_=vv, func=AF.Copy), reads=[v_res], writes=[Vp.res])
        ps = self.psum()
        for g in range(4):
            half, gp = g % 2, g // 2
            q_ap = act[:, 8 * half + 4 * gp:8 * half + 4 * gp + 4, b]
            S.op("pe", lambda e, g=g, gp=gp, q_ap=q_ap: e.matmul(ps[:, g * 4:(g + 1) * 4], lhsT=kt_buf[:, gp, :], rhs=q_ap, start=True, stop=(len(masks) == 0)),
                 reads=[kt_buf.res, self.actres], writes=[ps.res])
            for mi, (ml, mr, mres) in enumerate(masks):
                S.op("pe", lambda e, g=g, ml=ml, mr=mr, mi=mi: e.matmul(ps[:, g * 4:(g + 1) * 4], lhsT=ml, rhs=mr(g), start=False, stop=(mi == len(masks) - 1)),
                     reads=[self.cres] + mres, writes=[ps.res])
        S.op("act", lambda e: e.activation(out=pT[:, :], in_=ps[:, 0:16], func=AF.Exp), reads=[ps.res], writes=[pT.res])
        for g in range(4):
            for var in range(2):
                rhs = pT[:, g * 4 + var:g * 4 + 4:2]
                S.op("pe", lambda e, g=g, var=var, rhs=rhs: e.matmul(po[:, 2 * g:2 * g + 2], lhsT=Vp[:, g, var, :], rhs=rhs,
                                                                     start=(first and g == 0 and var == 0), stop=last, skip_group_check=True),
                     reads=[Vp.res, pT.res], writes=[po.res])
                S.op("pe", lambda e, g=g, var=var, rhs=rhs: e.matmul(psm[:, 2 * g:2 * g + 2], lhsT=self.OnesV[:, var, :], rhs=rhs,
                                                                     start=(first and g == 0 and var == 0), stop=last, skip_group_check=True),
                     reads=[self.cres, pT.res], writes=[psm.res])
        return pT

    def nsa_sample(self, N):
        S = self.S
        cfg = self.cfg
        NS, NPG = cfg.NSAMP, self.NPG
        hT, hy, act = self.hT, self.hy, self.act
        stage = int(os.environ.get("NSA_SSTAGE", "9"))
        if stage < 1 or 1 not in cfg.layers or not hasattr(self, "cmp_s"):
            for k in range(KD):
                S.op("act", lambda e, k=k: e.activation(out=self.yT[:, k, 0:N], in_=self.xT[:, k, 0:N], func=AF.Copy),
                     reads=[self.xT.res], writes=[self.hy.res])
            return
        S.op("pool", lambda e: e.memset(self.PGn[:, :, :], 0.0), reads=[self.hy.res], writes=[self.pgn_res])
        self.nsa_project(N, True, 0, 0)
        tf = self.tmpf
        pgi = 0
        res_o, res_s = self.res_o, self.res_s

        def fin(b, br, po, psm):
            S.op("act", lambda e: e.activation(out=res_o[:, br, b, :], in_=po[:, 0:8], func=AF.Copy), reads=[po.res], writes=[res_o.res])
            S.op("act", lambda e: e.activation(out=res_s[:, br, b, :], in_=psm[:, 0:8], func=AF.Copy), reads=[psm.res], writes=[res_s.res])
        newm = lambda b: [(self.ident_bf[:, :], (lambda g, b=b: self.newmask[:, b, :]), [])]
        XTs = self.XTs
        for b in range(NS):
            for grp in range(NPG // 8):
                for pi in range(8):
                    page = grp * 8 + pi
                    j = pgi % 2
                    pgi += 1
                    col = b * NPG + page
                    S.dma_gather(tf[:, j, :], self.cache_c[:, :], self.idxi[:, col:col + 1], reads=[self.idxi.res], writes=[self.tmpres[j]])
                    for c in range(4):
                        ps = self.psum()
                        S.op("pe", lambda e, c=c, ps=ps, j=j: e.transpose(ps[:, 0:P], tf[:, j, c * P:(c + 1) * P], self.ident[:, :]),
                             reads=[self.tmpres[j], self.cres], writes=[ps.res])
                        S.op("act", lambda e, c=c, ps=ps, pi=pi: e.activation(out=XTs[:, c, pi * P:(pi + 1) * P], in_=ps[:, 0:P], func=AF.Copy),
                             reads=[ps.res], writes=[XTs.res])

                def k_out(gp, ps, grp=grp):
                    S.op("act", lambda e: e.activation(out=self.KcS[:, gp, grp * 16:(grp + 1) * 16], in_=ps[:, 0:16], func=AF.Copy), reads=[ps.res], writes=[self.KcS.res])
                self.nsa_compress(16, lambda half, z, l: XTs[:, z * 2:z * 2 + 2, l:1024:64], XTs.res, k_out,
                                  lambda g2, grp=grp: self.hidVs[:, g2:4:2, grp * 16:(grp + 1) * 16], hid_res=self.hidVs.res)
            psv = self.psum()
            for g in range(4):
                S.op("pe", lambda e, g=g: e.matmul(psv[:, g * 64:(g + 1) * 64], lhsT=self.hidVs[:, g, :], rhs=self.W2V[:, :], start=True, stop=True),
                     reads=[self.hidVs.res, self.cres], writes=[psv.res])
            po, psm = self.pacc[0], self.pacc[1]
            pT = self.samp_page(b, psv[:, 0:256], psv.res, self.KcS, [], True, True, po, psm)
            fin(b, 0, po, psm)
            psr = self.psum()
            S.op("pe", lambda e: e.matmul(psr[:, 0:16], lhsT=self.ones_bf[:, 0:P], rhs=pT[:, :], start=True, stop=True), reads=[self.cres, pT.res], writes=[psr.res])
            sw = self.sw
            S.op("dve", lambda e: e.tensor_scalar(out=sw[:, 0:16], in0=psr[:, 0:16], scalar1=1e-30, scalar2=None, op0=ALU.max), reads=[psr.res], writes=[sw.res])
            S.op("dve", lambda e: e.reciprocal(out=sw[:, 0:16], in_=sw[:, 0:16]), reads=[sw.res], writes=[sw.res])
            S.op("dve", lambda e: e.tensor_tensor(out=sw[:, 16:32], in0=sw[:, 0:16], in1=pT[:, :], op=ALU.mult), reads=[sw.res, pT.res], writes=[sw.res])
            S.op("dve", lambda e, b=b: e.tensor_reduce(out=self.impT[:, b * 4:(b + 1) * 4], in_=sw[:, 16:32].rearrange("p (g r) -> p g r", g=4),
                                                       axis=mybir.AxisListType.X, op=ALU.add), reads=[sw.res], writes=[self.impT.res])
        R16 = 4 * NS
        pst = self.psum()
        S.op("pe", lambda e: e.transpose(pst[0:R16, 0:P], self.impT[:, :], self.ident[:, :]), reads=[self.impT.res, self.cres], writes=[pst.res])
        sc, sk = self.sc16, self.sk16
        S.op("act", lambda e: e.activation(out=sc[:, 0, :], in_=pst[0:R16, 0:P], func=AF.Copy), reads=[pst.res], writes=[sc.res])
        S.op("dve", lambda e: e.memset(sc[:, 0, 0:1], 1.0e4), reads=[sc.res], writes=[sc.res])
        S.op("dve", lambda e: e.memset(sc[:, 0, P - 1:P], 1.0e4), reads=[sc.res], writes=[sc.res])
        S.op("dve", lambda e: e.max(out=sk[:, 0:8], in_=sc[:, 0, :]), reads=[sc.res], writes=[sk.res])
        S.op("dve", lambda e: e.match_replace(out=sc[:, 1, :], in_to_replace=sk[:, 0:8], in_values=sc[:, 0, :], imm_value=-2.0), reads=[sc.res, sk.res], writes=[sc.res])
        S.op("dve", lambda e: e.max(out=sk[:, 8:16], in_=sc[:, 1, :]), reads=[sc.res], writes=[sk.res])
        S.op("dve", lambda e: e.tensor_scalar(out=self.ng16[:, :], in0=sc[:, 0, :], scalar1=sk[:, 14:15], scalar2=1.0, op0=ALU.is_ge, op1=ALU.subtract),
             reads=[sc.res, sk.res], writes=[self.ng16.res])
        pst2 = self.psum()
        pv = pst2.t[:].bitcast(BF16)
        S.op("pe", lambda e: e.transpose(pv[:, 0:R16], self.ng16[:, :], self.ident_bf[0:R16, 0:R16]), reads=[self.ng16.res, self.cres], writes=[pst2.res])
        S.op("act", lambda e: e.activation(out=self.negS[0:64, 0, :], in_=pv[0:64, 0:R16], func=AF.Copy), reads=[pst2.res], writes=[self.negS.res])
        S.op("act", lambda e: e.activation(out=self.negS[64:128, 1, :], in_=pv[64:128, 0:R16], func=AF.Copy), reads=[pst2.res], writes=[self.negS.res])
        sE = self.ws.next(("E64",), 4096, lambda w: e64_host())
        for b in range(NS):
            po, psm = self.pacc[0], self.pacc[1]
            for page in range(NPG):
                j = pgi % 2
                pgi += 1
                col = b * NPG + page
                S.dma_gather(tf[:, j, :], self.cache_s[:, :], self.idxi[:, col:col + 1], reads=[self.idxi.res], writes=[self.tmpres[j]])
                b64, v = page // 32, page % 32
                ms = [(sE[:, v * P:(v + 1) * P], (lambda g, b=b, b64=b64: self.negS[:, b64, b * 4 + g:b * 4 + g + 1].to_broadcast([P, 4])), [sE.res, self.negS.res])]
                self.samp_page(b, tf[:, j, 256:512], self.tmpres[j], None, ms, page == 0, False, po, psm, pg_ap=tf[:, j, :], pg_res=self.tmpres[j])
            self.samp_page(b, self.PGn[:, 0, 256:512], self.PGn.res, None, newm(b), False, True, po, psm, pg_ap=self.PGn[:, 0, :], pg_res=self.PGn.res)
            fin(b, 1, po, psm)
        for b in range(NS):
            po, psm = self.pacc[0], self.pacc[1]
            for page in range(4):
                j = pgi % 2
                pgi += 1
                S.dma("pool", [(tf[:, j, :], self.cache_w[b, page * P:(page + 1) * P, :])], writes=[self.tmpres[j]])
                self.samp_page(b, tf[:, j, 256:512], self.tmpres[j], None, [], page == 0, False, po, psm, pg_ap=tf[:, j, :], pg_res=self.tmpres[j])
            self.samp_page(b, self.PGn[:, 1, 256:512], self.PGn.res, None, newm(b), False, True, po, psm, pg_ap=self.PGn[:, 1, :], pg_res=self.PGn.res)
            fin(b, 2, po, psm)
            S.dma("sp", [(self.win_s[b, 0:511, :], self.cache_w[b, 1:512, :])], reads=[self.wcopy_res])
        S.op("dve", lambda e: e.tensor_scalar(out=res_s[:, :, :, :], in0=res_s[:, :, :, :], scalar1=1e-30, scalar2=None, op0=ALU.max), reads=[res_s.res], writes=[res_s.res])
        S.op("dve", lambda e: e.reciprocal(out=res_s[:, :, :, :], in_=res_s[:, :, :, :]), reads=[res_s.res], writes=[res_s.res])
        S.op("dve", lambda e: e.tensor_tensor(out=res_o[:, :, :, :], in0=res_o[:, :, :, :], in1=res_s[:, :, :, :], op=ALU.mult), reads=[res_s.res, res_o.res], writes=[res_o.res])
        gv = self.gates[0:NS, 0, :].rearrange("p (a c h) -> p a c h", a=3, c=8)
        sw = self.sw
        for b in range(NS):
            pg_ = self.psum()
            for hf in range(2):
                S.op("pe", lambda e, b=b, hf=hf: e.matmul(pg_[:, 0:24], lhsT=self.Sel[0:NS, b, hf, :], rhs=gv[:, :, :, hf], start=(hf == 0), stop=(hf == 1)),
                     reads=[self.cres, self.gates.res], writes=[pg_.res])
            S.op("dve", lambda e, b=b: e.tensor_tensor(out=sw[:, 32:56].rearrange("p (a c) -> p a c", a=3), in0=res_o[:, :, b, :],
                                                       in1=pg_[:, 0:24].rearrange("p (a c) -> p a c", a=3), op=ALU.mult),
                 reads=[res_o.res, pg_.res], writes=[sw.res])
            S.op("dve", lambda e: e.tensor_tensor(out=sw[:, 32:40], in0=sw[:, 32:40], in1=sw[:, 40:48], op=ALU.add), reads=[sw.res], writes=[sw.res])
            S.op("dve", lambda e: e.tensor_tensor(out=sw[:, 32:40], in0=sw[:, 32:40], in1=sw[:, 48:56], op=ALU.add), reads=[sw.res], writes=[sw.res])
            S.op("act", lambda e, b=b: e.activation(out=act[:, 0:8, b], in_=sw[:, 32:40], func=AF.Copy), reads=[sw.res], writes=[self.actres])
        self.mm_fm(("nwo",), lambda w: w["nsa_w_out"][0], KD, KD, lambda k: act[:, k, 0:N], self.actres, N, self.evac_y(N))

    def s5_setup(self):
        S = self.S
        cfg = self.cfg
        NS = cfg.NSAMP
        TC = 16
        self.TC = TC

        def st_layout(a):
            return np.asarray(a).reshape(32, 2, 64).transpose(1, 2, 0).reshape(P, 32)
        d_lr = self.din("s5_lr", [P, 32], lambda I, c: st_layout(I["ssm_lambda_re"][0]))
        d_li = self.din("s5_li", [P, 32], lambda I, c: st_layout(I["ssm_lambda_im"][0]))
        d_ls = self.din("s5_ls", [P, 32], lambda I, c: st_layout(np.broadcast_to(I["ssm_log_step"][0][:, None], (64, 64))))
        W = self.sb("s5w", [P, 24, 32], F32)
        self.s5res = Res("s5const")
        W.res = self.s5res
        S.dma("sp", [(W[:, 0, :], d_lr[:, :]), (W[:, 1, :], d_li[:, :]), (W[:, 2, :], d_ls[:, :])], writes=[W.res])
        LR, LI, DT, A_, TH, X, X2, SN, CS, T1, T2, MAG, ABR, ABI, DEN, FR, FI = range(17)
        LR, LI, LS = 0, 1, 2
        DT, A_, TH, X, X2, SN, CS, T1, T2, MAG, ABR, ABI, DEN, FR, FI = range(3, 18)

        def dve(fn):
            S.op("dve", fn, reads=[W.res], writes=[W.res])

        def tt(o, a, b, op):
            dve(lambda e: e.tensor_tensor(out=W[:, o, :], in0=W[:, a, :], in1=W[:, b, :], op=op))

        def ts(o, a, s1, op0, s2=None, op1=None):
            if op1 is None:
                dve(lambda e: e.tensor_scalar(out=W[:, o, :], in0=W[:, a, :], scalar1=s1, scalar2=None, op0=op0))
            else:
                dve(lambda e: e.tensor_scalar(out=W[:, o, :], in0=W[:, a, :], scalar1=s1, scalar2=s2, op0=op0, op1=op1))
        S.op("act", lambda e: e.activation(out=W[:, DT, :], in_=W[:, LS, :], func=AF.Exp), reads=[W.res], writes=[W.res])
        tt(A_, LR, DT, ALU.mult)
        tt(TH, LI, DT, ALU.mult)
        ts(MAG, A_, 1.0 / 5, ALU.mult, 1.0, ALU.add)
        for dd in (4.0, 3.0, 2.0, 1.0):
            tt(MAG, MAG, A_, ALU.mult)
            ts(MAG, MAG, 1.0 / dd, ALU.mult, 1.0, ALU.add)
        ts(X, TH, 1.0 / 16, ALU.mult)
        tt(X2, X, X, ALU.mult)
        ts(SN, X2, -1.0 / 156, ALU.mult, 1.0, ALU.add)
        for dd in (110.0, 72.0, 42.0, 20.0, 6.0):
            tt(SN, SN, X2, ALU.mult)
            ts(SN, SN, -1.0 / dd, ALU.mult, 1.0, ALU.add)
        tt(SN, SN, X, ALU.mult)
        ts(CS, X2, -1.0 / 182, ALU.mult, 1.0, ALU.add)
        for dd in (132.0, 90.0, 56.0, 30.0, 12.0, 2.0):
            tt(CS, CS, X2, ALU.mult)
            ts(CS, CS, -1.0 / dd, ALU.mult, 1.0, ALU.add)

        def double(c, s_):
            tt(T1, s_, c, ALU.mult)
            tt(T2, s_, s_, ALU.mult)
            ts(s_, T1, 2.0, ALU.mult)
            ts(c, T2, -2.0, ALU.mult, 1.0, ALU.add)
        for _ in range(4):
            double(CS, SN)
        tt(ABR, MAG, CS, ALU.mult)
        tt(ABI, MAG, SN, ALU.mult)
        tt(T1, LR, LR, ALU.mult)
        tt(T2, LI, LI, ALU.mult)
        tt(DEN, T1, T2, ALU.add)
        dve(lambda e: e.reciprocal(out=W[:, DEN, :], in_=W[:, DEN, :]))
        ts(T1, ABR, -1.0, ALU.add)
        tt(FR, T1, LR, ALU.mult)
        tt(T2, ABI, LI, ALU.mult)
        tt(FR, FR, T2, ALU.add)
        tt(FR, FR, DEN, ALU.mult)
        tt(FI, ABI, LR, ALU.mult)
        tt(T2, T1, LI, ALU.mult)
        tt(FI, FI, T2, ALU.subtract)
        tt(FI, FI, DEN, ALU.mult)
        self.s5W = W
        self.s5i = dict(MAG=MAG, CS=CS, SN=SN, ABR=ABR, ABI=ABI, FR=FR, FI=FI)
        self.cosT = self.sb("cosT", [P, 32, TC], F32)
        self.sinT = self.sb("sinT", [P, 32, TC], F32)
        self.rhoT = self.sb("rhoT", [P, 32, TC], F32)
        for bb in (self.cosT, self.sinT, self.rhoT):
            bb.res = self.s5res
        cT, sT, rT = self.cosT, self.sinT, self.rhoT
        dve(lambda e: e.memset(cT[:, :, 0:1], 1.0))
        dve(lambda e: e.memset(sT[:, :, 0:1], 0.0))
        dve(lambda e: e.tensor_copy(out=cT[:, :, 1], in_=W[:, CS, :]))
        dve(lambda e: e.tensor_copy(out=sT[:, :, 1], in_=W[:, SN, :]))
        for t in range(TC):
            dve(lambda e, t=t: e.tensor_copy(out=rT[:, :, t], in_=W[:, MAG, :]))
        WC, WS_, T3, T4 = 18, 19, 20, 21
        dve(lambda e: e.tensor_copy(out=W[:, WC, :], in_=W[:, CS, :]))
        dve(lambda e: e.tensor_copy(out=W[:, WS_, :], in_=W[:, SN, :]))
        tmpc = Buf.__new__(Buf)
        tmpc.t = self.act.t[:, 8:10, :].rearrange("p k t -> p (k t)").bitcast(F32).rearrange("p (c t) -> p c t", t=TC)
        tmpc.res = self.s5res
        n = 2
        while n < TC:
            double(WC, WS_)
            wc = W[:, WC, :].unsqueeze(2).to_broadcast([P, 32, n])
            wsn = W[:, WS_, :].unsqueeze(2).to_broadcast([P, 32, n])
            dve(lambda e, n=n, wc=wc: e.tensor_tensor(out=cT[:, :, n:2 * n], in0=cT[:, :, 0:n], in1=wc, op=ALU.mult))
            dve(lambda e, n=n, wsn=wsn: e.tensor_tensor(out=tmpc[:, :, 0:n], in0=sT[:, :, 0:n], in1=wsn, op=ALU.mult))
            dve(lambda e, n=n, wsn=wsn: e.tensor_tensor(out=sT[:, :, n:2 * n], in0=cT[:, :, 0:n], in1=wsn, op=ALU.mult))
            dve(lambda e, n=n: e.tensor_tensor(out=cT[:, :, n:2 * n], in0=cT[:, :, n:2 * n], in1=tmpc[:, :, 0:n], op=ALU.subtract))
            dve(lambda e, n=n, wc=wc: e.tensor_tensor(out=tmpc[:, :, 0:n], in0=sT[:, :, 0:n], in1=wc, op=ALU.mult))
            dve(lambda e, n=n: e.tensor_tensor(out=sT[:, :, n:2 * n], in0=sT[:, :, n:2 * n], in1=tmpc[:, :, 0:n], op=ALU.add))
            n *= 2
        self.s5tmp = tmpc
        self.s5x = self.sb("s5x", [P, 32, 2], F32)
        S.op("dve", lambda e: e.memset(self.s5x[:, :, :], 0.0), writes=[self.s5x.res])
        self.s5xs = self.sb("s5xs", [P, 32, NS, 2], F32)
        d_st = self.din("s5_st", [P, 32, NS, 2], lambda I, c: I["state_ssm"][0][c * NS:(c + 1) * NS].reshape(NS, 32, 2, 64, 2).transpose(2, 3, 1, 0, 4).reshape(P, 32, NS, 2))
        S.dma("sp", [(self.s5xs[:, :, :, :], d_st[:, :, :, :])], writes=[self.s5xs.res])
        self.s5_dv = None
        self.ssm_p = self.dout("ssm_p", [P, 32, 2])
        self.ssm_s = self.dout("ssm_s", [P, 32, NS, 2])
        self.ssm_anchor = Res("ssm_out")
        S.out_anchors.append(self.ssm_anchor)
        self.s5work = Buf.__new__(Buf)
        self.s5work.t = self.act.t[:, 8:20, :].rearrange("p k t -> p (k t)").bitcast(F32).rearrange("p (r c t) -> p r c t", r=6, c=32)
        self.s5work.res = self.actres
        self.s5xb = Buf.__new__(Buf)
        self.s5xb.t = self.act.t[:, 20:22, :].rearrange("p k t -> p (k t)").rearrange("p (z c t) -> p z c t", z=2, c=32)
        self.s5xb.res = self.actres
        self.s5zi = self.sb("s5zi", [P, 32, 2], F32)
        S.op("dve", lambda e: e.memset(self.rhoT[:, :, 0:1], 0.0), reads=[self.s5res], writes=[self.s5res])
        self.s5wkres = Res("s5wk")
        self.s5xbres = Res("s5xb")
        self.s5xb0 = self.act.t[:, 20:22, :].rearrange("p k t -> p (k t)").rearrange("p (c t) -> p c t", c=32)
        self.s5xb1 = self.sb("s5xb1", [P, 32, 32], BF16)

    def s5_prompt(self, it, N, sBr, sBi, sCr, sCi):
        S = self.S
        TC = self.TC
        hT, hy, act = self.hT, self.hy, self.act
        W, ix, wk = self.s5W, self.s5i, self.s5work
        wkres, xbres = self.s5wkres, self.s5xbres
        xb = [self.s5xb0, self.s5xb1.t]
        gT = act
        S.op("dve", lambda e: e.memset(self.cbar[:, 1:2], 0.0), reads=[self.actres, self.s5xb1.res], writes=[self.actres, wkres, xbres, self.cbar.res])
        banksets = [[self.pbanks[0], self.pbanks[1], self.pbanks[2], self.pbanks[3]],
                    [self.pbanks[4], self.pbanks[5], self.pacc[0], self.pacc[1]]]
        cT, sT, rT = self.cosT, self.sinT, self.rhoT
        c1, s1, mag = W[:, ix["CS"], :], W[:, ix["SN"], :], W[:, ix["MAG"], :]
        xp_, zi_ = self.s5x, self.s5zi
        R = lambda j: wk[:, j, :, :]
        flat = lambda j: wk[:, j, :, :].rearrange("p c t -> p (c t)")
        nsc = N // 32

        def emitB(sc):
            bs = banksets[sc % 2]
            t0 = sc * 32
            for z, slab in ((0, sBr), (1, sBi)):
                for c in range(32):
                    ps = bs[z * 2 + c // 16]
                    S.op("pe", lambda e, c=c, ps=ps, slab=slab: e.matmul(ps[:, (c % 16) * 32:(c % 16 + 1) * 32], lhsT=slab[:, c * P:(c + 1) * P],
                                                                         rhs=hT[:, c // 4, t0:t0 + 32], start=True, stop=True),
                         reads=[slab.res, hy.res], writes=[ps.res])
            return bs

        def dv(fn, extra=()):
            S.op("dve", fn, reads=[wkres, self.s5res] + list(extra), writes=[wkres])

        def mul(o, a, b):
            dv(lambda e: e.tensor_tensor(out=o, in0=a, in1=b, op=ALU.mult))

        def emitDVE(sc, bs):
            for sub in range(2):
                for hb in range(2):
                    Rh = lambda j, hb=hb: wk[:, j, hb * 16:(hb + 1) * 16, :]
                    PSv = lambda z, hb=hb, sub=sub: bs[z * 2 + hb][:, 0:512].rearrange("p (c t) -> p c t", t=32)[:, :, sub * 16:(sub + 1) * 16]
                    frh = W[:, ix["FR"], hb * 16:(hb + 1) * 16].unsqueeze(2).to_broadcast([P, 16, TC])
                    fih = W[:, ix["FI"], hb * 16:(hb + 1) * 16].unsqueeze(2).to_broadcast([P, 16, TC])
                    ex = [bs[hb].res, bs[2 + hb].res]
                    dv(lambda e, Rh=Rh, PSv=PSv, frh=frh: e.tensor_tensor(out=Rh(4), in0=PSv(0), in1=frh, op=ALU.mult), ex)
                    dv(lambda e, Rh=Rh, PSv=PSv, fih=fih: e.tensor_tensor(out=Rh(5), in0=PSv(1), in1=fih, op=ALU.mult), ex)
                    dv(lambda e, Rh=Rh: e.tensor_tensor(out=Rh(0), in0=Rh(4), in1=Rh(5), op=ALU.subtract))
                    dv(lambda e, Rh=Rh, PSv=PSv, frh=frh: e.tensor_tensor(out=Rh(4), in0=PSv(1), in1=frh, op=ALU.mult), ex)
                    dv(lambda e, Rh=Rh, PSv=PSv, fih=fih: e.tensor_tensor(out=Rh(5), in0=PSv(0), in1=fih, op=ALU.mult), ex)
                    dv(lambda e, Rh=Rh: e.tensor_tensor(out=Rh(1), in0=Rh(4), in1=Rh(5), op=ALU.add))
                mul(R(4), R(0), cT[:, :, :])
                mul(R(5), R(1), sT[:, :, :])
                dv(lambda e: e.tensor_tensor(out=R(2), in0=R(4), in1=R(5), op=ALU.add))
                mul(R(4), R(1), cT[:, :, :])
                mul(R(5), R(0), sT[:, :, :])
                dv(lambda e: e.tensor_tensor(out=R(3), in0=R(4), in1=R(5), op=ALU.subtract))
                def dz(fn):
                    S.op("dve", fn, reads=[xp_.res, self.s5res, zi_.res, wkres], writes=[zi_.res, wkres])
                dz(lambda e: e.tensor_tensor(out=wk[:, 4, :, 0], in0=xp_[:, :, 0], in1=c1, op=ALU.mult))
                dz(lambda e: e.tensor_tensor(out=wk[:, 4, :, 1], in0=xp_[:, :, 1], in1=s1, op=ALU.mult))
                dz(lambda e: e.tensor_tensor(out=zi_[:, :, 0], in0=wk[:, 4, :, 0], in1=wk[:, 4, :, 1], op=ALU.subtract))
                dz(lambda e: e.tensor_tensor(out=wk[:, 4, :, 2], in0=xp_[:, :, 1], in1=c1, op=ALU.mult))
                dz(lambda e: e.tensor_tensor(out=wk[:, 4, :, 3], in0=xp_[:, :, 0], in1=s1, op=ALU.mult))
                dz(lambda e: e.tensor_tensor(out=zi_[:, :, 1], in0=wk[:, 4, :, 2], in1=wk[:, 4, :, 3], op=ALU.add))
                dz(lambda e: e.tensor_tensor(out=wk[:, 4, :, 4], in0=zi_[:, :, 0], in1=mag, op=ALU.mult))
                dz(lambda e: e.tensor_tensor(out=wk[:, 4, :, 5], in0=zi_[:, :, 1], in1=mag, op=ALU.mult))
                dz(lambda e: e.tensor_tensor(out=wk[:, 2, :, 0], in0=wk[:, 2, :, 0], in1=wk[:, 4, :, 4], op=ALU.add))
                dz(lambda e: e.tensor_tensor(out=wk[:, 3, :, 0], in0=wk[:, 3, :, 0], in1=wk[:, 4, :, 5], op=ALU.add))
                rflat = rT[:, :, :].rearrange("p c t -> p (c t)")
                dv(lambda e: e.tensor_tensor_scan(out=flat(0), data0=rflat, data1=flat(2), initial=0.0, op0=ALU.mult, op1=ALU.add))
                dv(lambda e: e.tensor_tensor_scan(out=flat(1), data0=rflat, data1=flat(3), initial=0.0, op0=ALU.mult, op1=ALU.add))
                mul(R(4), R(0), cT[:, :, :])
                mul(R(5), R(1), sT[:, :, :])
                dv(lambda e: e.tensor_tensor(out=R(2), in0=R(4), in1=R(5), op=ALU.subtract))
                mul(R(4), R(0), sT[:, :, :])
                mul(R(5), R(1), cT[:, :, :])
                dv(lambda e: e.tensor_tensor(out=R(3), in0=R(4), in1=R(5), op=ALU.add))
                S.op("dve", lambda e: e.tensor_copy(out=xp_[:, :, 0], in_=wk[:, 2, :, TC - 1]), reads=[wkres], writes=[xp_.res])
                S.op("dve", lambda e: e.tensor_copy(out=xp_[:, :, 1], in_=wk[:, 3, :, TC - 1]), reads=[wkres], writes=[xp_.res])
                S.op("act", lambda e, sub=sub: e.activation(out=xb[0][:, :, sub * 16:(sub + 1) * 16], in_=R(2), func=AF.Copy), reads=[wkres], writes=[xbres])
                S.op("act", lambda e, sub=sub: e.activation(out=xb[1][:, :, sub * 16:(sub + 1) * 16], in_=R(3), func=AF.Copy, scale=-1.0), reads=[wkres], writes=[xbres])

        def emitC(sc, bs):
            py = bs[0]
            t0 = sc * 32
            for k in range(KD):
                n_ = 0
                for j in range(4):
                    c = 4 * k + j
                    for z, slab in ((0, sCr), (1, sCi)):
                        S.op("pe", lambda e, k=k, c=c, z=z, slab=slab, n_=n_: e.matmul(py[:, k * 32:(k + 1) * 32], lhsT=slab[:, c * P:(c + 1) * P],
                                                                                       rhs=xb[z][:, c, :], start=(n_ == 0), stop=(n_ == 7)),
                             reads=[slab.res, xbres], writes=[py.res])
                        n_ += 1
            tf = self.tmpf
            S.op("dve", lambda e: e.tensor_tensor(out=tf[:, 0, 0:KD * 32].rearrange("p (k t) -> p k t", t=32), in0=hT[:, :, t0:t0 + 32],
                                                  in1=self.s5d[:, :].unsqueeze(2).to_broadcast([P, KD, 32]), op=ALU.mult),
                 reads=[hy.res, self.cres], writes=[self.tmpres[0]])
            S.op("dve", lambda e: e.tensor_tensor(out=tf[:, 0, 0:KD * 32], in0=tf[:, 0, 0:KD * 32], in1=py[:, 0:KD * 32], op=ALU.add),
                 reads=[self.tmpres[0], py.res], writes=[self.tmpres[0]])
            S.op("act", lambda e: e.activation(out=gT[:, 0:KD, t0:t0 + 32], in_=tf[:, 0, 0:KD * 32].rearrange("p (k t) -> p k t", t=32), func=AF.Gelu),
                 reads=[self.tmpres[0]], writes=[self.actres])

        bs = emitB(0)
        for sc in range(nsc):
            bs_next = emitB(sc + 1) if sc + 1 < nsc else None
            emitDVE(sc, bs)
            emitC(sc, bs)
            bs = bs_next
        S.op("dve", lambda e: e.memset(self.cbar[:, 1:2], 0.0), reads=[wkres, xbres], writes=[self.actres, wkres, xbres, self.cbar.res])

    def s5(self, it, N, sample):
        S = self.S
        cfg = self.cfg
        NS = cfg.NSAMP
        TC = self.TC
        hT, hy, act = self.hT, self.hy, self.act
        W = self.s5W
        ix = self.s5i
        wk = self.s5work

        def padB(w, key):
            b = np.asarray(w[key][0])
            out = np.zeros((P, 32, P), np.float32)
            for c in range(32):
                for g2 in range(2):
                    g = 2 * c + g2
                    r0 = (c % 4) * 32 + g2 * 16
                    out[r0:r0 + 16, c, g2 * 64:(g2 + 1) * 64] = b[g].T
            return out.reshape(P, 32 * P)

        def padC(w, key):
            cm = np.asarray(w[key][0])
            out = np.zeros((P, 32, P), np.float32)
            for c in range(32):
                for g2 in range(2):
                    g = 2 * c + g2
                    f0 = (c % 4) * 32 + g2 * 16
                    out[g2 * 64:(g2 + 1) * 64, c, f0:f0 + 16] = cm[g].T
            return out.reshape(P, 32 * P)
        sBr = self.ws.next(("s5", "br"), 4096, lambda w: padB(w, "ssm_b_re"))
        sBi = self.ws.next(("s5", "bi"), 4096, lambda w: padB(w, "ssm_b_im"))
        sCr = self.ws.next(("s5", "cr"), 4096, lambda w: padC(w, "ssm_c_re"))
        sCi = self.ws.next(("s5", "ci"), 4096, lambda w: padC(w, "ssm_c_im"))
        nch = 1 if sample else 0
        T = N if sample else TC
        gT = act
        if not sample:
            self.s5_prompt(it, N, sBr, sBi, sCr, sCi)
        fr = W[:, ix["FR"], :].unsqueeze(2).to_broadcast([P, 32, T])
        fi = W[:, ix["FI"], :].unsqueeze(2).to_broadcast([P, 32, T])
        for ch in range(nch):
            t0 = ch * TC
            pb = [[self.psum(), self.psum()], [self.psum(), self.psum()]]
            for z, slab in ((0, sBr), (1, sBi)):
                for c in range(32):
                    ps = pb[z][c // 16]
                    S.op("pe", lambda e, c=c, ps=ps, slab=slab: e.matmul(ps[:, (c % 16) * T:(c % 16 + 1) * T], lhsT=slab[:, c * P:(c + 1) * P],
                                                                         rhs=hT[:, c // 4, t0:t0 + T], start=True, stop=True),
                         reads=[slab.res, hy.res], writes=[ps.res])
            def dv(fn):
                S.op("dve", fn, reads=[wk.res, self.s5res], writes=[wk.res])

            def mul(o, a, b):
                dv(lambda e: e.tensor_tensor(out=o, in0=a, in1=b, op=ALU.mult))
            R = lambda j: wk[:, j, :, 0:T]
            for hb in range(2):
                Rh = lambda j, hb=hb: wk[:, j, hb * 16:(hb + 1) * 16, 0:T]
                PSv = lambda z, hb=hb: pb[z][hb][:, 0:16 * T].rearrange("p (c t) -> p c t", t=T)
                frh = W[:, ix["FR"], hb * 16:(hb + 1) * 16].unsqueeze(2).to_broadcast([P, 16, T])
                fih = W[:, ix["FI"], hb * 16:(hb + 1) * 16].unsqueeze(2).to_broadcast([P, 16, T])

                def dp(fn, hb=hb):
                    S.op("dve", fn, reads=[wk.res, self.s5res, pb[0][hb].res, pb[1][hb].res], writes=[wk.res])
                dp(lambda e, Rh=Rh, PSv=PSv, frh=frh: e.tensor_tensor(out=Rh(4), in0=PSv(0), in1=frh, op=ALU.mult))
                dp(lambda e, Rh=Rh, PSv=PSv, fih=fih: e.tensor_tensor(out=Rh(5), in0=PSv(1), in1=fih, op=ALU.mult))
                dp(lambda e, Rh=Rh: e.tensor_tensor(out=Rh(0), in0=Rh(4), in1=Rh(5), op=ALU.subtract))
                dp(lambda e, Rh=Rh, PSv=PSv, frh=frh: e.tensor_tensor(out=Rh(4), in0=PSv(1), in1=frh, op=ALU.mult))
                dp(lambda e, Rh=Rh, PSv=PSv, fih=fih: e.tensor_tensor(out=Rh(5), in0=PSv(0), in1=fih, op=ALU.mult))
                dp(lambda e, Rh=Rh: e.tensor_tensor(out=Rh(1), in0=Rh(4), in1=Rh(5), op=ALU.add))
            if sample:
                xs_ = self.s5xs
                abr = W[:, ix["ABR"], :].unsqueeze(2).to_broadcast([P, 32, NS])
                abi = W[:, ix["ABI"], :].unsqueeze(2).to_broadcast([P, 32, NS])

                def dx(fn):
                    S.op("dve", fn, reads=[wk.res, self.s5res, xs_.res], writes=[wk.res])
                dx(lambda e: e.tensor_tensor(out=R(4), in0=xs_[:, :, :, 0], in1=abr, op=ALU.mult))
                dx(lambda e: e.tensor_tensor(out=R(5), in0=xs_[:, :, :, 1], in1=abi, op=ALU.mult))
                dv(lambda e: e.tensor_tensor(out=R(4), in0=R(4), in1=R(5), op=ALU.subtract))
                dv(lambda e: e.tensor_tensor(out=R(2), in0=R(4), in1=R(0), op=ALU.add))
                dx(lambda e: e.tensor_tensor(out=R(4), in0=xs_[:, :, :, 1], in1=abr, op=ALU.mult))
                dx(lambda e: e.tensor_tensor(out=R(5), in0=xs_[:, :, :, 0], in1=abi, op=ALU.mult))
                dv(lambda e: e.tensor_tensor(out=R(4), in0=R(4), in1=R(5), op=ALU.add))
                dv(lambda e: e.tensor_tensor(out=R(3), in0=R(4), in1=R(1), op=ALU.add))
                xo = self.s5xs
                S.op("dve", lambda e: e.tensor_copy(out=xo[:, :, :, 0], in_=R(2)), reads=[wk.res], writes=[xo.res])
                S.op("dve", lambda e: e.tensor_copy(out=xo[:, :, :, 1], in_=R(3)), reads=[wk.res], writes=[xo.res])
                S.dma("sp", [(self.ssm_s[:, :, :, :], xo[:, :, :, :])], reads=[xo.res])
                xb = self.s5xb
                S.op("dve", lambda e: e.tensor_copy(out=xb[:, 0, :, 0:T], in_=R(2)), reads=[wk.res], writes=[xb.res])
                S.op("dve", lambda e: e.tensor_scalar(out=xb[:, 1, :, 0:T], in0=R(3), scalar1=-1.0, scalar2=None, op0=ALU.mult), reads=[wk.res], writes=[xb.res])
            else:
                cT, sT, rT = self.cosT, self.sinT, self.rhoT
                mul(R(4), R(0), cT[:, :, :])
                mul(R(5), R(1), sT[:, :, :])
                dv(lambda e: e.tensor_tensor(out=R(2), in0=R(4), in1=R(5), op=ALU.add))
                mul(R(4), R(1), cT[:, :, :])
                mul(R(5), R(0), sT[:, :, :])
                dv(lambda e: e.tensor_tensor(out=R(3), in0=R(4), in1=R(5), op=ALU.subtract))
                xp_, zi_ = self.s5x, self.s5zi
                c1, s1 = W[:, ix["CS"], :], W[:, ix["SN"], :]

                def dz(fn):
                    S.op("dve", fn, reads=[xp_.res, self.s5res, zi_.res, wk.res], writes=[zi_.res, wk.res])
                dz(lambda e: e.tensor_tensor(out=wk[:, 4, :, 0], in0=xp_[:, :, 0], in1=c1, op=ALU.mult))
                dz(lambda e: e.tensor_tensor(out=wk[:, 4, :, 1], in0=xp_[:, :, 1], in1=s1, op=ALU.mult))
                dz(lambda e: e.tensor_tensor(out=zi_[:, :, 0], in0=wk[:, 4, :, 0], in1=wk[:, 4, :, 1], op=ALU.subtract))
                dz(lambda e: e.tensor_tensor(out=wk[:, 4, :, 2], in0=xp_[:, :, 1], in1=c1, op=ALU.mult))
                dz(lambda e: e.tensor_tensor(out=wk[:, 4, :, 3], in0=xp_[:, :, 0], in1=s1, op=ALU.mult))
                dz(lambda e: e.tensor_tensor(out=zi_[:, :, 1], in0=wk[:, 4, :, 2], in1=wk[:, 4, :, 3], op=ALU.add))
                zres = [Res("zr"), Res("zi")]
                for z in range(2):
                    for c in range(32):
                        S.op("dve", lambda e, z=z, c=c: e.tensor_tensor_scan(out=wk[:, 0 + z, c, :], data0=rT[:, c, :], data1=wk[:, 2 + z, c, :],
                                                                             initial=zi_[:, c, z:z + 1], op0=ALU.mult, op1=ALU.add),
                             reads=[wk.res, self.s5res, zi_.res], writes=[zres[z]])
                S.op("dve", lambda e: e.tensor_copy(out=wk[:, 5, 0, 0:1], in_=wk[:, 5, 0, 0:1]), reads=[zres[0], zres[1]], writes=[wk.res])
                xb = self.s5xb
                mul(R(4), R(0), cT[:, :, :])
                mul(R(5), R(1), sT[:, :, :])
                dv(lambda e: e.tensor_tensor(out=R(2), in0=R(4), in1=R(5), op=ALU.subtract))
                mul(R(4), R(0), sT[:, :, :])
                mul(R(5), R(1), cT[:, :, :])
                dv(lambda e: e.tensor_tensor(out=R(3), in0=R(4), in1=R(5), op=ALU.add))
                S.op("dve", lambda e: e.tensor_copy(out=xp_[:, :, 0], in_=wk[:, 2, :, T - 1]), reads=[wk.res], writes=[xp_.res])
                S.op("dve", lambda e: e.tensor_copy(out=xp_[:, :, 1], in_=wk[:, 3, :, T - 1]), reads=[wk.res], writes=[xp_.res])
                S.op("act", lambda e: e.activation(out=xb[:, 0, :, 0:T], in_=R(2), func=AF.Copy), reads=[wk.res], writes=[xb.res])
                S.op("act", lambda e: e.activation(out=xb[:, 1, :, 0:T], in_=R(3), func=AF.Copy, scale=-1.0), reads=[wk.res], writes=[xb.res])
            xb = self.s5xb
            py = self.psum()
            for k in range(KD):
                n_ = 0
                for j in range(4):
                    c = 4 * k + j
                    for z, slab in ((0, sCr), (1, sCi)):
                        S.op("pe", lambda e, k=k, c=c, z=z, slab=slab, n_=n_: e.matmul(py[:, k * T:(k + 1) * T], lhsT=slab[:, c * P:(c + 1) * P],
                                                                                       rhs=xb[:, z, c, 0:T], start=(n_ == 0), stop=(n_ == 7)),
                             reads=[slab.res, xb.res], writes=[py.res])
                        n_ += 1
            tf = self.tmpf
            S.op("dve", lambda e: e.tensor_tensor(out=tf[:, 0, 0:KD * T].rearrange("p (k t) -> p k t", t=T), in0=hT[:, :, t0:t0 + T],
                                                  in1=self.s5d[:, :].unsqueeze(2).to_broadcast([P, KD, T]), op=ALU.mult),
                 reads=[hy.res, self.cres], writes=[self.tmpres[0]])
            S.op("dve", lambda e: e.tensor_tensor(out=tf[:, 0, 0:KD * T], in0=tf[:, 0, 0:KD * T], in1=py[:, 0:KD * T], op=ALU.add),
                 reads=[self.tmpres[0], py.res], writes=[self.tmpres[0]])
            S.op("act", lambda e: e.activation(out=gT[:, 0:KD, t0:t0 + T], in_=tf[:, 0, 0:KD * T].rearrange("p (k t) -> p k t", t=T), func=AF.Gelu),
                 reads=[self.tmpres[0]], writes=[self.actres])
        if (not sample) and it == cfg.NT - 1:
            S.dma("sp", [(self.ssm_p[:, :, :], self.s5x[:, :, :])], reads=[self.s5x.res])
        yT = self.yT
        sg = self.tmpf
        for mg in range(0, KD, 4):
            s1_ = self.ws.next(("glu1", mg), 8 * 4 * P, lambda w, mg=mg: slab_fm(w["ssm_w_glu1"][0], 0, 8, mg, 4))
            s2_ = self.ws.next(("glu2", mg), 8 * 4 * P, lambda w, mg=mg: slab_fm(w["ssm_w_glu2"][0], 0, 8, mg, 4))
            for m in range(4):
                p1, p2 = self.psum(), self.psum()
                for (slab, ps) in ((s1_, p1), (s2_, p2)):
                    for k in range(KD):
                        S.op("pe", lambda e, k=k, slab=slab, ps=ps, m=m: e.matmul(ps[:, 0:N], lhsT=slab[:, (k * 4 + m) * P:(k * 4 + m + 1) * P], rhs=gT[:, k, 0:N],
                                                                                 start=(k == 0), stop=(k == KD - 1)), reads=[slab.res, self.actres], writes=[ps.res])
                mm = mg + m
                j = mm % 2
                S.op("act", lambda e, p2=p2, mm=mm, j=j: e.activation(out=sg[:, j, 0:N], in_=p2[:, 0:N], func=AF.Sigmoid, bias=self.s5b2[:, mm:mm + 1], scale=1.0),
                     reads=[p2.res, self.cres], writes=[self.tmpres[j]])
                S.op("dve", lambda e, p1=p1, mm=mm, j=j: e.scalar_tensor_tensor(out=yT[:, mm, 0:N], in0=p1[:, 0:N], scalar=self.s5b1[:, mm:mm + 1], in1=sg[:, j, 0:N],
                                                                               op0=ALU.add, op1=ALU.mult),
                     reads=[p1.res, self.tmpres[j], self.cres], writes=[hy.res])


def e64_host():
    a = np.zeros((P, 32, P), np.float32)
    for m in range(P):
        for v in range(32):
            for j in range(2):
                if (m % 64) == 2 * v + j:
                    a[m, v, j * 64:(j + 1) * 64] = 30000.0
    return a.reshape(P, 32 * P)


def caus_host():
    a = np.zeros((P, 8, 512), np.float32)
    key = np.arange(P)[:, None]
    q = np.arange(512)[None, :]
    for j in range(4):
        a[:, j, :] = np.where(j * P + key <= q, 0.0, -30000.0)
        a[:, 4 + j, :] = np.where(j * P + key >= q, 0.0, -30000.0)
    return a.reshape(P, 8 * 512)


def slab_fm(W, k0, kc, mg, mcnt):
    W = np.asarray(W)
    sub = W[k0 * P:(k0 + kc) * P, mg * P:(mg + mcnt) * P].reshape(kc, P, mcnt * P)
    return np.ascontiguousarray(sub.transpose(1, 0, 2)).reshape(P, kc * mcnt * P)


def slab_tm(W):
    W = np.asarray(W)
    return np.ascontiguousarray(W.reshape(8, P, 512).transpose(1, 0, 2)).reshape(P, 8 * 512)


_CACHE = {}


def get_prog(cfg_key=(8192, 4, 8192, (0, 1, 2, 3))):
    if cfg_key not in _CACHE:
        cfg = Cfg(seq=cfg_key[0], nsamp=cfg_key[1], past=cfg_key[2], layers=cfg_key[3])
        pr = Prog(cfg)
        pr.build()
        n = len(pr.ws.recipes)
        cfg2 = Cfg(seq=cfg_key[0], nsamp=cfg_key[1], past=cfg_key[2], layers=cfg_key[3], nslab_max=n)
        pr = Prog(cfg2)
        pr.build()
        _CACHE[cfg_key] = pr
    return _CACHE[cfg_key]


def run_prog(pr, inputs, ncores):
    I = {k: np.asarray(v) for k, v in inputs.items()}
    wts = pr.ws.host_array(I)
    in_maps = []
    for c in range(ncores):
        m = {}
        for name, (shape, fn, npdt) in pr.host.items():
            if name == "wts":
                m[name] = wts
            else:
                key = (name, c)
                a = np.ascontiguousarray(np.asarray(fn(I, c), dtype=npdt)).reshape(shape)
                m[name] = a
        in_maps.append(m)
    res = run_bass_kernel_spmd(pr.nc, in_maps, core_ids=list(range(ncores)))
    return res.results


def kernel(**inputs):
    pr = get_prog()
    NS = 4
    res = run_prog(pr, inputs, 8)
    f32 = np.float32
    y_prompt = np.stack([res[0]["yp"], res[1]["yp"]]).astype(f32)
    y_sample = np.concatenate([res[c]["ys"] for c in range(8)], 0)[:, None, :].astype(f32)
    kv = (2, 4, 64)
    new_cmp_p = np.stack([res[0]["cmp_p"], res[1]["cmp_p"]]).reshape((1, 2, 8192) + kv).astype(f32)
    new_slc_p = np.stack([res[0]["slc_p"], res[1]["slc_p"]]).reshape((1, 2, 8192) + kv).astype(f32)
    new_win_p = np.stack([res[0]["win_p"], res[1]["win_p"]]).reshape((1, 2, 512) + kv).astype(f32)
    new_cmp_s = np.concatenate([res[c]["cmp_s"] for c in range(8)], 0).reshape((1, 32, 1) + kv).astype(f32)
    new_slc_s = np.concatenate([res[c]["slc_s"] for c in range(8)], 0).reshape((1, 32, 1) + kv).astype(f32)
    new_win_s = np.concatenate([res[c]["win_s"] for c in range(8)], 0).reshape((1, 32, 512) + kv).astype(f32)

    def st_p(a):
        return a.reshape(2, 64, 32, 2).transpose(2, 0, 1, 3).reshape(64, 64, 2)

    def st_s(a):
        return a.reshape(2, 64, 32, NS, 2).transpose(3, 2, 0, 1, 4).reshape(NS, 64, 64, 2)
    new_ssm_p = np.stack([st_p(res[0]["ssm_p"]), st_p(res[1]["ssm_p"])])[None].astype(f32)
    new_ssm_s = np.concatenate([st_s(res[c]["ssm_s"]) for c in range(8)], 0)[None].astype(f32)
    gvs = np.concatenate([res[c]["gv"] for c in range(8)], 1)[:, :, None, :].astype(f32)
    return (y_prompt, y_sample, new_cmp_p, new_cmp_s, new_slc_p, new_slc_s, new_win_p, new_win_s, new_ssm_p, new_ssm_s, gvs)
```
